# Optimizing a Trainium2 kernel written in Bass

```python
import jax, jax.numpy as jnp
from jax import lax
import numpy as np

D_MODEL = 2048
BATCH = 16
SEQ = 2048
DEPTH = 1
DEC_BATCH = 4
DEC_SEQ = 4096
PAST_LEN = 128

HEAD_DIM = 128
HEADS_PER_GROUP = 4
DILATED_GROUPS = ((128, 1), (512, 4), (2048, 16))
N_GROUPS = 3
N_ATTN_HEADS = N_GROUPS * HEADS_PER_GROUP
ATTN_WIDTH = N_ATTN_HEADS * HEAD_DIM
ATTN_OUT_WIDTH = HEADS_PER_GROUP * HEAD_DIM
CONV_WIDTH = D_MODEL
CONV_TAPS = 3
ROPE_THETA = 10000.0
NORM_EPS = 1e-6
PEER_HEADS = 8
PEER_NKEYS = 128
PEER_EXPERTS = PEER_NKEYS * PEER_NKEYS
PEER_QDIM = 256
PEER_HALF = PEER_QDIM // 2
PEER_TOPK = 16
PEER_CHUNK = 128
IN_COLS = 3 * ATTN_WIDTH + 3 * CONV_WIDTH + 2 * D_MODEL

kernel_name = "hybrid_dilated_attn_shortconv_peer_encoder"

F32 = jnp.float32


def _rms_norm(x, g):
    xf = x.astype(F32)
    y = xf * lax.rsqrt(jnp.mean(xf * xf, axis=-1, keepdims=True) + NORM_EPS) * g.astype(F32)
    return y.astype(x.dtype)


def _rotary(x, positions):
    d = x.shape[-1]
    half = d // 2
    inv_freq = ROPE_THETA ** (-jnp.arange(half, dtype=F32) * 2.0 / d)
    ang = positions.astype(F32)[:, None] * inv_freq[None, :]
    cos = jnp.cos(ang)[None, :, None, :]
    sin = jnp.sin(ang)[None, :, None, :]
    xf = x.astype(F32)
    x1, x2 = xf[..., :half], xf[..., half:]
    return jnp.concatenate([x1 * cos - x2 * sin, x2 * cos + x1 * sin], axis=-1).astype(x.dtype)


def _dilated_band_attention(q, k, v, window, dilation):
    b, s, h, d = q.shape
    r = dilation
    half = (window // 2) // r
    blk = half
    l = s // r
    lp = -(-l // blk) * blk
    nb = lp // blk

    def to_classes(t):
        t = t.reshape(b, l, r, h, d).transpose(0, 2, 3, 1, 4)
        return jnp.pad(t, ((0, 0), (0, 0), (0, 0), (0, lp - l), (0, 0)))

    def neighbour_blocks(t):
        tp = jnp.pad(t, ((0, 0), (0, 0), (0, 0), (blk, blk), (0, 0))).reshape(b, r, h, nb + 2, blk, d)
        return jnp.concatenate([tp[:, :, :, :-2], tp[:, :, :, 1:-1], tp[:, :, :, 2:]], axis=-2)

    qb = to_classes(q).reshape(b, r, h, nb, blk, d)
    kb = neighbour_blocks(to_classes(k))
    vb = neighbour_blocks(to_classes(v))

    q_idx = jnp.arange(nb)[:, None] * blk + jnp.arange(blk)[None, :]
    k_idx = (jnp.arange(nb)[:, None] - 1) * blk + jnp.arange(3 * blk)[None, :]
    rel = k_idx[:, None, :] - q_idx[:, :, None]
    valid = (jnp.abs(rel) <= half) & (k_idx[:, None, :] >= 0) & (k_idx[:, None, :] < l)

    scores = jnp.einsum('brhnqd,brhnkd->brhnqk', qb.astype(F32), kb.astype(F32)) * (d ** -0.5)
    scores = jnp.where(valid, scores, -jnp.inf)
    m = jnp.max(scores, axis=-1, keepdims=True)
    p = jnp.exp(scores - m)
    den = jnp.sum(p, axis=-1, keepdims=True)
    o = jnp.einsum('brhnqk,brhnkd->brhnqd', p, vb.astype(F32)) / den
    lse = (m + jnp.log(den))[..., 0]

    o = o.reshape(b, r, h, lp, d)[:, :, :, :l].transpose(0, 3, 1, 2, 4).reshape(b, s, h, d)
    lse = lse.reshape(b, r, h, lp)[:, :, :, :l].transpose(0, 3, 1, 2).reshape(b, s, h)
    return o, lse


def _mixer(u, w_in, q_g, k_g, w_ao, conv_w, w_co, w_o):
    b, s, _ = u.shape
    proj = u @ w_in
    cuts = [ATTN_WIDTH, 2 * ATTN_WIDTH, 3 * ATTN_WIDTH,
            3 * ATTN_WIDTH + CONV_WIDTH, 3 * ATTN_WIDTH + 2 * CONV_WIDTH,
            3 * ATTN_WIDTH + 3 * CONV_WIDTH, 3 * ATTN_WIDTH + 3 * CONV_WIDTH + D_MODEL]
    q, k, v, cb, cc, cx, ga, gc = jnp.split(proj, cuts, axis=-1)

    pos = jnp.arange(s)
    q = _rotary(_rms_norm(q.reshape(b, s, N_ATTN_HEADS, HEAD_DIM), q_g), pos)
    k = _rotary(_rms_norm(k.reshape(b, s, N_ATTN_HEADS, HEAD_DIM), k_g), pos)
    v = v.reshape(b, s, N_ATTN_HEADS, HEAD_DIM)
    outs, lses = [], []
    for gi, (window, dilation) in enumerate(DILATED_GROUPS):
        sl = slice(gi * HEADS_PER_GROUP, (gi + 1) * HEADS_PER_GROUP)
        o, lse = _dilated_band_attention(q[:, :, sl], k[:, :, sl], v[:, :, sl], window, dilation)
        outs.append(o)
        lses.append(lse)
    alpha = jax.nn.softmax(jnp.stack(lses, axis=0), axis=0)
    attn = jnp.sum(alpha[..., None] * jnp.stack(outs, axis=0), axis=0)
    attn = attn.reshape(b, s, ATTN_OUT_WIDTH).astype(u.dtype)

    z = cc * cx
    zp = jnp.pad(z, ((0, 0), (1, 1), (0, 0)))
    zc = conv_w[0] * zp[:, :-2] + conv_w[1] * zp[:, 1:-1] + conv_w[2] * zp[:, 2:]
    conv = cb * zc

    mix = jax.nn.sigmoid(ga) * (attn @ w_ao) + jax.nn.sigmoid(gc) * (conv @ w_co)
    return mix @ w_o


def _peer(xn, w_q, sub_keys, u_tab, v_tab):
    b, s, dm = xn.shape
    t = b * s
    xf = xn.reshape(t, dm)
    q = (xf @ w_q).reshape(t, PEER_HEADS, 2, PEER_HALF)
    sc = jnp.einsum('thpd,hpnd->thpn', q.astype(F32), sub_keys.astype(F32))
    s1, i1 = lax.top_k(sc[:, :, 0], PEER_TOPK)
    s2, i2 = lax.top_k(sc[:, :, 1], PEER_TOPK)
    cand = (s1[..., :, None] + s2[..., None, :]).reshape(t, PEER_HEADS, PEER_TOPK * PEER_TOPK)
    best, flat = lax.top_k(cand, PEER_TOPK)
    e1 = jnp.take_along_axis(i1, flat // PEER_TOPK, axis=-1)
    e2 = jnp.take_along_axis(i2, flat % PEER_TOPK, axis=-1)
    experts = e1 * PEER_NKEYS + e2
    gates = jax.nn.softmax(best, axis=-1)

    nc = t // PEER_CHUNK

    def expert_block(args):
        xc, ec, gcb = args
        ug = u_tab[ec]
        vg = v_tab[ec]
        hid = jnp.einsum('cd,chkd->chk', xc, ug)
        a = (jax.nn.gelu(hid.astype(F32), approximate=False) * gcb).astype(xc.dtype)
        return jnp.einsum('chk,chkd->cd', a, vg)

    out = lax.map(expert_block, (xf.reshape(nc, PEER_CHUNK, dm),
                                 experts.reshape(nc, PEER_CHUNK, PEER_HEADS, PEER_TOPK),
                                 gates.reshape(nc, PEER_CHUNK, PEER_HEADS, PEER_TOPK)))
    return out.reshape(b, s, dm)


def _trunk(x, norm1_g, w_in, q_norm_g, k_norm_g, w_attn_out, conv_w, w_conv_out, w_o,
           norm2_g, peer_w_q, peer_sub_keys, peer_u, peer_v):
    for l in range(DEPTH):
        h = x + _mixer(_rms_norm(x, norm1_g[l]), w_in[l], q_norm_g[l], k_norm_g[l],
                       w_attn_out[l], conv_w[l], w_conv_out[l], w_o[l])
        x = h + _peer(_rms_norm(h, norm2_g[l]), peer_w_q[l], peer_sub_keys[l], peer_u[l], peer_v[l])
    return x


def setup_inputs(seed: int = 0) -> dict:
    key = jax.random.key(seed)
    ks = jax.random.split(key, 16)
    n = jax.random.normal
    return {
        "x_prompt": n(ks[0], (BATCH, SEQ, D_MODEL), F32),
        "x_sample": n(ks[1], (DEC_BATCH, DEC_SEQ, D_MODEL), F32),
        "norm1_g": 1.0 + 0.02 * n(ks[2], (DEPTH, D_MODEL), F32),
        "w_in": n(ks[3], (DEPTH, D_MODEL, IN_COLS), F32) * D_MODEL ** -0.5,
        "q_norm_g": 1.0 + 0.02 * n(ks[4], (DEPTH, HEAD_DIM), F32),
        "k_norm_g": 1.0 + 0.02 * n(ks[5], (DEPTH, HEAD_DIM), F32),
        "w_attn_out": n(ks[6], (DEPTH, ATTN_OUT_WIDTH, D_MODEL), F32) * ATTN_OUT_WIDTH ** -0.5,
        "conv_w": n(ks[7], (DEPTH, CONV_TAPS, CONV_WIDTH), F32) * CONV_TAPS ** -0.5,
        "w_conv_out": n(ks[8], (DEPTH, CONV_WIDTH, D_MODEL), F32) * CONV_WIDTH ** -0.5,
        "w_o": n(ks[9], (DEPTH, D_MODEL, D_MODEL), F32) * D_MODEL ** -0.5,
        "norm2_g": 1.0 + 0.02 * n(ks[10], (DEPTH, D_MODEL), F32),
        "peer_w_q": n(ks[11], (DEPTH, D_MODEL, PEER_HEADS * PEER_QDIM), F32) * D_MODEL ** -0.5,
        "peer_sub_keys": n(ks[12], (DEPTH, PEER_HEADS, 2, PEER_NKEYS, PEER_HALF), F32) * PEER_HALF ** -0.5,
        "peer_u": n(ks[13], (DEPTH, PEER_EXPERTS, D_MODEL), F32) * D_MODEL ** -0.5,
        "peer_v": n(ks[14], (DEPTH, PEER_EXPERTS, D_MODEL), F32) * (PEER_HEADS * PEER_TOPK) ** -0.5,
    }


def reference(x_prompt, x_sample, norm1_g, w_in, q_norm_g, k_norm_g, w_attn_out, conv_w,
              w_conv_out, w_o, norm2_g, peer_w_q, peer_sub_keys, peer_u, peer_v):
    y_prompt = _trunk(x_prompt, norm1_g, w_in, q_norm_g, k_norm_g, w_attn_out, conv_w,
                      w_conv_out, w_o, norm2_g, peer_w_q, peer_sub_keys, peer_u, peer_v)
    y_sample = _trunk(x_sample, norm1_g, w_in, q_norm_g, k_norm_g, w_attn_out, conv_w,
                      w_conv_out, w_o, norm2_g, peer_w_q, peer_sub_keys, peer_u, peer_v)
    return (y_prompt, y_sample)
```

```python
import numpy as np
import ml_dtypes
from contextlib import ExitStack
import concourse.bass as bass
import concourse.mybir as mybir
from concourse.alu_op_type import AluOpType as ALU
from concourse.bass_utils import run_bass_kernel_spmd

F32 = mybir.dt.float32
BF16 = mybir.dt.bfloat16
U32 = mybir.dt.uint32
AF = mybir.ActivationFunctionType
AX = mybir.AxisListType

D = 2048
INC = 14848
NTOK = 6144
NT = NTOK // 128
TPS = 16
NEXP = 16384
EPS = 1e-6
NEGB = -30000.0


class Buf:
    __slots__ = ("name", "lw", "rd", "sem", "dcnt")

    def __init__(self, name):
        self.name = name
        self.lw = None
        self.rd = {}
        self.sem = None
        self.dcnt = 0


class Sched:
    def __init__(self, nc):
        self.nc = nc
        self.eng = {"pe": nc.tensor, "dve": nc.vector, "act": nc.scalar,
                    "pool": nc.gpsimd, "sp": nc.sync}
        self.sems = {}
        self.cnt = {}
        for e in ("pe", "dve", "act", "pool"):
            self.sems[e] = nc.semaphore("tl_" + e).__enter__()
            self.cnt[e] = 0
        self.waited = {e: {} for e in self.eng}
        self.ndsem = 0
        self.nins = 0
        self.allbufs = []

    def buf(self, name):
        b = Buf(name)
        self.allbufs.append(b)
        return b

    def _deps(self, reads, writes):
        d = {}
        for b in reads:
            if b.lw is not None:
                k, v = b.lw
                if d.get(k, 0) < v:
                    d[k] = v
        for b in writes:
            if b.lw is not None:
                k, v = b.lw
                if d.get(k, 0) < v:
                    d[k] = v
            for k, v in b.rd.items():
                if d.get(k, 0) < v:
                    d[k] = v
        return d

    def _wait(self, e, deps):
        w = self.waited[e]
        for k, v in deps.items():
            if w.get(k, 0) < v:
                self.eng[e].wait_ge(self.sems[k], v)
                w[k] = v
                self.nins += 1

    def _mark(self, tok, reads, writes):
        k, v = tok
        for b in reads:
            if b.rd.get(k, 0) < v:
                b.rd[k] = v
        for b in writes:
            b.lw = tok
            b.rd = {}

    def op(self, e, fn, reads=(), writes=()):
        d = self._deps(reads, writes)
        if e == "pe":
            d.pop("pe", None)
        self._wait(e, d)
        ins = fn()
        self.cnt[e] += 1
        ins.then_inc(self.sems[e], 1)
        self.nins += 1
        self._mark((e, self.cnt[e]), reads, writes)

    def _dsem(self, b):
        if b.sem is None:
            key = "d%d" % self.ndsem
            self.ndsem += 1
            self.sems[key] = self.nc.semaphore(key).__enter__()
            b.sem = key
        return b.sem

    def dma(self, q, pairs, reads, writes, syncbuf, **kw):
        self._wait(q, self._deps(reads, writes))
        key = self._dsem(syncbuf)
        for (o, i) in pairs:
            ins = self.eng[q].dma_start(out=o, in_=i, **kw)
            syncbuf.dcnt += 16
            ins.then_inc(self.sems[key], 16)
            self.nins += 1
        self._mark((key, syncbuf.dcnt), reads, writes)

    def gather(self, out, in_, off_ap, reads, writes, syncbuf):
        q = "pool"
        self._wait(q, self._deps(reads, writes))
        key = self._dsem(syncbuf)
        ins = self.nc.gpsimd.indirect_dma_start(
            out=out, out_offset=None, in_=in_,
            in_offset=bass.IndirectOffsetOnAxis(ap=off_ap, axis=0))
        syncbuf.dcnt += 16
        ins.then_inc(self.sems[key], 16)
        self.nins += 1
        self._mark((key, syncbuf.dcnt), reads, writes)

    def barrier(self):
        allv = {}
        for e in ("pe", "dve", "act", "pool"):
            if self.cnt[e]:
                allv[e] = self.cnt[e]
        for b in self.allbufs:
            if b.sem is not None and b.dcnt:
                allv[b.sem] = b.dcnt
        for e in self.eng:
            self._wait(e, allv)


class T:
    def __init__(self, t, b, bs=None):
        self.t = t
        self.b = b
        self.bs = bs


def build(nt=NT, phases="0ABC", debug=False):
    nc = bass.Bass("TRN2", target_bir_lowering=False)
    s = Sched(nc)
    ntok = nt * 128
    kind_dbg = "ExternalOutput" if debug else "Internal"

    def din(name, shape, dt=F32):
        return nc.dram_tensor(name, shape, dt, kind="ExternalInput").ap()

    def dscr(name, shape, dt, dbg=False):
        return T(nc.dram_tensor(name, shape, dt, kind=(kind_dbg if dbg else "Internal")).ap(), s.buf(name))

    xs = din("xs", [NTOK, D])
    ccd = din("ccd", [NTOK, 128])
    ssd = din("ssd", [NTOK, 128])
    f01d = din("f01", [128, 1])
    maskd = din("maskb", [128, 9 * 128])
    norm1_g = din("norm1_g", [1, D])
    w_in = din("w_in", [D, INC])
    q_norm_g = din("q_norm_g", [1, 128])
    k_norm_g = din("k_norm_g", [1, 128])
    w_ao = din("w_attn_out", [512, D])
    conv_w = din("conv_w", [3, D])
    w_co = din("w_conv_out", [D, D])
    w_o = din("w_o", [D, D])
    norm2_g = din("norm2_g", [1, D])
    w_q = din("peer_w_q", [D, D])
    subk = din("peer_sub_keys", [16, 128, 128])
    peer_u = din("peer_u", [NEXP, D])
    peer_v = din("peer_v", [NEXP, D])
    ys = nc.dram_tensor("ys", [NTOK, D], F32, kind="ExternalOutput").ap()
    b_ys = s.buf("ys")

    w_in_b = dscr("w_in_b", [D, INC], BF16)
    wgrp = [s.buf("w_in_grp%d" % i) for i in range(4)]

    def WGROUP(c):
        return 0 if c < 4 else (1 if c < 8 else (2 if c < 16 else 3))
    w_ao_b = dscr("w_ao_b", [512, D], BF16)
    w_co_b = dscr("w_co_b", [D, D], BF16)
    w_o_b = dscr("w_o_b", [D, D], BF16)
    w_q_b = dscr("w_q_b", [D, D], BF16)
    pu_b = dscr("pu_b", [NEXP, D], BF16)
    pv_b = dscr("pv_b", [NEXP, D], BF16)
    qT_s = dscr("qT_s", [12, 128, NTOK], BF16, True)
    kT_s = dscr("kT_s", [12, 128, NTOK], BF16, True)
    v_s = dscr("v_s", [NTOK, 12 * 129], BF16, True)
    cb_s = dscr("cb_s", [NTOK, D], F32, True)
    z_s = dscr("z_s", [NTOK + 2, D], F32, True)
    sga_s = dscr("sga_s", [NTOK, D], F32, True)
    sgc_s = dscr("sgc_s", [NTOK, D], F32, True)
    h_s = dscr("h_s", [NTOK, D], F32, True)
    xn2_s = dscr("xn2_s", [NTOK, D], BF16, True)
    idxT_s = dscr("idxT_s", [NT, 128, 128], U32, True)
    gT_s = dscr("gT_s", [NT, 128, 128], F32, True)
    sc_s = dscr("sc_s", [NTOK, 2048], F32)
    mix_s = dscr("mix_s", [NTOK, D], BF16)
    dbg_attn = dscr("dbg_attn", [NTOK, 512], F32, True) if debug else None
    dbg_sc = dscr("dbg_sc", [NTOK, 2048], F32, True) if debug else None

    def sbt(st, name, shape, dt, nb=0):
        t = st.enter_context(nc.sbuf_tensor(name, shape, dt))
        bs = [s.buf(name + "_%d" % i) for i in range(nb)] if nb else None
        return T(t, s.buf(name), bs)

    def pst(st, name):
        t = st.enter_context(nc.psum_tensor(name, [128, 512], F32))
        return T(t, s.buf(name))

    V = nc.vector
    A = nc.scalar
    G = nc.gpsimd
    PE = nc.tensor

    with ExitStack() as gst:
        identf = sbt(gst, "identf", [128, 128], F32)
        identb = sbt(gst, "identb", [128, 128], BF16)
        s.op("pool", lambda: G.memset(identf.t[:], 0.0), [], [identf.b])
        s.op("pool", lambda: G.affine_select(out=identf.t[:], in_=identf.t[:], pattern=[[-1, 128]],
                                             compare_op=ALU.not_equal, fill=1.0, base=0,
                                             channel_multiplier=1), [identf.b], [identf.b])
        s.op("dve", lambda: V.tensor_copy(out=identb.t[:], in_=identf.t[:]), [identf.b], [identb.b])

        if "0" in phases:
            def cast(dst, src, rows, cols, chunk):
                vd = dst.t.rearrange("r (a b) -> (r a) b", b=chunk)
                vs = src.rearrange("r (a b) -> (r a) b", b=chunk)
                n = rows * cols // chunk
                per = 4096
                for r0 in range(0, n, per):
                    r1 = min(n, r0 + per)
                    s.dma("pool", [(vd[r0:r1, :], vs[r0:r1, :])], [], [dst.b], dst.b)
            for cj in range(8):
                gbuf = wgrp[WGROUP(cj * 4)]
                c0, c1 = cj * 2048, min(INC, (cj + 1) * 2048)
                s.dma("pool", [(w_in_b.t[:, c0:c1], w_in[:, c0:c1])], [], [gbuf], gbuf)
            cast(w_ao_b, w_ao, 512, D, 2048)
            cast(w_co_b, w_co, D, D, 2048)
            cast(w_o_b, w_o, D, D, 2048)
            cast(w_q_b, w_q, D, D, 2048)

        if "A" in phases:
            with ExitStack() as st:
                g1b = sbt(st, "g1b", [128, D], F32)
                gq = sbt(st, "gq", [128, 256], F32)
                gk = sbt(st, "gk", [128, 256], F32)
                s.dma("sp", [(g1b.t[:], norm1_g.to_broadcast([128, D]))], [], [g1b.b], g1b.b)
                s.dma("sp", [(gq.t[:, 0:128], q_norm_g.to_broadcast([128, 128])),
                             (gq.t[:, 128:192], q_norm_g[:, 64:128].to_broadcast([128, 64])),
                             (gq.t[:, 192:256], q_norm_g[:, 0:64].to_broadcast([128, 64]))], [], [gq.b], gq.b)
                s.dma("sp", [(gk.t[:, 0:128], k_norm_g.to_broadcast([128, 128])),
                             (gk.t[:, 128:192], k_norm_g[:, 64:128].to_broadcast([128, 64])),
                             (gk.t[:, 192:256], k_norm_g[:, 0:64].to_broadcast([128, 64]))], [], [gk.b], gk.b)

                xt = [sbt(st, "xt%d" % i, [128, D], F32) for i in range(2)]
                rope = [sbt(st, "rope%d" % i, [128, 256], F32) for i in range(2)]
                xnb = [sbt(st, "xnb%d" % i, [128, D], BF16) for i in range(2)]
                junk = sbt(st, "junkA", [128, 512], BF16)
                stat = [sbt(st, "stat%d" % i, [128, 4], F32) for i in range(2)]
                xnT = [sbt(st, "xnT%d" % i, [128, 16, 512], BF16, nb=4) for i in range(2)]
                tabs = [sbt(st, "tabs%d" % i, [128, 4, 4, 128], F32, nb=4) for i in range(1)]
                wblk = [sbt(st, "wblk%d" % i, [128, 16, 512], BF16) for i in range(2)]
                ps = [pst(st, "psA%d" % i) for i in range(8)]
                qs = [sbt(st, "qs%d" % i, [128, 512], F32) for i in range(2)]
                qst = [sbt(st, "qst%d" % i, [128, 8], F32) for i in range(2)]
                tA = [sbt(st, "tA%d" % i, [128, 512], F32) for i in range(2)]
                tB = [sbt(st, "tB%d" % i, [128, 512], F32) for i in range(1)]
                ob = [sbt(st, "ob%d" % i, [128, 512], BF16) for i in range(2)]
                qTst = [sbt(st, "qTst%d" % i, [128, 12, 512], BF16) for i in range(1)]
                kTst = [sbt(st, "kTst%d" % i, [128, 12, 512], BF16) for i in range(1)]
                vst = [sbt(st, "vst%d" % i, [128, 4, 12 * 129], BF16) for i in range(1)]
                fst = [sbt(st, "fst%d" % i, [128, 4, 512], F32, nb=4) for i in range(2)]
                ccst = sbt(st, "ccst", [128, 4, D], F32, nb=16)
                s.op("dve", lambda: V.memset(vst[0].t[:], 1.0), [], [vst[0].b])
                s.op("dve", lambda: V.memset(fst[0].t[0:1, :, :], 0.0), [], fst[0].bs)
                s.dma("sp", [(z_s.t[0:1, :], fst[0].t[0:1, :, :]), (z_s.t[NTOK + 1:NTOK + 2, :], fst[0].t[0:1, :, :])],
                      fst[0].bs, [z_s.b], fst[0].b)

                nmac = nt // 4
                uvq = [(dd, ss_, r0) for (dd, ss_) in ((pu_b, peer_u), (pv_b, peer_v)) for r0 in range(0, NEXP, 4096)]

                def wload(gb):
                    wb = wblk[gb % 2]
                    c = gb % 29
                    s.dma("sp", [(wb.t[:], w_in_b.t[:, c * 512:(c + 1) * 512].rearrange("(k p) c -> p k c", p=128))],
                          [wgrp[WGROUP(c)]], [wb.b], wb.b)
                psi = 0
                pti = 0
                fsi = 0
                qi = 0
                for m in range(nmac):
                    xT = xnT[m % 2]
                    tb = tabs[0]
                    if uvq and m >= 1:
                        (dstT_, src_, r0) = uvq.pop(0)
                        s.dma("pool", [(dstT_.t[r0:r0 + 4096, :], src_[r0:r0 + 4096, :])], [xt[(m * 4) % 2].b], [dstT_.b], dstT_.b)
                    for sti in range(4):
                        i = m * 4 + sti
                        T0 = i * 128
                        x_ = xt[i % 2]
                        r_ = rope[i % 2]
                        xb_ = xnb[i % 2]
                        st_ = stat[i % 2]
                        s.dma("sp", [(x_.t[:], xs[T0:T0 + 128, :])], [], [x_.b], x_.b)
                        s.dma("sp", [(r_.t[:, 0:128], ccd[T0:T0 + 128, :]), (r_.t[:, 128:256], ssd[T0:T0 + 128, :])],
                              [], [r_.b], r_.b)
                        tbv = tb.t
                        s.op("dve", lambda: V.tensor_tensor(out=tbv[:, sti, 0, :], in0=r_.t[:, 0:128], in1=gq.t[:, 0:128], op=ALU.mult),
                             [r_.b, gq.b], [tb.bs[sti]])
                        s.op("dve", lambda: V.tensor_tensor(out=tbv[:, sti, 1, :], in0=r_.t[:, 128:256], in1=gq.t[:, 128:256], op=ALU.mult),
                             [r_.b, gq.b], [tb.bs[sti]])
                        s.op("dve", lambda: V.tensor_tensor(out=tbv[:, sti, 2, :], in0=r_.t[:, 0:128], in1=gk.t[:, 0:128], op=ALU.mult),
                             [r_.b, gk.b], [tb.bs[sti]])
                        s.op("dve", lambda: V.tensor_tensor(out=tbv[:, sti, 3, :], in0=r_.t[:, 128:256], in1=gk.t[:, 128:256], op=ALU.mult),
                             [r_.b, gk.b], [tb.bs[sti]])
                        s.op("act", lambda: A.activation(out=xb_.t[:], in_=x_.t[:], func=AF.Square, accum_out=st_.t[:, 0:1]),
                             [x_.b], [xb_.b, st_.b])
                        s.op("act", lambda: A.activation(out=st_.t[:, 1:2], in_=st_.t[:, 0:1], func=AF.Sqrt, scale=1.0 / D, bias=EPS),
                             [st_.b], [st_.b])
                        s.op("dve", lambda: V.reciprocal(out=st_.t[:, 2:3], in_=st_.t[:, 1:2]), [st_.b], [st_.b])
                        s.op("dve", lambda: V.scalar_tensor_tensor(out=xb_.t[:], in0=x_.t[:], scalar=st_.t[:, 2:3], in1=g1b.t[:],
                                                                   op0=ALU.mult, op1=ALU.mult),
                             [x_.b, st_.b, g1b.b], [xb_.b])
                        for hf in range(2):
                            p_ = ps[6 + hf]
                            pv = p_.t[:, :].bitcast(BF16)

                            def tr(hf=hf, pv=pv):
                                for kk in range(8):
                                    k = hf * 8 + kk
                                    ins = PE.transpose(out=pv[:, kk * 128:(kk + 1) * 128], in_=xb_.t[:, k * 128:(k + 1) * 128],
                                                       identity=identb.t[:])
                                return ins
                            s.op("pe", tr, [xb_.b, identb.b], [p_.b])
                            s.op("act", lambda hf=hf, pv=pv: A.copy(out=xT.t[:, hf * 8:(hf + 1) * 8, sti * 128:(sti + 1) * 128],
                                                                    in_=pv.rearrange("p (k t) -> p k t", k=8)),
                                 [p_.b], [xT.bs[sti]])
                    for cb in range(29):
                        gb = m * 29 + cb
                        if gb == 0:
                            wload(0)
                        if gb + 1 < nmac * 29:
                            wload(gb + 1)
                        wb = wblk[gb % 2]
                        typ = cb // 3 if cb < 9 else 3 + (cb - 9) // 4
                        sub = cb % 3 if cb < 9 else (cb - 9) % 4
                        if typ >= 3 and typ != 4:
                            f_ = fst[fsi % 2]
                            fsi += 1
                        for sti in range(4):
                            i = m * 4 + sti
                            p_ = ps[psi % 6]
                            psi += 1

                            def mm(p_=p_, sti=sti):
                                for k in range(16):
                                    ins = PE.matmul(p_.t[:, :], lhsT=xT.t[:, k, sti * 128:(sti + 1) * 128], rhs=wb.t[:, k, :],
                                                    start=(k == 0), stop=(k == 15))
                                return ins
                            s.op("pe", mm, [xT.bs[sti], wb.b], [p_.b])
                            if typ <= 1:
                                q_ = qs[qi % 2]
                                qs_ = qst[qi % 2]
                                a_ = tA[qi % 2]
                                b_ = tB[0]
                                o_ = ob[qi % 2]
                                qi += 1
                                tg = tb.t[:, sti, 2 * typ, :]
                                tsn = tb.t[:, sti, 2 * typ + 1, :]
                                s.op("act", lambda: A.copy(out=q_.t[:], in_=p_.t[:, :]), [p_.b], [q_.b])

                                def sq():
                                    for h in range(4):
                                        ins = A.activation(out=junk.t[:, h * 128:(h + 1) * 128], in_=p_.t[:, h * 128:(h + 1) * 128],
                                                           func=AF.Square, accum_out=qs_.t[:, h:h + 1])
                                    return ins
                                s.op("act", sq, [p_.b], [junk.b, qs_.b])
                                s.op("act", lambda: A.activation(out=qs_.t[:, 4:8], in_=qs_.t[:, 0:4], func=AF.Sqrt, scale=1.0 / 128, bias=EPS),
                                     [qs_.b], [qs_.b])
                                s.op("dve", lambda: V.reciprocal(out=qs_.t[:, 0:4], in_=qs_.t[:, 4:8]), [qs_.b], [qs_.b])
                                q3 = q_.t[:, :].rearrange("p (h d) -> p h d", h=4)
                                a3 = a_.t[:, :].rearrange("p (h d) -> p h d", h=4)
                                b3 = b_.t[:, :].rearrange("p (h d) -> p h d", h=4)
                                s.op("dve", lambda: V.tensor_tensor(out=a3, in0=q3, in1=tg.unsqueeze(1).to_broadcast([128, 4, 128]), op=ALU.mult),
                                     [q_.b, tb.bs[sti]], [a_.b])
                                s.op("dve", lambda: V.tensor_tensor(out=b3[:, :, 0:64], in0=q3[:, :, 64:128],
                                                                    in1=tsn[:, 0:64].unsqueeze(1).to_broadcast([128, 4, 64]), op=ALU.mult),
                                     [q_.b, tb.bs[sti]], [b_.b])
                                s.op("dve", lambda: V.tensor_tensor(out=b3[:, :, 64:128], in0=q3[:, :, 0:64],
                                                                    in1=tsn[:, 64:128].unsqueeze(1).to_broadcast([128, 4, 64]), op=ALU.mult),
                                     [q_.b, tb.bs[sti]], [b_.b])
                                s.op("dve", lambda: V.tensor_tensor(out=a_.t[:, :], in0=a_.t[:, :], in1=b_.t[:, :], op=ALU.add),
                                     [a_.b, b_.b], [a_.b])
                                s.op("dve", lambda: V.tensor_tensor(out=o_.t[:, :].rearrange("p (h d) -> p h d", h=4), in0=a3,
                                                                    in1=qs_.t[:, 0:4].unsqueeze(2).to_broadcast([128, 4, 128]), op=ALU.mult),
                                     [a_.b, qs_.b], [o_.b])
                                pt_ = ps[6 + pti % 2]
                                pti += 1
                                ptv = pt_.t[:, :].bitcast(BF16)

                                def tr2():
                                    for h in range(4):
                                        ins = PE.transpose(out=ptv[:, h * 128:(h + 1) * 128], in_=o_.t[:, h * 128:(h + 1) * 128],
                                                           identity=identb.t[:])
                                    return ins
                                s.op("pe", tr2, [o_.b, identb.b], [pt_.b])
                                dstT = (qTst if typ == 0 else kTst)[0]
                                s.op("act", lambda: A.copy(out=dstT.t[:, sub * 4:(sub + 1) * 4, sti * 128:(sti + 1) * 128],
                                                           in_=ptv[:, 0:512].rearrange("p (h t) -> p h t", h=4)),
                                     [pt_.b], [dstT.b])
                            elif typ == 2:
                                v_ = vst[0]
                                vv = v_.t[:, sti, :].rearrange("p (h e) -> p h e", e=129)
                                s.op("act", lambda: A.copy(out=vv[:, sub * 4:(sub + 1) * 4, 0:128],
                                                           in_=p_.t[:, :].rearrange("p (h d) -> p h d", h=4)),
                                     [p_.b], [v_.b])
                            elif typ == 3:
                                s.op("act", lambda: A.copy(out=f_.t[:, sti, :], in_=p_.t[:, :]), [p_.b], [f_.bs[sti]])
                            elif typ == 4:
                                s.op("act", lambda: A.copy(out=ccst.t[:, sti, sub * 512:(sub + 1) * 512], in_=p_.t[:, :]),
                                     [p_.b], [ccst.bs[sti * 4 + sub]])
                            elif typ == 5:
                                s.op("dve", lambda: V.tensor_tensor(out=f_.t[:, sti, :], in0=p_.t[:, :],
                                                                    in1=ccst.t[:, sti, sub * 512:(sub + 1) * 512], op=ALU.mult),
                                     [p_.b, ccst.bs[sti * 4 + sub]], [f_.bs[sti]])
                            else:
                                s.op("act", lambda: A.activation(out=f_.t[:, sti, :], in_=p_.t[:, :], func=AF.Sigmoid),
                                     [p_.b], [f_.bs[sti]])
                        R0 = m * 512
                        if typ >= 3 and typ != 4:
                            dst = {3: cb_s, 5: z_s, 6: sga_s, 7: sgc_s}[typ]
                            ro = 1 if typ == 5 else 0
                            s.dma("sp", [(dst.t[ro + R0:ro + R0 + 512, sub * 512:(sub + 1) * 512].rearrange("(s p) c -> p s c", p=128),
                                          f_.t[:, :, :])], f_.bs, [dst.b], f_.b)
                        if cb == 5:
                            for (src, dst) in ((qTst[0], qT_s), (kTst[0], kT_s)):
                                s.dma("sp", [(dst.t[:, :, R0:R0 + 512].rearrange("h d t -> d h t"), src.t[:, :, :])],
                                      [src.b], [dst.b], src.b)
                        if cb == 8:
                            v_ = vst[0]
                            s.dma("sp", [(v_s.t[R0:R0 + 512, :].rearrange("(s p) c -> p s c", p=128), v_.t[:, :, :])],
                                  [v_.b], [v_s.b], v_.b)
                while uvq:
                    (dstT_, src_, r0) = uvq.pop(0)
                    s.dma("pool", [(dstT_.t[r0:r0 + 4096, :], src_[r0:r0 + 4096, :])], [], [dstT_.b], dstT_.b)
            s.barrier()

        if "B" in phases:
            phase_b(nc, s, locals())
            phase_b2(nc, s, locals())
        if "C" in phases:
            phase_c(nc, s, locals())
        s.barrier()
    return nc, s


def phase_b(nc, s, L):
    from types import SimpleNamespace
    ns = SimpleNamespace(**L)
    V, A, G, PE = nc.vector, nc.scalar, nc.gpsimd, nc.tensor
    sbt, pst, nt, debug = ns.sbt, ns.pst, ns.nt, ns.debug
    identf, identb = ns.identf, ns.identb
    ISQ = 1.0 / np.sqrt(128.0)
    with ExitStack() as st:
        wtap = [sbt(st, "wtap%d" % i, [128, D], F32) for i in range(3)]
        for i in range(3):
            s.dma("sp", [(wtap[i].t[:], ns.conv_w[i:i + 1, :].to_broadcast([128, D]))], [], [wtap[i].b], wtap[i].b)
        f01 = sbt(st, "f01sb", [128, 2], F32)
        s.dma("sp", [(f01.t[:, 0:1], ns.f01d[:, :])], [], [f01.b], f01.b)
        s.op("dve", lambda: V.tensor_scalar(out=f01.t[:, 1:2], in0=f01.t[:, 0:1], scalar1=-1.0, scalar2=None, op0=ALU.add),
             [f01.b], [f01.b])
        ev = sbt(st, "ev", [128, 4], F32)
        for (c, col, usef) in ((0, 0, True), (1, 0, False), (2, 127, True), (3, 127, False)):
            if usef:
                s.op("dve", lambda: V.tensor_scalar(out=ev.t[:, c:c + 1], in0=identf.t[:, col:col + 1], scalar1=f01.t[:, 1:2],
                                                    scalar2=1.0, op0=ALU.mult, op1=ALU.add), [identf.b, f01.b], [ev.b])
            else:
                s.op("dve", lambda: V.tensor_scalar(out=ev.t[:, c:c + 1], in0=identf.t[:, col:col + 1], scalar1=-1.0,
                                                    scalar2=1.0, op0=ALU.mult, op1=ALU.add), [identf.b], [ev.b])
        cbt = sbt(st, "cbt", [128, D], F32)
        zp = sbt(st, "zp", [128, D], F32)
        zc = sbt(st, "zc", [128, D], F32)
        zn = sbt(st, "zn", [128, D], F32)
        maskn = sbt(st, "maskn", [128, 9 * 128], BF16)
        maskx = sbt(st, "maskx", [128, 9 * 128], BF16)
        s.dma("sp", [(cbt.t[:, 0:1152], ns.maskd[:, :])], [], [cbt.b], cbt.b)
        s.op("dve", lambda: V.tensor_copy(out=maskn.t[:], in_=cbt.t[:, 0:1152]), [cbt.b], [maskn.b])
        s.op("dve", lambda: V.tensor_scalar(out=zp.t[:, 0:1152], in0=cbt.t[:, 0:1152], scalar1=-NEGB, scalar2=f01.t[:, 0:1],
                                            op0=ALU.add, op1=ALU.mult), [cbt.b, f01.b], [zp.b])
        s.op("dve", lambda: V.tensor_scalar(out=maskx.t[:], in0=zp.t[:, 0:1152], scalar1=NEGB, scalar2=None, op0=ALU.add),
             [zp.b], [maskx.b])
        gqk = sbt(st, "gqk", [128, 256], F32)
        negc = sbt(st, "negc", [128, 4], F32)
        s.dma("sp", [(gqk.t[:, 0:128], ns.q_norm_g.to_broadcast([128, 128])),
                     (gqk.t[:, 128:256], ns.k_norm_g.to_broadcast([128, 128]))], [], [gqk.b], gqk.b)
        s.op("dve", lambda: V.tensor_reduce(out=negc.t[:, 0:2], in_=gqk.t[:, :].rearrange("p (a d) -> p a d", a=2), axis=AX.X,
                                            op=ALU.max, apply_absolute_value=True), [gqk.b], [negc.b])
        s.op("dve", lambda: V.tensor_tensor(out=negc.t[:, 2:3], in0=negc.t[:, 0:1], in1=negc.t[:, 1:2], op=ALU.mult), [negc.b], [negc.b])
        s.op("dve", lambda: V.tensor_scalar(out=negc.t[:, 3:4], in0=negc.t[:, 2:3], scalar1=-float(np.sqrt(128.0)) * 1.001, scalar2=None,
                                            op0=ALU.mult), [negc.b], [negc.b])
        iota16 = sbt(st, "iota16", [128, 16], F32)
        s.op("pool", lambda: G.iota(iota16.t[:], pattern=[[1, 16]], base=0, channel_multiplier=0,
                                    allow_small_or_imprecise_dtypes=True), [], [iota16.b])
        ps = [pst(st, "psB%d" % i) for i in range(8)]
        REACH = (1, 2, 8)
        kTt = [sbt(st, "kTt%d" % g, [128, 4, (2 * REACH[g] + 1) * 128], BF16) for g in range(3)]
        Vt = [sbt(st, "Vt%d" % g, [128, 2 * REACH[g] + 1, 516], BF16) for g in range(3)]
        qTt = sbt(st, "qTt", [128, 12, 128], BF16)
        Eb = [sbt(st, "Eb%d" % i, [128, 512], BF16) for i in range(2)]
        rden = sbt(st, "rden", [128, 4], F32)
        attnb = sbt(st, "attnb", [128, 512], BF16)
        attnT2 = [sbt(st, "attnT%d" % i, [128, 4, 128], BF16) for i in range(2)]
        cvb = sbt(st, "cvb", [128, D], BF16)
        convT = sbt(st, "convT", [128, 16, 128], BF16)
        mixT = convT
        xn2T = convT
        mixb = cvb
        xn2b = cvb
        wblk = [sbt(st, "wblkB%d" % i, [128, 4, 512], BF16) for i in range(1)]
        wco_r = sbt(st, "wco_r", [128, 16, D], BF16)
        s.dma("sp", [(wco_r.t[:, :, :], ns.w_co_b.t[:, :].rearrange("(k p) c -> p k c", p=128))], [ns.w_co_b.b], [wco_r.b], wco_r.b)
        sgab = [sbt(st, "sgab%d" % i, [128, 512], F32) for i in range(1)]
        sgcb = [sbt(st, "sgcb%d" % i, [128, 512], F32) for i in range(1)]
        m1 = [sbt(st, "m1_%d" % i, [128, 512], F32) for i in range(1)]
        m2 = [sbt(st, "m2_%d" % i, [128, 512], F32) for i in range(1)]
        dba = sbt(st, "dba", [128, 512], F32) if debug else None

        wseq = []
        for i in range(nt):
            for n in range(4):
                wseq.append((ns.w_ao_b, 4, n))
        wstate = {"issued": 0, "used": 0}

        def wissue():
            g = wstate["issued"]
            if g >= len(wseq):
                return
            src, nk, n = wseq[g]
            wb = wblk[0]
            s.dma("sp", [(wb.t[:, 0:nk, :], src.t[:, n * 512:(n + 1) * 512].rearrange("(k p) c -> p k c", p=128))],
                  [src.b], [wb.b], wb.b)
            wstate["issued"] += 1

        def wget():
            wissue()
            return wblk[0]

        def mask_ap(g, dl, cross):
            r = REACH[g]
            mi = g * 3 + (0 if dl == -r else (2 if dl == r else 1))
            return (maskx if cross else maskn), mi

        def att_gen(i):
                T0 = i * 128
                lo, hi = (0, 2 * TPS) if i < 2 * TPS else (2 * TPS, 3 * TPS)
                hi = min(hi, nt)
                s.dma("sp", [(qTt.t[:, :, :], ns.qT_s.t[:, :, T0:T0 + 128].rearrange("h d t -> d h t"))], [ns.qT_s.b], [qTt.b], qTt.b)
                krange = []
                for g in range(3):
                    k0 = max(lo, i - REACH[g])
                    k1 = min(hi - 1, i + REACH[g])
                    nk = k1 - k0 + 1
                    krange.append((k0, k1))
                    s.dma("sp", [(kTt[g].t[:, :, 0:nk * 128], ns.kT_s.t[4 * g:4 * g + 4, :, k0 * 128:(k1 + 1) * 128].rearrange("h d t -> d h t"))],
                          [ns.kT_s.b], [kTt[g].b], kTt[g].b)
                    s.dma("sp", [(Vt[g].t[:, 0:nk, :], ns.v_s.t[k0 * 128:(k1 + 1) * 128, g * 516:(g + 1) * 516].rearrange("(b p) c -> p b c", p=128))],
                          [ns.v_s.b], [Vt[g].b], Vt[g].b)
                yield
                bcount = 0
                for j in range(4):
                    blocks = []
                    for g in range(3):
                        k0, k1 = krange[g]
                        for kt in range(k0, k1 + 1):
                            cross = (i < 2 * TPS) and ((i < TPS) != (kt < TPS))
                            blocks.append((g, kt, kt - k0, cross))
                    O_ = ps[2 + j // 2]
                    Oj = O_.t[:, (j % 2) * 129:(j % 2) * 129 + 129]
                    nb_tot = len(blocks)
                    done = 0
                    for b0 in range(0, nb_tot, 4):
                        bl = blocks[b0:b0 + 4]
                        S_ = ps[bcount % 2]
                        E_ = Eb[bcount % 2]
                        bcount += 1

                        def smm(bl=bl, S_=S_):
                            for bi, (g, kt, ko, cross) in enumerate(bl):
                                mt, mi = mask_ap(g, kt - i, cross)
                                PE.matmul(S_.t[:, bi * 128:(bi + 1) * 128], lhsT=kTt[g].t[:, j, ko * 128:(ko + 1) * 128],
                                          rhs=qTt.t[:, 4 * g + j, :], start=True, stop=False)
                                ins = PE.matmul(S_.t[:, bi * 128:(bi + 1) * 128], lhsT=identb.t[:, :],
                                                rhs=mt.t[:, mi * 128:(mi + 1) * 128], start=False, stop=True)
                            return ins
                        s.op("pe", smm, [kTt[0].b, kTt[1].b, kTt[2].b, qTt.b, identb.b, maskn.b, maskx.b], [S_.b])
                        w = len(bl) * 128
                        s.op("act", lambda: A.activation(out=E_.t[:, 0:w], in_=S_.t[:, 0:w], func=AF.Exp, scale=ISQ, bias=negc.t[:, 3:4]),
                             [S_.b, negc.b], [E_.b])

                        def pv(bl=bl, E_=E_, done=done):
                            for bi, (g, kt, ko, cross) in enumerate(bl):
                                ins = PE.matmul(Oj, lhsT=E_.t[:, bi * 128:(bi + 1) * 128], rhs=Vt[g].t[:, ko, j * 129:(j + 1) * 129],
                                                start=(done + bi == 0), stop=(done + bi == nb_tot - 1))
                            return ins
                        s.op("pe", pv, [E_.b, Vt[0].b, Vt[1].b, Vt[2].b], [O_.b])
                        done += len(bl)
                        yield
                for hb in range(2):
                    O_ = ps[2 + hb]
                    s.op("dve", lambda: V.reciprocal(out=rden.t[:, 2 * hb:2 * hb + 2],
                                                     in_=O_.t[:, 0:258].rearrange("p (s e) -> p s e", e=129)[:, :, 128]), [O_.b], [rden.b])
                for j in range(4):
                    O_ = ps[2 + j // 2]
                    s.op("act", lambda: A.activation(out=attnb.t[:, j * 128:(j + 1) * 128], in_=O_.t[:, (j % 2) * 129:(j % 2) * 129 + 128],
                                                     func=AF.Copy, scale=rden.t[:, j:j + 1]), [O_.b, rden.b], [attnb.b])
                if debug:
                    s.op("dve", lambda: V.tensor_copy(out=dba.t[:], in_=attnb.t[:]), [attnb.b], [dba.b])
                    s.dma("pool", [(ns.dbg_attn.t[T0:T0 + 128, :], dba.t[:])], [dba.b], [ns.dbg_attn.b], dba.b)
                p_ = ps[6]
                pv_ = p_.t[:, :].bitcast(BF16)

                def tra():
                    for jj in range(4):
                        ins = PE.transpose(out=pv_[:, jj * 128:(jj + 1) * 128], in_=attnb.t[:, jj * 128:(jj + 1) * 128], identity=identb.t[:])
                    return ins
                s.op("pe", tra, [attnb.b, identb.b], [p_.b])
                s.op("act", lambda: A.copy(out=attnT2[i % 2].t[:, :, :], in_=pv_[:, 0:512].rearrange("p (k t) -> p k t", k=4)), [p_.b], [attnT2[i % 2].b])
                yield

        def conv_loads(i):
            T0 = i * 128
            s.dma("sp", [(cbt.t[:], ns.cb_s.t[T0:T0 + 128, :])], [ns.cb_s.b], [cbt.b], cbt.b)
            s.dma("sp", [(zp.t[:], ns.z_s.t[T0:T0 + 128, :])], [ns.z_s.b], [zp.b], zp.b)
            s.dma("sp", [(zc.t[:], ns.z_s.t[T0 + 1:T0 + 129, :])], [ns.z_s.b], [zc.b], zc.b)
            s.dma("sp", [(zn.t[:], ns.z_s.t[T0 + 2:T0 + 130, :])], [ns.z_s.b], [zn.b], zn.b)

        conv_loads(0)
        for _ in att_gen(0):
            pass
        agn = att_gen(1) if nt > 1 else iter(())
        next(agn, None)
        for i in range(nt):
            T0 = i * 128
            ag = agn

            def adv(k):
                for _ in range(k):
                    next(ag, None)
            if i % TPS == 0 and i > 0:
                c = 0 if i == TPS else 1
                s.op("pool", lambda: G.tensor_scalar(out=zp.t[:], in0=zp.t[:], scalar1=ev.t[:, c:c + 1], scalar2=None, op0=ALU.mult),
                     [zp.b, ev.b], [zp.b])
            if i % TPS == TPS - 1 and i < 3 * TPS - 1:
                c = 2 if i == TPS - 1 else 3
                s.op("pool", lambda: G.tensor_scalar(out=zn.t[:], in0=zn.t[:], scalar1=ev.t[:, c:c + 1], scalar2=None, op0=ALU.mult),
                     [zn.b, ev.b], [zn.b])
            s.op("pool", lambda: G.tensor_tensor(out=zp.t[:], in0=zp.t[:], in1=wtap[0].t[:], op=ALU.mult), [zp.b, wtap[0].b], [zp.b])
            s.op("dve", lambda: V.tensor_tensor(out=zc.t[:], in0=zc.t[:], in1=wtap[1].t[:], op=ALU.mult), [zc.b, wtap[1].b], [zc.b])
            s.op("dve", lambda: V.tensor_tensor(out=zn.t[:], in0=zn.t[:], in1=wtap[2].t[:], op=ALU.mult), [zn.b, wtap[2].b], [zn.b])
            s.op("dve", lambda: V.tensor_tensor(out=zc.t[:], in0=zc.t[:], in1=zn.t[:], op=ALU.add), [zc.b, zn.b], [zc.b])
            s.op("dve", lambda: V.tensor_tensor(out=zp.t[:], in0=zp.t[:], in1=zc.t[:], op=ALU.add), [zp.b, zc.b], [zp.b])
            s.op("dve", lambda: V.tensor_tensor(out=cvb.t[:], in0=zp.t[:], in1=cbt.t[:], op=ALU.mult), [zp.b, cbt.b], [cvb.b])
            if i + 1 < nt:
                conv_loads(i + 1)

            def tr16(srcT, dstT):
                for hf in range(2):
                    p_ = ps[6 + hf]
                    pv = p_.t[:, :].bitcast(BF16)

                    def tr(hf=hf, pv=pv):
                        for kk in range(8):
                            k = hf * 8 + kk
                            ins = PE.transpose(out=pv[:, kk * 128:(kk + 1) * 128], in_=srcT.t[:, k * 128:(k + 1) * 128], identity=identb.t[:])
                        return ins
                    s.op("pe", tr, [srcT.b, identb.b], [p_.b])
                    s.op("act", lambda hf=hf, pv=pv: A.copy(out=dstT.t[:, hf * 8:(hf + 1) * 8, :], in_=pv.rearrange("p (k t) -> p k t", k=8)),
                         [p_.b], [dstT.b])
            adv(8)
            tr16(cvb, convT)
            for n in range(4):
                wa = wget()
                pa, pc = ps[4], ps[5]
                sa, sc_ = sgab[0], sgcb[0]
                s.dma("sp", [(sa.t[:], ns.sga_s.t[T0:T0 + 128, n * 512:(n + 1) * 512])], [ns.sga_s.b], [sa.b], sa.b)
                s.dma("sp", [(sc_.t[:], ns.sgc_s.t[T0:T0 + 128, n * 512:(n + 1) * 512])], [ns.sgc_s.b], [sc_.b], sc_.b)

                attnT = attnT2[i % 2]

                def mma():
                    for jj in range(4):
                        ins = PE.matmul(pa.t[:, :], lhsT=attnT.t[:, jj, :], rhs=wa.t[:, jj, :], start=(jj == 0), stop=(jj == 3))
                    return ins
                s.op("pe", mma, [attnT.b, wa.b], [pa.b])

                def mmc():
                    for k in range(16):
                        ins = PE.matmul(pc.t[:, :], lhsT=convT.t[:, k, :], rhs=wco_r.t[:, k, n * 512:(n + 1) * 512], start=(k == 0), stop=(k == 15))
                    return ins
                s.op("pe", mmc, [convT.b, wco_r.b], [pc.b])
                a1, a2 = m1[0], m2[0]
                s.op("dve", lambda: V.tensor_tensor(out=a1.t[:], in0=pa.t[:, :], in1=sa.t[:], op=ALU.mult), [pa.b, sa.b], [a1.b])
                s.op("dve", lambda: V.tensor_tensor(out=a2.t[:], in0=pc.t[:, :], in1=sc_.t[:], op=ALU.mult), [pc.b, sc_.b], [a2.b])
                s.op("pool", lambda: G.tensor_tensor(out=mixb.t[:, n * 512:(n + 1) * 512], in0=a1.t[:], in1=a2.t[:], op=ALU.add),
                     [a1.b, a2.b], [mixb.b])
                adv(3)
            s.dma("pool", [(ns.mix_s.t[T0:T0 + 128, :], mixb.t[:])], [mixb.b], [ns.mix_s.b], mixb.b)
            adv(100)
            agn = att_gen(i + 2) if i + 2 < nt else iter(())
            next(agn, None)
    s.barrier()


def phase_b2(nc, s, L):
    from types import SimpleNamespace
    ns = SimpleNamespace(**L)
    V, A, G, PE = nc.vector, nc.scalar, nc.gpsimd, nc.tensor
    sbt, pst, nt, debug = ns.sbt, ns.pst, ns.nt, ns.debug
    identf, identb = ns.identf, ns.identb
    with ExitStack() as st:
        g2b = sbt(st, "g2b", [128, D], F32)
        s.dma("sp", [(g2b.t[:], ns.norm2_g.to_broadcast([128, D]))], [], [g2b.b], g2b.b)
        ps = [pst(st, "psB2_%d" % i) for i in range(8)]
        hb = [sbt(st, "hb%d" % i, [128, D], F32) for i in range(2)]
        scb = [sbt(st, "scb%d" % i, [128, D], F32) for i in range(2)]
        mixl = [sbt(st, "mixl%d" % i, [128, D], BF16) for i in range(2)]
        xn2l = [sbt(st, "xn2l%d" % i, [128, D], BF16) for i in range(2)]
        mixT = sbt(st, "mixT2", [128, 16, 128], BF16)
        xn2T = sbt(st, "xn2T2", [128, 16, 128], BF16)
        xpb = [sbt(st, "xpb%d" % i, [128, 512], F32) for i in range(2)]
        st2l = [sbt(st, "st2_%d" % i, [128, 4], F32) for i in range(2)]
        wo_r = sbt(st, "wo_r", [128, 16, D], BF16)
        wq_r = sbt(st, "wq_r", [128, 16, D], BF16)
        s.dma("sp", [(wo_r.t[:, :, :], ns.w_o_b.t[:, :].rearrange("(k p) c -> p k c", p=128))], [ns.w_o_b.b], [wo_r.b], wo_r.b)
        s.dma("sp", [(wq_r.t[:, :, :], ns.w_q_b.t[:, :].rearrange("(k p) c -> p k c", p=128))], [ns.w_q_b.b], [wq_r.b], wq_r.b)
        zc = hb[0]
        skT = sbt(st, "skT", [128, 16, 128], BF16)
        pqT = sbt(st, "pqT", [128, 16, 128], BF16)
        skb = pqT
        s.dma("sp", [(zc.t[:, :].rearrange("p (a d) -> p a d", a=16), ns.subk.rearrange("a n d -> n a d"))], [], [zc.b], zc.b)
        s.op("dve", lambda: V.tensor_copy(out=skb.t[:, :, :], in_=zc.t[:, :].rearrange("p (a d) -> p a d", a=16)), [zc.b], [skb.b])
        for hf in range(2):
            p_ = ps[6 + hf]
            pv = p_.t[:, :].bitcast(BF16)

            def trk(hf=hf, pv=pv):
                for kk in range(8):
                    ins = PE.transpose(out=pv[:, kk * 128:(kk + 1) * 128], in_=skb.t[:, hf * 8 + kk, :], identity=identb.t[:])
                return ins
            s.op("pe", trk, [skb.b, identb.b], [p_.b])
            s.op("act", lambda hf=hf, pv=pv: A.copy(out=skT.t[:, hf * 8:(hf + 1) * 8, :], in_=pv.rearrange("p (k t) -> p k t", k=8)),
                 [p_.b], [skT.b])


        def tr16(srcT, dstT):
            for hf in range(2):
                p_ = ps[6 + hf]
                pv = p_.t[:, :].bitcast(BF16)

                def tr(hf=hf, pv=pv):
                    for kk in range(8):
                        k = hf * 8 + kk
                        ins = PE.transpose(out=pv[:, kk * 128:(kk + 1) * 128], in_=srcT.t[:, k * 128:(k + 1) * 128], identity=identb.t[:])
                    return ins
                s.op("pe", tr, [srcT.b, identb.b], [p_.b])
                s.op("act", lambda hf=hf, pv=pv: A.copy(out=dstT.t[:, hf * 8:(hf + 1) * 8, :], in_=pv.rearrange("p (k t) -> p k t", k=8)),
                     [p_.b], [dstT.b])

        def adv(k):
            pass

        for i in range(nt):
            T0 = i * 128
            mixb = mixl[i % 2]
            xn2b = xn2l[i % 2]
            zn = hb[i % 2]
            cbt = scb[i % 2]
            st2 = st2l[i % 2]
            s.dma("sp", [(mixb.t[:], ns.mix_s.t[T0:T0 + 128, :])], [ns.mix_s.b], [mixb.b], mixb.b)
            tr16(mixb, mixT)
            adv(2)
            h_ = zn
            for n in range(4):
                wo = wo_r
                po = ps[4 + n % 2]
                xp = xpb[n % 2]
                s.dma("sp", [(xp.t[:], ns.xs[T0:T0 + 128, n * 512:(n + 1) * 512])], [], [xp.b], xp.b)

                def mmo():
                    for k in range(16):
                        ins = PE.matmul(po.t[:, :], lhsT=mixT.t[:, k, :], rhs=wo.t[:, k, n * 512:(n + 1) * 512], start=(k == 0), stop=(k == 15))
                    return ins
                s.op("pe", mmo, [mixT.b, wo.b], [po.b])
                s.op("dve", lambda: V.tensor_tensor(out=h_.t[:, n * 512:(n + 1) * 512], in0=po.t[:, :], in1=xp.t[:], op=ALU.add),
                     [po.b, xp.b], [h_.b])
                adv(1)
            s.dma("pool", [(ns.h_s.t[T0:T0 + 128, :], h_.t[:])], [h_.b], [ns.h_s.b], h_.b)
            s.op("act", lambda: A.activation(out=xn2b.t[:], in_=h_.t[:], func=AF.Square, accum_out=st2.t[:, 0:1]), [h_.b], [xn2b.b, st2.b])
            s.op("act", lambda: A.activation(out=st2.t[:, 1:2], in_=st2.t[:, 0:1], func=AF.Sqrt, scale=1.0 / D, bias=EPS), [st2.b], [st2.b])
            s.op("dve", lambda: V.reciprocal(out=st2.t[:, 2:3], in_=st2.t[:, 1:2]), [st2.b], [st2.b])
            s.op("dve", lambda: V.scalar_tensor_tensor(out=xn2b.t[:], in0=h_.t[:], scalar=st2.t[:, 2:3], in1=g2b.t[:], op0=ALU.mult, op1=ALU.mult),
                 [h_.b, st2.b, g2b.b], [xn2b.b])
            s.dma("pool", [(ns.xn2_s.t[T0:T0 + 128, :], xn2b.t[:])], [xn2b.b], [ns.xn2_s.b], xn2b.b)
            tr16(xn2b, xn2T)
            for n in range(4):
                wq_ = wq_r
                pq = ps[4 + n % 2]

                def mmq():
                    for c in range(4):
                        for k in range(16):
                            ins = PE.matmul(pq.t[:, c * 128:(c + 1) * 128], lhsT=wq_.t[:, k, n * 512 + c * 128:n * 512 + (c + 1) * 128], rhs=xn2T.t[:, k, :],
                                            start=(k == 0), stop=(k == 15))
                    return ins
                s.op("pe", mmq, [wq_.b, xn2T.b], [pq.b])
                s.op("act", lambda: A.copy(out=pqT.t[:, 4 * n:4 * n + 4, :], in_=pq.t[:, :].rearrange("p (c t) -> p c t", c=4)), [pq.b], [pqT.b])
            sc = cbt
            for b in range(4):
                pb = ps[b]

                def mms():
                    for c in range(4):
                        hp = 4 * b + c
                        ins = PE.matmul(pb.t[:, c * 128:(c + 1) * 128], lhsT=pqT.t[:, hp, :], rhs=skT.t[:, hp, :], start=True, stop=True)
                    return ins
                s.op("pe", mms, [pqT.b, skT.b], [pb.b])
                s.op("act", lambda: A.copy(out=sc.t[:, b * 512:(b + 1) * 512], in_=pb.t[:, :]), [pb.b], [sc.b])
            if debug:
                s.dma("pool", [(ns.dbg_sc.t[T0:T0 + 128, :], sc.t[:])], [sc.b], [ns.dbg_sc.b], sc.b)
            s.dma("pool", [(ns.sc_s.t[T0:T0 + 128, :], sc.t[:])], [sc.b], [ns.sc_s.b], sc.b)

    s.barrier()


def phase_c(nc, s, L):
    from types import SimpleNamespace
    ns = SimpleNamespace(**L)
    V, A, G, PE = nc.vector, nc.scalar, nc.gpsimd, nc.tensor
    sbt, nt = ns.sbt, ns.nt
    identb = ns.identb
    F32R = mybir.dt.float32r
    RING = 8
    with ExitStack() as st:
        csel = sbt(st, "csel", [128, 255], F32)
        s.op("dve", lambda: V.memset(csel.t[:], 0.0), [], [csel.b])
        s.op("dve", lambda: V.memset(csel.t[:, 127:128], 1.0), [csel.b], [csel.b])
        ht = [sbt(st, "ht%d" % i, [128, D], F32) for i in range(2)]
        xt = [sbt(st, "xn2t%d" % i, [128, D], BF16) for i in range(2)]
        it = [sbt(st, "idxt%d" % i, [128, 128], U32) for i in range(2)]
        gt = [sbt(st, "gtt%d" % i, [128, 128], F32) for i in range(2)]
        U = [sbt(st, "U%d" % i, [128, D], BF16) for i in range(RING)]
        Vv = [sbt(st, "Vv%d" % i, [128, D], BF16) for i in range(RING)]
        junk = sbt(st, "junkC", [128, 1024], BF16)
        hacc = sbt(st, "hacc", [128, 128, 2], F32, nb=128)
        gl = [sbt(st, "gl%d" % i, [128, 2], F32) for i in range(4)]
        Z = [sbt(st, "Z%d" % i, [128, 128], BF16) for i in range(4)]
        yt = sbt(st, "yt", [128, D], F32)
        bc_t = st.enter_context(nc.psum_tensor("bcC", [128, 2048], F32))
        bc = [T(bc_t, s.buf("bcC%d" % i)) for i in range(2)]
        out_t = st.enter_context(nc.psum_tensor("outC", [128, 2048], F32))
        outp = T(out_t, s.buf("outC"))

        identf = ns.identf
        debug = ns.debug
        sct = [sbt(st, "sct%d" % i, [128, D], F32) for i in range(2)]
        cand = sbt(st, "cand", [128, D], F32)
        oh = sbt(st, "oh", [128, D], F32)
        iota16 = sbt(st, "iota16c", [128, 16], F32)
        s.op("pool", lambda: G.iota(iota16.t[:], pattern=[[1, 16]], base=0, channel_multiplier=0,
                                    allow_small_or_imprecise_dtypes=True), [], [iota16.b])
        s16 = sbt(st, "s16", [128, 16, 16], F32)
        i16 = sbt(st, "i16", [128, 16, 16], U32)
        i16f = sbt(st, "i16f", [128, 16, 16], F32)
        work = sbt(st, "work", [128, 256], F32)
        best = sbt(st, "best", [128, 8, 16], F32)
        flat = sbt(st, "flat", [128, 8, 16], U32)
        au = sbt(st, "au", [128, 128], U32)
        bu = sbt(st, "bu", [128, 128], U32)
        af = sbt(st, "af", [128, 128], F32)
        bf_ = sbt(st, "bf", [128, 128], F32)
        e1 = sbt(st, "e1", [128, 128], F32)
        e2 = sbt(st, "e2", [128, 128], F32)
        ef = sbt(st, "ef", [128, 128], F32)
        gat = sbt(st, "gat", [128, 128], F32)
        gsum = sbt(st, "gsum", [128, 16], F32)

        def topk_gen(i):
            T0 = i * 128
            sc = sct[i % 2]
            idxo, gto = it[i % 2], gt[i % 2]
            zp = cand
            s.dma("sp", [(sc.t[:], ns.sc_s.t[T0:T0 + 128, :])], [ns.sc_s.b], [sc.b], sc.b)
            yield
            sc3 = sc.t[:, :].rearrange("p (a n) -> p a n", a=16)

            def top16(src_ap, vals, idxs, wk):
                s.op("dve", lambda: V.max(out=vals[:, 0:8], in_=src_ap), [sc.b, zp.b], [s16.b, best.b])
                yield
                s.op("dve", lambda: V.max_index(out=idxs[:, 0:8], in_max=vals[:, 0:8], in_values=src_ap), [sc.b, zp.b, s16.b, best.b], [i16.b, flat.b])
                yield
                s.op("dve", lambda: V.match_replace(out=wk, in_to_replace=vals[:, 0:8], in_values=src_ap, imm_value=-1e30),
                     [sc.b, zp.b, s16.b, best.b], [work.b])
                yield
                s.op("dve", lambda: V.max(out=vals[:, 8:16], in_=wk), [work.b], [s16.b, best.b])
                yield
                s.op("dve", lambda: V.max_index(out=idxs[:, 8:16], in_max=vals[:, 8:16], in_values=wk), [work.b, s16.b, best.b], [i16.b, flat.b])
                yield
            for hp in range(16):
                yield from top16(sc3[:, hp, :], s16.t[:, hp, :], i16.t[:, hp, :], work.t[:, 0:128])
            s.op("dve", lambda: V.tensor_copy(out=i16f.t[:, :, :], in_=i16.t[:, :, :]), [i16.b], [i16f.b])
            yield
            s4 = s16.t[:, :, :].rearrange("p (h two) k -> p h two k", two=2)
            i4 = i16f.t[:, :, :].rearrange("p (h two) k -> p h two k", two=2)
            cand4 = cand.t[:, :].rearrange("p (h a b) -> p h a b", h=8, a=16)
            s.op("pool", lambda: G.tensor_tensor(out=cand4, in0=s4[:, :, 0, :].unsqueeze(3).to_broadcast([128, 8, 16, 16]),
                                                 in1=s4[:, :, 1, :].unsqueeze(2).to_broadcast([128, 8, 16, 16]), op=ALU.add), [s16.b], [cand.b])
            yield
            cand3 = cand.t[:, :].rearrange("p (h n) -> p h n", h=8)
            for h in range(8):
                yield from top16(cand3[:, h, :], best.t[:, h, :], flat.t[:, h, :], work.t[:, 0:256])
            flat2 = flat.t[:, :, :].rearrange("p h k -> p (h k)")
            s.op("dve", lambda: V.tensor_scalar(out=au.t[:], in0=flat2, scalar1=4, scalar2=None, op0=ALU.logical_shift_right), [flat.b], [au.b])
            yield
            s.op("dve", lambda: V.tensor_scalar(out=bu.t[:], in0=flat2, scalar1=15, scalar2=None, op0=ALU.bitwise_and), [flat.b], [bu.b])
            yield
            s.op("dve", lambda: V.tensor_copy(out=af.t[:], in_=au.t[:]), [au.b], [af.b])
            yield
            s.op("dve", lambda: V.tensor_copy(out=bf_.t[:], in_=bu.t[:]), [bu.b], [bf_.b])
            yield
            oh4 = oh.t[:, :].rearrange("p (h k j) -> p h k j", h=8, k=16)
            io4 = iota16.t[:, :].unsqueeze(1).unsqueeze(1).to_broadcast([128, 8, 16, 16])
            for (xf_, half, eo) in ((af, 0, e1), (bf_, 1, e2)):
                x4 = xf_.t[:, :].rearrange("p (h k) -> p h k", h=8).unsqueeze(3).to_broadcast([128, 8, 16, 16])
                s.op("dve", lambda: V.tensor_tensor(out=oh4, in0=x4, in1=io4, op=ALU.is_equal), [xf_.b, iota16.b], [oh.b])
                yield
                s.op("pool", lambda: G.tensor_tensor(out=oh4, in0=oh4, in1=i4[:, :, half, :].unsqueeze(2).to_broadcast([128, 8, 16, 16]), op=ALU.mult),
                     [oh.b, i16f.b], [oh.b])
                yield
                s.op("dve", lambda: V.tensor_reduce(out=eo.t[:, :].rearrange("p (h k) -> p h k", h=8), in_=oh4, axis=AX.X, op=ALU.add), [oh.b], [eo.b])
                yield
            s.op("dve", lambda: V.scalar_tensor_tensor(out=ef.t[:], in0=e1.t[:], scalar=128.0, in1=e2.t[:], op0=ALU.mult, op1=ALU.add),
                 [e1.b, e2.b], [ef.b])
            yield
            g3 = gat.t[:, :].rearrange("p (h k) -> p h k", h=8)
            s.op("dve", lambda: V.tensor_tensor(out=g3, in0=best.t[:, :, :], in1=best.t[:, :, 0:1].to_broadcast([128, 8, 16]), op=ALU.subtract),
                 [best.b], [gat.b])
            yield
            s.op("act", lambda: A.activation(out=gat.t[:], in_=gat.t[:], func=AF.Exp), [gat.b], [gat.b])
            yield
            s.op("dve", lambda: V.tensor_reduce(out=gsum.t[:, 0:8], in_=g3, axis=AX.X, op=ALU.add), [gat.b], [gsum.b])
            yield
            s.op("dve", lambda: V.reciprocal(out=gsum.t[:, 8:16], in_=gsum.t[:, 0:8]), [gsum.b], [gsum.b])
            yield
            s.op("dve", lambda: V.tensor_tensor(out=g3, in0=g3, in1=gsum.t[:, 8:16].unsqueeze(2).to_broadcast([128, 8, 16]), op=ALU.mult),
                 [gat.b, gsum.b], [gat.b])
            yield
            s.op("pe", lambda: PE.transpose(out=bc_t[:, 0:128], in_=ef.t[:, :], identity=identf.t[:]), [ef.b, identf.b], [bc[0].b])
            s.op("dve", lambda: V.tensor_copy(out=idxo.t[:], in_=bc_t[:, 0:128]), [bc[0].b], [idxo.b])
            yield
            s.op("pe", lambda: PE.transpose(out=bc_t[:, 1024:1152], in_=gat.t[:, :], identity=identf.t[:]), [gat.b, identf.b], [bc[1].b])
            s.op("dve", lambda: V.tensor_copy(out=gto.t[:], in_=bc_t[:, 1024:1152]), [bc[1].b], [gto.b])
            yield
            if debug:
                s.dma("sp", [(ns.idxT_s.t[i, :, :], idxo.t[:])], [idxo.b], [ns.idxT_s.b], idxo.b)
                s.dma("sp", [(ns.gT_s.t[i, :, :], gto.t[:])], [gto.b], [ns.gT_s.b], gto.b)

        for _ in topk_gen(0):
            pass
        tg = 0
        for i in range(nt):
            T0 = i * 128
            h_, x_, i_, g_ = ht[i % 2], xt[i % 2], it[i % 2], gt[i % 2]
            tgen = topk_gen(i + 1) if i + 1 < nt else iter(())
            s.dma("sp", [(x_.t[:], ns.xn2_s.t[T0:T0 + 128, :])], [ns.xn2_s.b], [x_.b], x_.b)
            s.dma("sp", [(h_.t[:], ns.h_s.t[T0:T0 + 128, :])], [ns.h_s.b], [h_.b], h_.b)

            def matvec(t, r, zz):
                def mv():
                    for p in range(4):
                        ins = PE.matmul(outp.t[:, p * 512:(p + 1) * 512], lhsT=zz.t[:, :],
                                        rhs=Vv[r].t[:, p * 512:(p + 1) * 512], start=(t == 0), stop=(t == 127))
                    return ins
                s.op("pe", mv, [zz.b, Vv[r].b], [outp.b])
            LAG = 2
            pend = []
            for t in range(128):
                r = tg % RING
                z_ = Z[tg % 4]
                gl_ = gl[tg % 4]
                s.gather(U[r].t[:], ns.pu_b.t[:, :], i_.t[:, t:t + 1], [i_.b, ns.pu_b.b], [U[r].b], U[r].b)
                s.gather(Vv[r].t[:], ns.pv_b.t[:, :], i_.t[:, t:t + 1], [i_.b, ns.pv_b.b], [Vv[r].b], Vv[r].b)
                for hf in range(2):
                    b_ = bc[hf]

                    def bcm(hf=hf):
                        for p in range(2):
                            c0 = hf * 1024 + p * 512
                            ins = PE.matmul(bc_t[:, c0:c0 + 512], lhsT=identb.t[:, t:t + 1].to_broadcast([128, 128]),
                                            rhs=x_.t[:, c0:c0 + 512], start=True, stop=True)
                        return ins
                    s.op("pe", bcm, [x_.b, identb.b], [b_.b])
                    s.op("dve", lambda hf=hf: V.scalar_tensor_tensor(out=junk.t[:, :], in0=U[r].t[:, hf * 1024:(hf + 1) * 1024], scalar=1.0,
                                                                     in1=bc_t[:, hf * 1024:(hf + 1) * 1024], op0=ALU.mult, op1=ALU.mult,
                                                                     accum_out=hacc.t[:, t, hf:hf + 1]),
                         [U[r].b, b_.b], [junk.b, hacc.bs[t]])
                s.op("act", lambda: A.activation(out=gl_.t[:, 0:1], in_=hacc.t[:, t, 0:1], func=AF.Gelu, bias=hacc.t[:, t, 1:2]),
                     [hacc.bs[t]], [gl_.b])
                s.op("act", lambda: A.activation(out=gl_.t[:, 1:2], in_=gl_.t[:, 0:1], func=AF.Copy, scale=g_.t[:, t:t + 1]),
                     [gl_.b, g_.b], [gl_.b])
                s.op("act", lambda: A.activation(out=z_.t[:, :], in_=csel.t[:, 127 - t:255 - t], func=AF.Copy, scale=gl_.t[:, 1:2]),
                     [gl_.b, csel.b], [z_.b])
                pend.append((t, r, z_))
                if len(pend) > LAG:
                    matvec(*pend.pop(0))
                tg += 1
                next(tgen, None)
                next(tgen, None)
            while pend:
                matvec(*pend.pop(0))
            for _ in tgen:
                pass
            for p in range(4):
                s.op("dve", lambda: V.tensor_tensor(out=yt.t[:, p * 512:(p + 1) * 512], in0=outp.t[:, p * 512:(p + 1) * 512],
                                                    in1=h_.t[:, p * 512:(p + 1) * 512], op=ALU.add), [outp.b, h_.b], [yt.b])
            s.dma("sp", [(ns.ys[T0:T0 + 128, :], yt.t[:])], [yt.b], [ns.b_ys], yt.b)
    s.barrier()


def _masks():
    i = np.arange(128)[:, None]
    j = np.arange(128)[None, :]
    out = []
    for (r, deltas) in ((1, (-1, 0, 1)), (4, (-2, 0, 2)), (16, (-8, 0, 8))):
        for dl in deltas:
            rel = dl * 128 + i - j
            ok = (np.abs(rel) <= 64 * r) & (rel % r == 0)
            out.append(np.where(ok, 0.0, NEGB))
    return np.concatenate(out, axis=1).astype(np.float32)


def _rope_tables(npos):
    half = 64
    inv = (10000.0 ** (-np.arange(half, dtype=np.float32) * 2.0 / 128)).astype(np.float32)
    ang = np.arange(npos, dtype=np.float32)[:, None] * inv[None, :]
    c = np.cos(ang).astype(np.float32)
    sn = np.sin(ang).astype(np.float32)
    return np.concatenate([c, c], 1), np.concatenate([-sn, sn], 1)


def core_stream(x_prompt, x_sample, c):
    if c < 4:
        return [x_sample[c], x_prompt[c]]
    return [x_prompt[4 + 3 * (c - 4) + j] for j in range(3)]


def prep_core(inputs, c):
    seqs = core_stream(inputs["x_prompt"], inputs["x_sample"], c)
    cc4, ss4 = _rope_tables(4096)
    m = {"xs": np.ascontiguousarray(np.concatenate(seqs, 0)),
         "ccd": np.ascontiguousarray(np.concatenate([cc4[:len(q)] for q in seqs], 0)),
         "ssd": np.ascontiguousarray(np.concatenate([ss4[:len(q)] for q in seqs], 0)),
         "f01": np.full((128, 1), 1.0 if c < 4 else 0.0, np.float32),
         "maskb": _masks()}
    for k in ("norm1_g", "w_in", "q_norm_g", "k_norm_g", "w_attn_out", "conv_w", "w_conv_out", "w_o",
              "norm2_g", "peer_w_q", "peer_u", "peer_v"):
        m[k] = np.ascontiguousarray(inputs[k][0])
    m["peer_sub_keys"] = np.ascontiguousarray(inputs["peer_sub_keys"][0].reshape(16, 128, 128))
    return m


def kernel(**inputs):
    inputs = {k: np.asarray(v) for k, v in inputs.items()}
    nc, _ = build()
    in_maps = [prep_core(inputs, c) for c in range(8)]
    res = run_bass_kernel_spmd(nc, in_maps, core_ids=list(range(8)))
    yp = np.empty((16, 2048, D), np.float32)
    ysm = np.empty((4, 4096, D), np.float32)
    for c in range(8):
        y = res.results[c]["ys"]
        if c < 4:
            ysm[c] = y[0:4096]
            yp[c] = y[4096:6144]
        else:
            for j in range(3):
                yp[4 + 3 * (c - 4) + j] = y[2048 * j:2048 * (j + 1)]
    return (yp, ysm)
```

```python
import numpy as np
import ml_dtypes
from contextlib import ExitStack
import concourse.bass as bass
import concourse.mybir as mybir
from concourse.alu_op_type import AluOpType as ALU
from concourse.bass_utils import run_bass_kernel_spmd

F32 = mybir.dt.float32
BF16 = mybir.dt.bfloat16
U32 = mybir.dt.uint32
AF = mybir.ActivationFunctionType
AX = mybir.AxisListType

D = 2048
INC = 14848
NTOK = 6144
NT = NTOK // 128
TPS = 16
NEXP = 16384
EPS = 1e-6
NEGB = -30000.0


class Buf:
    __slots__ = ("name", "lw", "rd", "sem", "dcnt")

    def __init__(self, name):
        self.name = name
        self.lw = None
        self.rd = {}
        self.sem = None
        self.dcnt = 0


class Sched:
    def __init__(self, nc):
        self.nc = nc
        self.eng = {"pe": nc.tensor, "dve": nc.vector, "act": nc.scalar,
                    "pool": nc.gpsimd, "sp": nc.sync}
        self.sems = {}
        self.cnt = {}
        for e in ("pe", "dve", "act", "pool"):
            self.sems[e] = nc.semaphore("tl_" + e).__enter__()
            self.cnt[e] = 0
        self.waited = {e: {} for e in self.eng}
        self.ndsem = 0
        self.nins = 0
        self.allbufs = []

    def buf(self, name):
        b = Buf(name)
        self.allbufs.append(b)
        return b

    def _deps(self, reads, writes):
        d = {}
        for b in reads:
            if b.lw is not None:
                k, v = b.lw
                if d.get(k, 0) < v:
                    d[k] = v
        for b in writes:
            if b.lw is not None:
                k, v = b.lw
                if d.get(k, 0) < v:
                    d[k] = v
            for k, v in b.rd.items():
                if d.get(k, 0) < v:
                    d[k] = v
        return d

    def _wait(self, e, deps):
        w = self.waited[e]
        for k, v in deps.items():
            if w.get(k, 0) < v:
                self.eng[e].wait_ge(self.sems[k], v)
                w[k] = v
                self.nins += 1

    def _mark(self, tok, reads, writes):
        k, v = tok
        for b in reads:
            if b.rd.get(k, 0) < v:
                b.rd[k] = v
        for b in writes:
            b.lw = tok
            b.rd = {}

    def op(self, e, fn, reads=(), writes=()):
        d = self._deps(reads, writes)
        if e == "pe":
            d.pop("pe", None)
        self._wait(e, d)
        ins = fn()
        self.cnt[e] += 1
        ins.then_inc(self.sems[e], 1)
        self.nins += 1
        self._mark((e, self.cnt[e]), reads, writes)

    def _dsem(self, b):
        if b.sem is None:
            key = "d%d" % self.ndsem
            self.ndsem += 1
            self.sems[key] = self.nc.semaphore(key).__enter__()
            b.sem = key
        return b.sem

    def dma(self, q, pairs, reads, writes, syncbuf, **kw):
        self._wait(q, self._deps(reads, writes))
        key = self._dsem(syncbuf)
        for (o, i) in pairs:
            ins = self.eng[q].dma_start(out=o, in_=i, **kw)
            syncbuf.dcnt += 16
            ins.then_inc(self.sems[key], 16)
            self.nins += 1
        self._mark((key, syncbuf.dcnt), reads, writes)

    def gather(self, out, in_, off_ap, reads, writes, syncbuf):
        q = "pool"
        self._wait(q, self._deps(reads, writes))
        key = self._dsem(syncbuf)
        ins = self.nc.gpsimd.indirect_dma_start(
            out=out, out_offset=None, in_=in_,
            in_offset=bass.IndirectOffsetOnAxis(ap=off_ap, axis=0))
        syncbuf.dcnt += 16
        ins.then_inc(self.sems[key], 16)
        self.nins += 1
        self._mark((key, syncbuf.dcnt), reads, writes)

    def barrier(self):
        allv = {}
        for e in ("pe", "dve", "act", "pool"):
            if self.cnt[e]:
                allv[e] = self.cnt[e]
        for b in self.allbufs:
            if b.sem is not None and b.dcnt:
                allv[b.sem] = b.dcnt
        for e in self.eng:
            self._wait(e, allv)


class T:
    def __init__(self, t, b, bs=None):
        self.t = t
        self.b = b
        self.bs = bs


def build(nt=NT, phases="0ABC", debug=False):
    nc = bass.Bass("TRN2", target_bir_lowering=False)
    s = Sched(nc)
    ntok = nt * 128
    kind_dbg = "ExternalOutput" if debug else "Internal"

    def din(name, shape, dt=F32):
        return nc.dram_tensor(name, shape, dt, kind="ExternalInput").ap()

    def dscr(name, shape, dt, dbg=False):
        return T(nc.dram_tensor(name, shape, dt, kind=(kind_dbg if dbg else "Internal")).ap(), s.buf(name))

    xs = din("xs", [NTOK, D])
    ccd = din("ccd", [NTOK, 128])
    ssd = din("ssd", [NTOK, 128])
    f01d = din("f01", [128, 1])
    maskd = din("maskb", [128, 9 * 128])
    norm1_g = din("norm1_g", [1, D])
    w_in = din("w_in", [D, INC])
    q_norm_g = din("q_norm_g", [1, 128])
    k_norm_g = din("k_norm_g", [1, 128])
    w_ao = din("w_attn_out", [512, D])
    conv_w = din("conv_w", [3, D])
    w_co = din("w_conv_out", [D, D])
    w_o = din("w_o", [D, D])
    norm2_g = din("norm2_g", [1, D])
    w_q = din("peer_w_q", [D, D])
    subk = din("peer_sub_keys", [16, 128, 128])
    peer_u = din("peer_u", [NEXP, D])
    peer_v = din("peer_v", [NEXP, D])
    ys = nc.dram_tensor("ys", [NTOK, D], F32, kind="ExternalOutput").ap()
    b_ys = s.buf("ys")

    w_in_b = dscr("w_in_b", [D, INC], BF16)
    wgrp = [s.buf("w_in_grp%d" % i) for i in range(4)]

    def WGROUP(c):
        return 0 if c < 4 else (1 if c < 8 else (2 if c < 16 else 3))
    w_ao_b = dscr("w_ao_b", [512, D], BF16)
    w_co_b = dscr("w_co_b", [D, D], BF16)
    w_o_b = dscr("w_o_b", [D, D], BF16)
    w_q_b = dscr("w_q_b", [D, D], BF16)
    pu_b = dscr("pu_b", [NEXP, D], BF16)
    pv_b = dscr("pv_b", [NEXP, D], BF16)
    qT_s = dscr("qT_s", [12, 128, NTOK], BF16, True)
    kT_s = dscr("kT_s", [12, 128, NTOK], BF16, True)
    v_s = dscr("v_s", [NTOK, 12 * 129], BF16, True)
    cb_s = dscr("cb_s", [NTOK, D], F32, True)
    z_s = dscr("z_s", [NTOK + 2, D], F32, True)
    sga_s = dscr("sga_s", [NTOK, D], F32, True)
    sgc_s = dscr("sgc_s", [NTOK, D], F32, True)
    h_s = dscr("h_s", [NTOK, D], F32, True)
    xn2_s = dscr("xn2_s", [NTOK, D], BF16, True)
    idxT_s = dscr("idxT_s", [NT, 128, 128], U32, True)
    gT_s = dscr("gT_s", [NT, 128, 128], F32, True)
    sc_s = dscr("sc_s", [NTOK, 2048], F32)
    mix_s = dscr("mix_s", [NTOK, D], BF16)
    s16_s = dscr("s16_s", [NT, 128, 256], F32)
    i16_s = dscr("i16_s", [NT, 128, 256], U32)
    dbg_attn = dscr("dbg_attn", [NTOK, 512], F32, True) if debug else None
    dbg_sc = dscr("dbg_sc", [NTOK, 2048], F32, True) if debug else None

    def sbt(st, name, shape, dt, nb=0):
        t = st.enter_context(nc.sbuf_tensor(name, shape, dt))
        bs = [s.buf(name + "_%d" % i) for i in range(nb)] if nb else None
        return T(t, s.buf(name), bs)

    def pst(st, name):
        t = st.enter_context(nc.psum_tensor(name, [128, 512], F32))
        return T(t, s.buf(name))

    V = nc.vector
    A = nc.scalar
    G = nc.gpsimd
    PE = nc.tensor

    with ExitStack() as gst:
        identf = sbt(gst, "identf", [128, 128], F32)
        identb = sbt(gst, "identb", [128, 128], BF16)
        s.op("pool", lambda: G.memset(identf.t[:], 0.0), [], [identf.b])
        s.op("pool", lambda: G.affine_select(out=identf.t[:], in_=identf.t[:], pattern=[[-1, 128]],
                                             compare_op=ALU.not_equal, fill=1.0, base=0,
                                             channel_multiplier=1), [identf.b], [identf.b])
        s.op("dve", lambda: V.tensor_copy(out=identb.t[:], in_=identf.t[:]), [identf.b], [identb.b])

        if "0" in phases:
            def cast(dst, src, rows, cols, chunk):
                vd = dst.t.rearrange("r (a b) -> (r a) b", b=chunk)
                vs = src.rearrange("r (a b) -> (r a) b", b=chunk)
                n = rows * cols // chunk
                per = 4096
                for r0 in range(0, n, per):
                    r1 = min(n, r0 + per)
                    s.dma("pool", [(vd[r0:r1, :], vs[r0:r1, :])], [], [dst.b], dst.b)
            for cj in range(8):
                gbuf = wgrp[WGROUP(cj * 4)]
                c0, c1 = cj * 2048, min(INC, (cj + 1) * 2048)
                s.dma("pool", [(w_in_b.t[:, c0:c1], w_in[:, c0:c1])], [], [gbuf], gbuf)
            cast(w_ao_b, w_ao, 512, D, 2048)
            cast(w_co_b, w_co, D, D, 2048)
            cast(w_o_b, w_o, D, D, 2048)
            cast(w_q_b, w_q, D, D, 2048)

        if "A" in phases:
            with ExitStack() as st:
                g1b = sbt(st, "g1b", [128, D], F32)
                gq = sbt(st, "gq", [128, 256], F32)
                gk = sbt(st, "gk", [128, 256], F32)
                s.dma("sp", [(g1b.t[:], norm1_g.to_broadcast([128, D]))], [], [g1b.b], g1b.b)
                s.dma("sp", [(gq.t[:, 0:128], q_norm_g.to_broadcast([128, 128])),
                             (gq.t[:, 128:192], q_norm_g[:, 64:128].to_broadcast([128, 64])),
                             (gq.t[:, 192:256], q_norm_g[:, 0:64].to_broadcast([128, 64]))], [], [gq.b], gq.b)
                s.dma("sp", [(gk.t[:, 0:128], k_norm_g.to_broadcast([128, 128])),
                             (gk.t[:, 128:192], k_norm_g[:, 64:128].to_broadcast([128, 64])),
                             (gk.t[:, 192:256], k_norm_g[:, 0:64].to_broadcast([128, 64]))], [], [gk.b], gk.b)

                xt = [sbt(st, "xt%d" % i, [128, D], F32) for i in range(2)]
                rope = [sbt(st, "rope%d" % i, [128, 256], F32) for i in range(2)]
                xnb = [sbt(st, "xnb%d" % i, [128, D], BF16) for i in range(2)]
                junk = sbt(st, "junkA", [128, 512], BF16)
                stat = [sbt(st, "stat%d" % i, [128, 4], F32) for i in range(2)]
                xnT = [sbt(st, "xnT%d" % i, [128, 16, 512], BF16, nb=4) for i in range(2)]
                tabs = [sbt(st, "tabs%d" % i, [128, 4, 4, 128], F32, nb=4) for i in range(1)]
                wblk = [sbt(st, "wblk%d" % i, [128, 16, 512], BF16) for i in range(2)]
                ps = [pst(st, "psA%d" % i) for i in range(8)]
                qs = [sbt(st, "qs%d" % i, [128, 512], F32) for i in range(2)]
                qst = [sbt(st, "qst%d" % i, [128, 8], F32) for i in range(2)]
                tA = [sbt(st, "tA%d" % i, [128, 512], F32) for i in range(2)]
                tB = [sbt(st, "tB%d" % i, [128, 512], F32) for i in range(1)]
                ob = [sbt(st, "ob%d" % i, [128, 512], BF16) for i in range(2)]
                qTst = [sbt(st, "qTst%d" % i, [128, 12, 512], BF16) for i in range(1)]
                kTst = [sbt(st, "kTst%d" % i, [128, 12, 512], BF16) for i in range(1)]
                vst = [sbt(st, "vst%d" % i, [128, 4, 12 * 129], BF16) for i in range(1)]
                fst = [sbt(st, "fst%d" % i, [128, 4, 512], F32, nb=4) for i in range(2)]
                ccst = sbt(st, "ccst", [128, 4, D], F32, nb=16)
                s.op("dve", lambda: V.memset(vst[0].t[:], 1.0), [], [vst[0].b])
                s.op("dve", lambda: V.memset(fst[0].t[0:1, :, :], 0.0), [], fst[0].bs)
                s.dma("sp", [(z_s.t[0:1, :], fst[0].t[0:1, :, :]), (z_s.t[NTOK + 1:NTOK + 2, :], fst[0].t[0:1, :, :])],
                      fst[0].bs, [z_s.b], fst[0].b)

                nmac = nt // 4
                uvq = [(dd, ss_, r0) for (dd, ss_) in ((pu_b, peer_u), (pv_b, peer_v)) for r0 in range(0, NEXP, 4096)]

                def wload(gb):
                    wb = wblk[gb % 2]
                    c = gb % 29
                    s.dma("sp", [(wb.t[:], w_in_b.t[:, c * 512:(c + 1) * 512].rearrange("(k p) c -> p k c", p=128))],
                          [wgrp[WGROUP(c)]], [wb.b], wb.b)
                psi = 0
                pti = 0
                fsi = 0
                qi = 0
                for m in range(nmac):
                    xT = xnT[m % 2]
                    tb = tabs[0]
                    if uvq and m >= 1:
                        (dstT_, src_, r0) = uvq.pop(0)
                        s.dma("pool", [(dstT_.t[r0:r0 + 4096, :], src_[r0:r0 + 4096, :])], [xt[(m * 4) % 2].b], [dstT_.b], dstT_.b)
                    for sti in range(4):
                        i = m * 4 + sti
                        T0 = i * 128
                        x_ = xt[i % 2]
                        r_ = rope[i % 2]
                        xb_ = xnb[i % 2]
                        st_ = stat[i % 2]
                        s.dma("sp", [(x_.t[:], xs[T0:T0 + 128, :])], [], [x_.b], x_.b)
                        s.dma("sp", [(r_.t[:, 0:128], ccd[T0:T0 + 128, :]), (r_.t[:, 128:256], ssd[T0:T0 + 128, :])],
                              [], [r_.b], r_.b)
                        tbv = tb.t
                        s.op("dve", lambda: V.tensor_tensor(out=tbv[:, sti, 0, :], in0=r_.t[:, 0:128], in1=gq.t[:, 0:128], op=ALU.mult),
                             [r_.b, gq.b], [tb.bs[sti]])
                        s.op("dve", lambda: V.tensor_tensor(out=tbv[:, sti, 1, :], in0=r_.t[:, 128:256], in1=gq.t[:, 128:256], op=ALU.mult),
                             [r_.b, gq.b], [tb.bs[sti]])
                        s.op("dve", lambda: V.tensor_tensor(out=tbv[:, sti, 2, :], in0=r_.t[:, 0:128], in1=gk.t[:, 0:128], op=ALU.mult),
                             [r_.b, gk.b], [tb.bs[sti]])
                        s.op("dve", lambda: V.tensor_tensor(out=tbv[:, sti, 3, :], in0=r_.t[:, 128:256], in1=gk.t[:, 128:256], op=ALU.mult),
                             [r_.b, gk.b], [tb.bs[sti]])
                        s.op("act", lambda: A.activation(out=xb_.t[:], in_=x_.t[:], func=AF.Square, accum_out=st_.t[:, 0:1]),
                             [x_.b], [xb_.b, st_.b])
                        s.op("act", lambda: A.activation(out=st_.t[:, 1:2], in_=st_.t[:, 0:1], func=AF.Sqrt, scale=1.0 / D, bias=EPS),
                             [st_.b], [st_.b])
                        s.op("dve", lambda: V.reciprocal(out=st_.t[:, 2:3], in_=st_.t[:, 1:2]), [st_.b], [st_.b])
                        s.op("dve", lambda: V.scalar_tensor_tensor(out=xb_.t[:], in0=x_.t[:], scalar=st_.t[:, 2:3], in1=g1b.t[:],
                                                                   op0=ALU.mult, op1=ALU.mult),
                             [x_.b, st_.b, g1b.b], [xb_.b])
                        for hf in range(2):
                            p_ = ps[6 + hf]
                            pv = p_.t[:, :].bitcast(BF16)

                            def tr(hf=hf, pv=pv):
                                for kk in range(8):
                                    k = hf * 8 + kk
                                    ins = PE.transpose(out=pv[:, kk * 128:(kk + 1) * 128], in_=xb_.t[:, k * 128:(k + 1) * 128],
                                                       identity=identb.t[:])
                                return ins
                            s.op("pe", tr, [xb_.b, identb.b], [p_.b])
                            s.op("act", lambda hf=hf, pv=pv: A.copy(out=xT.t[:, hf * 8:(hf + 1) * 8, sti * 128:(sti + 1) * 128],
                                                                    in_=pv.rearrange("p (k t) -> p k t", k=8)),
                                 [p_.b], [xT.bs[sti]])
                    for cb in range(29):
                        gb = m * 29 + cb
                        if gb == 0:
                            wload(0)
                        if gb + 1 < nmac * 29:
                            wload(gb + 1)
                        wb = wblk[gb % 2]
                        typ = cb // 3 if cb < 9 else 3 + (cb - 9) // 4
                        sub = cb % 3 if cb < 9 else (cb - 9) % 4
                        if typ >= 3 and typ != 4:
                            f_ = fst[fsi % 2]
                            fsi += 1
                        for sti in range(4):
                            i = m * 4 + sti
                            p_ = ps[psi % 6]
                            psi += 1

                            def mm(p_=p_, sti=sti):
                                for k in range(16):
                                    ins = PE.matmul(p_.t[:, :], lhsT=xT.t[:, k, sti * 128:(sti + 1) * 128], rhs=wb.t[:, k, :],
                                                    start=(k == 0), stop=(k == 15))
                                return ins
                            s.op("pe", mm, [xT.bs[sti], wb.b], [p_.b])
                            if typ <= 1:
                                q_ = qs[qi % 2]
                                qs_ = qst[qi % 2]
                                a_ = tA[qi % 2]
                                b_ = tB[0]
                                o_ = ob[qi % 2]
                                qi += 1
                                tg = tb.t[:, sti, 2 * typ, :]
                                tsn = tb.t[:, sti, 2 * typ + 1, :]
                                s.op("act", lambda: A.copy(out=q_.t[:], in_=p_.t[:, :]), [p_.b], [q_.b])

                                def sq():
                                    for h in range(4):
                                        ins = A.activation(out=junk.t[:, h * 128:(h + 1) * 128], in_=p_.t[:, h * 128:(h + 1) * 128],
                                                           func=AF.Square, accum_out=qs_.t[:, h:h + 1])
                                    return ins
                                s.op("act", sq, [p_.b], [junk.b, qs_.b])
                                s.op("act", lambda: A.activation(out=qs_.t[:, 4:8], in_=qs_.t[:, 0:4], func=AF.Sqrt, scale=1.0 / 128, bias=EPS),
                                     [qs_.b], [qs_.b])
                                s.op("dve", lambda: V.reciprocal(out=qs_.t[:, 0:4], in_=qs_.t[:, 4:8]), [qs_.b], [qs_.b])
                                q3 = q_.t[:, :].rearrange("p (h d) -> p h d", h=4)
                                a3 = a_.t[:, :].rearrange("p (h d) -> p h d", h=4)
                                b3 = b_.t[:, :].rearrange("p (h d) -> p h d", h=4)
                                s.op("dve", lambda: V.tensor_tensor(out=a3, in0=q3, in1=tg.unsqueeze(1).to_broadcast([128, 4, 128]), op=ALU.mult),
                                     [q_.b, tb.bs[sti]], [a_.b])
                                s.op("dve", lambda: V.tensor_tensor(out=b3[:, :, 0:64], in0=q3[:, :, 64:128],
                                                                    in1=tsn[:, 0:64].unsqueeze(1).to_broadcast([128, 4, 64]), op=ALU.mult),
                                     [q_.b, tb.bs[sti]], [b_.b])
                                s.op("dve", lambda: V.tensor_tensor(out=b3[:, :, 64:128], in0=q3[:, :, 0:64],
                                                                    in1=tsn[:, 64:128].unsqueeze(1).to_broadcast([128, 4, 64]), op=ALU.mult),
                                     [q_.b, tb.bs[sti]], [b_.b])
                                s.op("dve", lambda: V.tensor_tensor(out=a_.t[:, :], in0=a_.t[:, :], in1=b_.t[:, :], op=ALU.add),
                                     [a_.b, b_.b], [a_.b])
                                s.op("dve", lambda: V.tensor_tensor(out=o_.t[:, :].rearrange("p (h d) -> p h d", h=4), in0=a3,
                                                                    in1=qs_.t[:, 0:4].unsqueeze(2).to_broadcast([128, 4, 128]), op=ALU.mult),
                                     [a_.b, qs_.b], [o_.b])
                                pt_ = ps[6 + pti % 2]
                                pti += 1
                                ptv = pt_.t[:, :].bitcast(BF16)

                                def tr2():
                                    for h in range(4):
                                        ins = PE.transpose(out=ptv[:, h * 128:(h + 1) * 128], in_=o_.t[:, h * 128:(h + 1) * 128],
                                                           identity=identb.t[:])
                                    return ins
                                s.op("pe", tr2, [o_.b, identb.b], [pt_.b])
                                dstT = (qTst if typ == 0 else kTst)[0]
                                s.op("act", lambda: A.copy(out=dstT.t[:, sub * 4:(sub + 1) * 4, sti * 128:(sti + 1) * 128],
                                                           in_=ptv[:, 0:512].rearrange("p (h t) -> p h t", h=4)),
                                     [pt_.b], [dstT.b])
                            elif typ == 2:
                                v_ = vst[0]
                                vv = v_.t[:, sti, :].rearrange("p (h e) -> p h e", e=129)
                                s.op("act", lambda: A.copy(out=vv[:, sub * 4:(sub + 1) * 4, 0:128],
                                                           in_=p_.t[:, :].rearrange("p (h d) -> p h d", h=4)),
                                     [p_.b], [v_.b])
                            elif typ == 3:
                                s.op("act", lambda: A.copy(out=f_.t[:, sti, :], in_=p_.t[:, :]), [p_.b], [f_.bs[sti]])
                            elif typ == 4:
                                s.op("act", lambda: A.copy(out=ccst.t[:, sti, sub * 512:(sub + 1) * 512], in_=p_.t[:, :]),
                                     [p_.b], [ccst.bs[sti * 4 + sub]])
                            elif typ == 5:
                                s.op("dve", lambda: V.tensor_tensor(out=f_.t[:, sti, :], in0=p_.t[:, :],
                                                                    in1=ccst.t[:, sti, sub * 512:(sub + 1) * 512], op=ALU.mult),
                                     [p_.b, ccst.bs[sti * 4 + sub]], [f_.bs[sti]])
                            else:
                                s.op("act", lambda: A.activation(out=f_.t[:, sti, :], in_=p_.t[:, :], func=AF.Sigmoid),
                                     [p_.b], [f_.bs[sti]])
                        R0 = m * 512
                        if typ >= 3 and typ != 4:
                            dst = {3: cb_s, 5: z_s, 6: sga_s, 7: sgc_s}[typ]
                            ro = 1 if typ == 5 else 0
                            s.dma("sp", [(dst.t[ro + R0:ro + R0 + 512, sub * 512:(sub + 1) * 512].rearrange("(s p) c -> p s c", p=128),
                                          f_.t[:, :, :])], f_.bs, [dst.b], f_.b)
                        if cb == 5:
                            for (src, dst) in ((qTst[0], qT_s), (kTst[0], kT_s)):
                                s.dma("sp", [(dst.t[:, :, R0:R0 + 512].rearrange("h d t -> d h t"), src.t[:, :, :])],
                                      [src.b], [dst.b], src.b)
                        if cb == 8:
                            v_ = vst[0]
                            s.dma("sp", [(v_s.t[R0:R0 + 512, :].rearrange("(s p) c -> p s c", p=128), v_.t[:, :, :])],
                                  [v_.b], [v_s.b], v_.b)
                while uvq:
                    (dstT_, src_, r0) = uvq.pop(0)
                    s.dma("pool", [(dstT_.t[r0:r0 + 4096, :], src_[r0:r0 + 4096, :])], [], [dstT_.b], dstT_.b)
            s.barrier()

        if "B" in phases:
            phase_b(nc, s, locals())
            phase_b2(nc, s, locals())
        if "C" in phases:
            phase_c(nc, s, locals())
        s.barrier()
    return nc, s


def phase_b(nc, s, L):
    from types import SimpleNamespace
    ns = SimpleNamespace(**L)
    V, A, G, PE = nc.vector, nc.scalar, nc.gpsimd, nc.tensor
    sbt, pst, nt, debug = ns.sbt, ns.pst, ns.nt, ns.debug
    identf, identb = ns.identf, ns.identb
    ISQ = 1.0 / np.sqrt(128.0)
    with ExitStack() as st:
        wtap = [sbt(st, "wtap%d" % i, [128, D], F32) for i in range(3)]
        for i in range(3):
            s.dma("sp", [(wtap[i].t[:], ns.conv_w[i:i + 1, :].to_broadcast([128, D]))], [], [wtap[i].b], wtap[i].b)
        f01 = sbt(st, "f01sb", [128, 2], F32)
        s.dma("sp", [(f01.t[:, 0:1], ns.f01d[:, :])], [], [f01.b], f01.b)
        s.op("dve", lambda: V.tensor_scalar(out=f01.t[:, 1:2], in0=f01.t[:, 0:1], scalar1=-1.0, scalar2=None, op0=ALU.add),
             [f01.b], [f01.b])
        ev = sbt(st, "ev", [128, 4], F32)
        for (c, col, usef) in ((0, 0, True), (1, 0, False), (2, 127, True), (3, 127, False)):
            if usef:
                s.op("dve", lambda: V.tensor_scalar(out=ev.t[:, c:c + 1], in0=identf.t[:, col:col + 1], scalar1=f01.t[:, 1:2],
                                                    scalar2=1.0, op0=ALU.mult, op1=ALU.add), [identf.b, f01.b], [ev.b])
            else:
                s.op("dve", lambda: V.tensor_scalar(out=ev.t[:, c:c + 1], in0=identf.t[:, col:col + 1], scalar1=-1.0,
                                                    scalar2=1.0, op0=ALU.mult, op1=ALU.add), [identf.b], [ev.b])
        cbt = sbt(st, "cbt", [128, D], F32)
        zp = sbt(st, "zp", [128, D], F32)
        zc = sbt(st, "zc", [128, D], F32)
        zn = sbt(st, "zn", [128, D], F32)
        maskn = sbt(st, "maskn", [128, 9 * 128], BF16)
        maskx = sbt(st, "maskx", [128, 9 * 128], BF16)
        s.dma("sp", [(cbt.t[:, 0:1152], ns.maskd[:, :])], [], [cbt.b], cbt.b)
        s.op("dve", lambda: V.tensor_copy(out=maskn.t[:], in_=cbt.t[:, 0:1152]), [cbt.b], [maskn.b])
        s.op("dve", lambda: V.tensor_scalar(out=zp.t[:, 0:1152], in0=cbt.t[:, 0:1152], scalar1=-NEGB, scalar2=f01.t[:, 0:1],
                                            op0=ALU.add, op1=ALU.mult), [cbt.b, f01.b], [zp.b])
        s.op("dve", lambda: V.tensor_scalar(out=maskx.t[:], in0=zp.t[:, 0:1152], scalar1=NEGB, scalar2=None, op0=ALU.add),
             [zp.b], [maskx.b])
        gqk = sbt(st, "gqk", [128, 256], F32)
        negc = sbt(st, "negc", [128, 4], F32)
        s.dma("sp", [(gqk.t[:, 0:128], ns.q_norm_g.to_broadcast([128, 128])),
                     (gqk.t[:, 128:256], ns.k_norm_g.to_broadcast([128, 128]))], [], [gqk.b], gqk.b)
        s.op("dve", lambda: V.tensor_reduce(out=negc.t[:, 0:2], in_=gqk.t[:, :].rearrange("p (a d) -> p a d", a=2), axis=AX.X,
                                            op=ALU.max, apply_absolute_value=True), [gqk.b], [negc.b])
        s.op("dve", lambda: V.tensor_tensor(out=negc.t[:, 2:3], in0=negc.t[:, 0:1], in1=negc.t[:, 1:2], op=ALU.mult), [negc.b], [negc.b])
        s.op("dve", lambda: V.tensor_scalar(out=negc.t[:, 3:4], in0=negc.t[:, 2:3], scalar1=-float(np.sqrt(128.0)) * 1.001, scalar2=None,
                                            op0=ALU.mult), [negc.b], [negc.b])
        iota16 = sbt(st, "iota16", [128, 16], F32)
        s.op("pool", lambda: G.iota(iota16.t[:], pattern=[[1, 16]], base=0, channel_multiplier=0,
                                    allow_small_or_imprecise_dtypes=True), [], [iota16.b])
        ps = [pst(st, "psB%d" % i) for i in range(8)]
        REACH = (1, 2, 8)
        kTt = [sbt(st, "kTt%d" % g, [128, 4, (2 * REACH[g] + 1) * 128], BF16) for g in range(3)]
        Vt = [sbt(st, "Vt%d" % g, [128, 2 * REACH[g] + 1, 516], BF16) for g in range(3)]
        qTt = sbt(st, "qTt", [128, 12, 128], BF16)
        Eb = [sbt(st, "Eb%d" % i, [128, 512], BF16) for i in range(2)]
        rden = sbt(st, "rden", [128, 4], F32)
        attnb = sbt(st, "attnb", [128, 512], BF16)
        attnT2 = [sbt(st, "attnT%d" % i, [128, 4, 128], BF16) for i in range(2)]
        cvb = sbt(st, "cvb", [128, D], BF16)
        convT = sbt(st, "convT", [128, 16, 128], BF16)
        mixT = convT
        xn2T = convT
        mixb = cvb
        xn2b = cvb
        wblk = [sbt(st, "wblkB%d" % i, [128, 4, 512], BF16) for i in range(1)]
        wco_r = sbt(st, "wco_r", [128, 16, D], BF16)
        s.dma("sp", [(wco_r.t[:, :, :], ns.w_co_b.t[:, :].rearrange("(k p) c -> p k c", p=128))], [ns.w_co_b.b], [wco_r.b], wco_r.b)
        sgab = [sbt(st, "sgab%d" % i, [128, 512], F32) for i in range(1)]
        sgcb = [sbt(st, "sgcb%d" % i, [128, 512], F32) for i in range(1)]
        m1 = [sbt(st, "m1_%d" % i, [128, 512], F32) for i in range(1)]
        m2 = [sbt(st, "m2_%d" % i, [128, 512], F32) for i in range(1)]
        dba = sbt(st, "dba", [128, 512], F32) if debug else None

        wseq = []
        for i in range(nt):
            for n in range(4):
                wseq.append((ns.w_ao_b, 4, n))
        wstate = {"issued": 0, "used": 0}

        def wissue():
            g = wstate["issued"]
            if g >= len(wseq):
                return
            src, nk, n = wseq[g]
            wb = wblk[0]
            s.dma("sp", [(wb.t[:, 0:nk, :], src.t[:, n * 512:(n + 1) * 512].rearrange("(k p) c -> p k c", p=128))],
                  [src.b], [wb.b], wb.b)
            wstate["issued"] += 1

        def wget():
            wissue()
            return wblk[0]

        def mask_ap(g, dl, cross):
            r = REACH[g]
            mi = g * 3 + (0 if dl == -r else (2 if dl == r else 1))
            return (maskx if cross else maskn), mi

        def att_gen(i):
                T0 = i * 128
                lo, hi = (0, 2 * TPS) if i < 2 * TPS else (2 * TPS, 3 * TPS)
                hi = min(hi, nt)
                s.dma("sp", [(qTt.t[:, :, :], ns.qT_s.t[:, :, T0:T0 + 128].rearrange("h d t -> d h t"))], [ns.qT_s.b], [qTt.b], qTt.b)
                krange = []
                for g in range(3):
                    k0 = max(lo, i - REACH[g])
                    k1 = min(hi - 1, i + REACH[g])
                    nk = k1 - k0 + 1
                    krange.append((k0, k1))
                    s.dma("sp", [(kTt[g].t[:, :, 0:nk * 128], ns.kT_s.t[4 * g:4 * g + 4, :, k0 * 128:(k1 + 1) * 128].rearrange("h d t -> d h t"))],
                          [ns.kT_s.b], [kTt[g].b], kTt[g].b)
                    s.dma("sp", [(Vt[g].t[:, 0:nk, :], ns.v_s.t[k0 * 128:(k1 + 1) * 128, g * 516:(g + 1) * 516].rearrange("(b p) c -> p b c", p=128))],
                          [ns.v_s.b], [Vt[g].b], Vt[g].b)
                yield
                bcount = 0
                for j in range(4):
                    blocks = []
                    for g in range(3):
                        k0, k1 = krange[g]
                        for kt in range(k0, k1 + 1):
                            cross = (i < 2 * TPS) and ((i < TPS) != (kt < TPS))
                            blocks.append((g, kt, kt - k0, cross))
                    O_ = ps[2 + j // 2]
                    Oj = O_.t[:, (j % 2) * 129:(j % 2) * 129 + 129]
                    nb_tot = len(blocks)
                    done = 0
                    for b0 in range(0, nb_tot, 4):
                        bl = blocks[b0:b0 + 4]
                        S_ = ps[bcount % 2]
                        E_ = Eb[bcount % 2]
                        bcount += 1

                        def smm(bl=bl, S_=S_):
                            for bi, (g, kt, ko, cross) in enumerate(bl):
                                mt, mi = mask_ap(g, kt - i, cross)
                                PE.matmul(S_.t[:, bi * 128:(bi + 1) * 128], lhsT=kTt[g].t[:, j, ko * 128:(ko + 1) * 128],
                                          rhs=qTt.t[:, 4 * g + j, :], start=True, stop=False)
                                ins = PE.matmul(S_.t[:, bi * 128:(bi + 1) * 128], lhsT=identb.t[:, :],
                                                rhs=mt.t[:, mi * 128:(mi + 1) * 128], start=False, stop=True)
                            return ins
                        s.op("pe", smm, [kTt[0].b, kTt[1].b, kTt[2].b, qTt.b, identb.b, maskn.b, maskx.b], [S_.b])
                        w = len(bl) * 128
                        s.op("act", lambda: A.activation(out=E_.t[:, 0:w], in_=S_.t[:, 0:w], func=AF.Exp, scale=ISQ, bias=negc.t[:, 3:4]),
                             [S_.b, negc.b], [E_.b])

                        def pv(bl=bl, E_=E_, done=done):
                            for bi, (g, kt, ko, cross) in enumerate(bl):
                                ins = PE.matmul(Oj, lhsT=E_.t[:, bi * 128:(bi + 1) * 128], rhs=Vt[g].t[:, ko, j * 129:(j + 1) * 129],
                                                start=(done + bi == 0), stop=(done + bi == nb_tot - 1))
                            return ins
                        s.op("pe", pv, [E_.b, Vt[0].b, Vt[1].b, Vt[2].b], [O_.b])
                        done += len(bl)
                        yield
                for hb in range(2):
                    O_ = ps[2 + hb]
                    s.op("dve", lambda: V.reciprocal(out=rden.t[:, 2 * hb:2 * hb + 2],
                                                     in_=O_.t[:, 0:258].rearrange("p (s e) -> p s e", e=129)[:, :, 128]), [O_.b], [rden.b])
                for j in range(4):
                    O_ = ps[2 + j // 2]
                    s.op("act", lambda: A.activation(out=attnb.t[:, j * 128:(j + 1) * 128], in_=O_.t[:, (j % 2) * 129:(j % 2) * 129 + 128],
                                                     func=AF.Copy, scale=rden.t[:, j:j + 1]), [O_.b, rden.b], [attnb.b])
                if debug:
                    s.op("dve", lambda: V.tensor_copy(out=dba.t[:], in_=attnb.t[:]), [attnb.b], [dba.b])
                    s.dma("pool", [(ns.dbg_attn.t[T0:T0 + 128, :], dba.t[:])], [dba.b], [ns.dbg_attn.b], dba.b)
                p_ = ps[6]
                pv_ = p_.t[:, :].bitcast(BF16)

                def tra():
                    for jj in range(4):
                        ins = PE.transpose(out=pv_[:, jj * 128:(jj + 1) * 128], in_=attnb.t[:, jj * 128:(jj + 1) * 128], identity=identb.t[:])
                    return ins
                s.op("pe", tra, [attnb.b, identb.b], [p_.b])
                s.op("act", lambda: A.copy(out=attnT2[i % 2].t[:, :, :], in_=pv_[:, 0:512].rearrange("p (k t) -> p k t", k=4)), [p_.b], [attnT2[i % 2].b])
                yield

        def conv_loads(i):
            T0 = i * 128
            s.dma("sp", [(cbt.t[:], ns.cb_s.t[T0:T0 + 128, :])], [ns.cb_s.b], [cbt.b], cbt.b)
            s.dma("sp", [(zp.t[:], ns.z_s.t[T0:T0 + 128, :])], [ns.z_s.b], [zp.b], zp.b)
            s.dma("sp", [(zc.t[:], ns.z_s.t[T0 + 1:T0 + 129, :])], [ns.z_s.b], [zc.b], zc.b)
            s.dma("sp", [(zn.t[:], ns.z_s.t[T0 + 2:T0 + 130, :])], [ns.z_s.b], [zn.b], zn.b)

        conv_loads(0)
        for _ in att_gen(0):
            pass
        agn = att_gen(1) if nt > 1 else iter(())
        next(agn, None)
        for i in range(nt):
            T0 = i * 128
            ag = agn

            def adv(k):
                for _ in range(k):
                    next(ag, None)
            if i % TPS == 0 and i > 0:
                c = 0 if i == TPS else 1
                s.op("pool", lambda: G.tensor_scalar(out=zp.t[:], in0=zp.t[:], scalar1=ev.t[:, c:c + 1], scalar2=None, op0=ALU.mult),
                     [zp.b, ev.b], [zp.b])
            if i % TPS == TPS - 1 and i < 3 * TPS - 1:
                c = 2 if i == TPS - 1 else 3
                s.op("pool", lambda: G.tensor_scalar(out=zn.t[:], in0=zn.t[:], scalar1=ev.t[:, c:c + 1], scalar2=None, op0=ALU.mult),
                     [zn.b, ev.b], [zn.b])
            s.op("pool", lambda: G.tensor_tensor(out=zp.t[:], in0=zp.t[:], in1=wtap[0].t[:], op=ALU.mult), [zp.b, wtap[0].b], [zp.b])
            s.op("dve", lambda: V.tensor_tensor(out=zc.t[:], in0=zc.t[:], in1=wtap[1].t[:], op=ALU.mult), [zc.b, wtap[1].b], [zc.b])
            s.op("dve", lambda: V.tensor_tensor(out=zn.t[:], in0=zn.t[:], in1=wtap[2].t[:], op=ALU.mult), [zn.b, wtap[2].b], [zn.b])
            s.op("dve", lambda: V.tensor_tensor(out=zc.t[:], in0=zc.t[:], in1=zn.t[:], op=ALU.add), [zc.b, zn.b], [zc.b])
            s.op("dve", lambda: V.tensor_tensor(out=zp.t[:], in0=zp.t[:], in1=zc.t[:], op=ALU.add), [zp.b, zc.b], [zp.b])
            s.op("dve", lambda: V.tensor_tensor(out=cvb.t[:], in0=zp.t[:], in1=cbt.t[:], op=ALU.mult), [zp.b, cbt.b], [cvb.b])
            if i + 1 < nt:
                conv_loads(i + 1)

            def tr16(srcT, dstT):
                for hf in range(2):
                    p_ = ps[6 + hf]
                    pv = p_.t[:, :].bitcast(BF16)

                    def tr(hf=hf, pv=pv):
                        for kk in range(8):
                            k = hf * 8 + kk
                            ins = PE.transpose(out=pv[:, kk * 128:(kk + 1) * 128], in_=srcT.t[:, k * 128:(k + 1) * 128], identity=identb.t[:])
                        return ins
                    s.op("pe", tr, [srcT.b, identb.b], [p_.b])
                    s.op("act", lambda hf=hf, pv=pv: A.copy(out=dstT.t[:, hf * 8:(hf + 1) * 8, :], in_=pv.rearrange("p (k t) -> p k t", k=8)),
                         [p_.b], [dstT.b])
            adv(8)
            tr16(cvb, convT)
            for n in range(4):
                wa = wget()
                pa, pc = ps[4], ps[5]
                sa, sc_ = sgab[0], sgcb[0]
                s.dma("sp", [(sa.t[:], ns.sga_s.t[T0:T0 + 128, n * 512:(n + 1) * 512])], [ns.sga_s.b], [sa.b], sa.b)
                s.dma("sp", [(sc_.t[:], ns.sgc_s.t[T0:T0 + 128, n * 512:(n + 1) * 512])], [ns.sgc_s.b], [sc_.b], sc_.b)

                attnT = attnT2[i % 2]

                def mma():
                    for jj in range(4):
                        ins = PE.matmul(pa.t[:, :], lhsT=attnT.t[:, jj, :], rhs=wa.t[:, jj, :], start=(jj == 0), stop=(jj == 3))
                    return ins
                s.op("pe", mma, [attnT.b, wa.b], [pa.b])

                def mmc():
                    for k in range(16):
                        ins = PE.matmul(pc.t[:, :], lhsT=convT.t[:, k, :], rhs=wco_r.t[:, k, n * 512:(n + 1) * 512], start=(k == 0), stop=(k == 15))
                    return ins
                s.op("pe", mmc, [convT.b, wco_r.b], [pc.b])
                a1, a2 = m1[0], m2[0]
                s.op("dve", lambda: V.tensor_tensor(out=a1.t[:], in0=pa.t[:, :], in1=sa.t[:], op=ALU.mult), [pa.b, sa.b], [a1.b])
                s.op("dve", lambda: V.tensor_tensor(out=a2.t[:], in0=pc.t[:, :], in1=sc_.t[:], op=ALU.mult), [pc.b, sc_.b], [a2.b])
                s.op("pool", lambda: G.tensor_tensor(out=mixb.t[:, n * 512:(n + 1) * 512], in0=a1.t[:], in1=a2.t[:], op=ALU.add),
                     [a1.b, a2.b], [mixb.b])
                adv(3)
            s.dma("pool", [(ns.mix_s.t[T0:T0 + 128, :], mixb.t[:])], [mixb.b], [ns.mix_s.b], mixb.b)
            adv(100)
            agn = att_gen(i + 2) if i + 2 < nt else iter(())
            next(agn, None)
    s.barrier()


def phase_b2(nc, s, L):
    from types import SimpleNamespace
    ns = SimpleNamespace(**L)
    V, A, G, PE = nc.vector, nc.scalar, nc.gpsimd, nc.tensor
    sbt, pst, nt, debug = ns.sbt, ns.pst, ns.nt, ns.debug
    identf, identb = ns.identf, ns.identb
    with ExitStack() as st:
        g2b = sbt(st, "g2b", [128, D], F32)
        s.dma("sp", [(g2b.t[:], ns.norm2_g.to_broadcast([128, D]))], [], [g2b.b], g2b.b)
        ps = [pst(st, "psB2_%d" % i) for i in range(8)]
        hb = [sbt(st, "hb%d" % i, [128, D], F32) for i in range(2)]
        scb = [sbt(st, "scb%d" % i, [128, D], F32) for i in range(2)]
        mixl = [sbt(st, "mixl%d" % i, [128, D], BF16) for i in range(2)]
        xn2l = [sbt(st, "xn2l%d" % i, [128, D], BF16) for i in range(2)]
        mixT = sbt(st, "mixT2", [128, 16, 128], BF16)
        xn2T = sbt(st, "xn2T2", [128, 16, 128], BF16)
        xpb = [sbt(st, "xpb%d" % i, [128, 512], F32) for i in range(2)]
        st2l = [sbt(st, "st2_%d" % i, [128, 4], F32) for i in range(2)]
        wo_r = sbt(st, "wo_r", [128, 16, D], BF16)
        wq_r = sbt(st, "wq_r", [128, 16, D], BF16)
        s.dma("sp", [(wo_r.t[:, :, :], ns.w_o_b.t[:, :].rearrange("(k p) c -> p k c", p=128))], [ns.w_o_b.b], [wo_r.b], wo_r.b)
        s.dma("sp", [(wq_r.t[:, :, :], ns.w_q_b.t[:, :].rearrange("(k p) c -> p k c", p=128))], [ns.w_q_b.b], [wq_r.b], wq_r.b)
        zc = hb[0]
        skT = sbt(st, "skT", [128, 16, 128], BF16)
        pqT = sbt(st, "pqT", [128, 16, 128], BF16)
        skb = pqT
        s.dma("sp", [(zc.t[:, :].rearrange("p (a d) -> p a d", a=16), ns.subk.rearrange("a n d -> n a d"))], [], [zc.b], zc.b)
        s.op("dve", lambda: V.tensor_copy(out=skb.t[:, :, :], in_=zc.t[:, :].rearrange("p (a d) -> p a d", a=16)), [zc.b], [skb.b])
        for hf in range(2):
            p_ = ps[6 + hf]
            pv = p_.t[:, :].bitcast(BF16)

            def trk(hf=hf, pv=pv):
                for kk in range(8):
                    ins = PE.transpose(out=pv[:, kk * 128:(kk + 1) * 128], in_=skb.t[:, hf * 8 + kk, :], identity=identb.t[:])
                return ins
            s.op("pe", trk, [skb.b, identb.b], [p_.b])
            s.op("act", lambda hf=hf, pv=pv: A.copy(out=skT.t[:, hf * 8:(hf + 1) * 8, :], in_=pv.rearrange("p (k t) -> p k t", k=8)),
                 [p_.b], [skT.b])


        def tr16(srcT, dstT):
            for hf in range(2):
                p_ = ps[6 + hf]
                pv = p_.t[:, :].bitcast(BF16)

                def tr(hf=hf, pv=pv):
                    for kk in range(8):
                        k = hf * 8 + kk
                        ins = PE.transpose(out=pv[:, kk * 128:(kk + 1) * 128], in_=srcT.t[:, k * 128:(k + 1) * 128], identity=identb.t[:])
                    return ins
                s.op("pe", tr, [srcT.b, identb.b], [p_.b])
                s.op("act", lambda hf=hf, pv=pv: A.copy(out=dstT.t[:, hf * 8:(hf + 1) * 8, :], in_=pv.rearrange("p (k t) -> p k t", k=8)),
                     [p_.b], [dstT.b])

        NH1 = 10
        s16b = sbt(st, "s16b", [128, 16, 16], F32)
        i16b = sbt(st, "i16b", [128, 16, 16], U32)
        workb = sbt(st, "workb", [128, 128], F32)

        def l1_gen(i):
            sc = scb[i % 2]
            sc3 = sc.t[:, :].rearrange("p (a n) -> p a n", a=16)
            for hp in range(NH1):
                src_ap, vals, idxs, wk = sc3[:, hp, :], s16b.t[:, hp, :], i16b.t[:, hp, :], workb.t[:, :]
                s.op("dve", lambda: V.max(out=vals[:, 0:8], in_=src_ap), [sc.b], [s16b.b])
                yield
                s.op("dve", lambda: V.max_index(out=idxs[:, 0:8], in_max=vals[:, 0:8], in_values=src_ap), [sc.b, s16b.b], [i16b.b])
                yield
                s.op("dve", lambda: V.match_replace(out=wk, in_to_replace=vals[:, 0:8], in_values=src_ap, imm_value=-1e30), [sc.b, s16b.b], [workb.b])
                yield
                s.op("dve", lambda: V.max(out=vals[:, 8:16], in_=wk), [workb.b], [s16b.b])
                yield
                s.op("dve", lambda: V.max_index(out=idxs[:, 8:16], in_max=vals[:, 8:16], in_values=wk), [workb.b, s16b.b], [i16b.b])
                yield
            s.dma("pool", [(ns.s16_s.t[i, :, 0:NH1 * 16], s16b.t[:, 0:NH1, :].rearrange("p a k -> p (a k)"))], [s16b.b], [ns.s16_s.b], s16b.b)
            s.dma("pool", [(ns.i16_s.t[i, :, 0:NH1 * 16], i16b.t[:, 0:NH1, :].rearrange("p a k -> p (a k)"))], [i16b.b], [ns.i16_s.b], i16b.b)

        tstate = {"g": iter(())}

        def adv(k):
            for _ in range(k):
                next(tstate["g"], None)

        for i in range(nt):
            T0 = i * 128
            mixb = mixl[i % 2]
            xn2b = xn2l[i % 2]
            zn = hb[i % 2]
            cbt = scb[i % 2]
            st2 = st2l[i % 2]
            s.dma("sp", [(mixb.t[:], ns.mix_s.t[T0:T0 + 128, :])], [ns.mix_s.b], [mixb.b], mixb.b)
            tr16(mixb, mixT)
            adv(10)
            h_ = zn
            for n in range(4):
                wo = wo_r
                po = ps[4 + n % 2]
                xp = xpb[n % 2]
                s.dma("sp", [(xp.t[:], ns.xs[T0:T0 + 128, n * 512:(n + 1) * 512])], [], [xp.b], xp.b)

                def mmo():
                    for k in range(16):
                        ins = PE.matmul(po.t[:, :], lhsT=mixT.t[:, k, :], rhs=wo.t[:, k, n * 512:(n + 1) * 512], start=(k == 0), stop=(k == 15))
                    return ins
                s.op("pe", mmo, [mixT.b, wo.b], [po.b])
                s.op("dve", lambda: V.tensor_tensor(out=h_.t[:, n * 512:(n + 1) * 512], in0=po.t[:, :], in1=xp.t[:], op=ALU.add),
                     [po.b, xp.b], [h_.b])
                adv(6)
            s.dma("pool", [(ns.h_s.t[T0:T0 + 128, :], h_.t[:])], [h_.b], [ns.h_s.b], h_.b)
            s.op("act", lambda: A.activation(out=xn2b.t[:], in_=h_.t[:], func=AF.Square, accum_out=st2.t[:, 0:1]), [h_.b], [xn2b.b, st2.b])
            s.op("act", lambda: A.activation(out=st2.t[:, 1:2], in_=st2.t[:, 0:1], func=AF.Sqrt, scale=1.0 / D, bias=EPS), [st2.b], [st2.b])
            s.op("dve", lambda: V.reciprocal(out=st2.t[:, 2:3], in_=st2.t[:, 1:2]), [st2.b], [st2.b])
            s.op("dve", lambda: V.scalar_tensor_tensor(out=xn2b.t[:], in0=h_.t[:], scalar=st2.t[:, 2:3], in1=g2b.t[:], op0=ALU.mult, op1=ALU.mult),
                 [h_.b, st2.b, g2b.b], [xn2b.b])
            s.dma("pool", [(ns.xn2_s.t[T0:T0 + 128, :], xn2b.t[:])], [xn2b.b], [ns.xn2_s.b], xn2b.b)
            tr16(xn2b, xn2T)
            for n in range(4):
                wq_ = wq_r
                pq = ps[4 + n % 2]

                def mmq():
                    for c in range(4):
                        for k in range(16):
                            ins = PE.matmul(pq.t[:, c * 128:(c + 1) * 128], lhsT=wq_.t[:, k, n * 512 + c * 128:n * 512 + (c + 1) * 128], rhs=xn2T.t[:, k, :],
                                            start=(k == 0), stop=(k == 15))
                    return ins
                s.op("pe", mmq, [wq_.b, xn2T.b], [pq.b])
                s.op("act", lambda: A.copy(out=pqT.t[:, 4 * n:4 * n + 4, :], in_=pq.t[:, :].rearrange("p (c t) -> p c t", c=4)), [pq.b], [pqT.b])
                adv(4)
            sc = cbt
            for b in range(4):
                pb = ps[b]

                def mms():
                    for c in range(4):
                        hp = 4 * b + c
                        ins = PE.matmul(pb.t[:, c * 128:(c + 1) * 128], lhsT=pqT.t[:, hp, :], rhs=skT.t[:, hp, :], start=True, stop=True)
                    return ins
                s.op("pe", mms, [pqT.b, skT.b], [pb.b])
                s.op("act", lambda: A.copy(out=sc.t[:, b * 512:(b + 1) * 512], in_=pb.t[:, :]), [pb.b], [sc.b])
            if debug:
                s.dma("pool", [(ns.dbg_sc.t[T0:T0 + 128, :], sc.t[:])], [sc.b], [ns.dbg_sc.b], sc.b)
            s.dma("pool", [(ns.sc_s.t[T0:T0 + 128, :], sc.t[:])], [sc.b], [ns.sc_s.b], sc.b)
            adv(1000)
            tstate["g"] = l1_gen(i)

        adv(1000)
    s.barrier()


def phase_c(nc, s, L):
    from types import SimpleNamespace
    ns = SimpleNamespace(**L)
    V, A, G, PE = nc.vector, nc.scalar, nc.gpsimd, nc.tensor
    sbt, nt = ns.sbt, ns.nt
    identb = ns.identb
    F32R = mybir.dt.float32r
    RING = 8
    with ExitStack() as st:
        csel = sbt(st, "csel", [128, 255], F32)
        s.op("dve", lambda: V.memset(csel.t[:], 0.0), [], [csel.b])
        s.op("dve", lambda: V.memset(csel.t[:, 127:128], 1.0), [csel.b], [csel.b])
        ht = [sbt(st, "ht%d" % i, [128, D], F32) for i in range(2)]
        xt = [sbt(st, "xn2t%d" % i, [128, D], BF16) for i in range(2)]
        it = [sbt(st, "idxt%d" % i, [128, 128], U32) for i in range(2)]
        gt = [sbt(st, "gtt%d" % i, [128, 128], F32) for i in range(2)]
        U = [sbt(st, "U%d" % i, [128, D], BF16) for i in range(RING)]
        Vv = [sbt(st, "Vv%d" % i, [128, D], BF16) for i in range(RING)]
        junk = sbt(st, "junkC", [128, 1024], BF16)
        hacc = sbt(st, "hacc", [128, 128, 2], F32, nb=128)
        gl = [sbt(st, "gl%d" % i, [128, 2], F32) for i in range(4)]
        Z = [sbt(st, "Z%d" % i, [128, 128], BF16) for i in range(4)]
        yt = sbt(st, "yt", [128, D], F32)
        bc_t = st.enter_context(nc.psum_tensor("bcC", [128, 2048], F32))
        bc = [T(bc_t, s.buf("bcC%d" % i)) for i in range(2)]
        out_t = st.enter_context(nc.psum_tensor("outC", [128, 2048], F32))
        outp = T(out_t, s.buf("outC"))

        identf = ns.identf
        debug = ns.debug
        sct = [sbt(st, "sct%d" % i, [128, D], F32) for i in range(2)]
        cand = sbt(st, "cand", [128, D], F32)
        oh = sbt(st, "oh", [128, D], F32)
        iota16 = sbt(st, "iota16c", [128, 16], F32)
        s.op("pool", lambda: G.iota(iota16.t[:], pattern=[[1, 16]], base=0, channel_multiplier=0,
                                    allow_small_or_imprecise_dtypes=True), [], [iota16.b])
        s16 = sbt(st, "s16", [128, 16, 16], F32)
        i16 = sbt(st, "i16", [128, 16, 16], U32)
        i16f = sbt(st, "i16f", [128, 16, 16], F32)
        work = sbt(st, "work", [128, 256], F32)
        best = sbt(st, "best", [128, 8, 16], F32)
        flat = sbt(st, "flat", [128, 8, 16], U32)
        au = sbt(st, "au", [128, 128], U32)
        bu = sbt(st, "bu", [128, 128], U32)
        af = sbt(st, "af", [128, 128], F32)
        bf_ = sbt(st, "bf", [128, 128], F32)
        e1 = sbt(st, "e1", [128, 128], F32)
        e2 = sbt(st, "e2", [128, 128], F32)
        ef = sbt(st, "ef", [128, 128], F32)
        gat = sbt(st, "gat", [128, 128], F32)
        gsum = sbt(st, "gsum", [128, 16], F32)

        def topk_gen(i):
            T0 = i * 128
            sc = sct[i % 2]
            idxo, gto = it[i % 2], gt[i % 2]
            zp = cand
            s.dma("sp", [(sc.t[:], ns.sc_s.t[T0:T0 + 128, :])], [ns.sc_s.b], [sc.b], sc.b)
            s.dma("sp", [(s16.t[:, 0:10, :].rearrange("p a k -> p (a k)"), ns.s16_s.t[i, :, 0:160])], [ns.s16_s.b], [s16.b], s16.b)
            s.dma("sp", [(i16.t[:, 0:10, :].rearrange("p a k -> p (a k)"), ns.i16_s.t[i, :, 0:160])], [ns.i16_s.b], [i16.b], i16.b)
            yield
            sc3 = sc.t[:, :].rearrange("p (a n) -> p a n", a=16)

            def top16(src_ap, vals, idxs, wk):
                s.op("dve", lambda: V.max(out=vals[:, 0:8], in_=src_ap), [sc.b, zp.b], [s16.b, best.b])
                yield
                s.op("dve", lambda: V.max_index(out=idxs[:, 0:8], in_max=vals[:, 0:8], in_values=src_ap), [sc.b, zp.b, s16.b, best.b], [i16.b, flat.b])
                yield
                s.op("dve", lambda: V.match_replace(out=wk, in_to_replace=vals[:, 0:8], in_values=src_ap, imm_value=-1e30),
                     [sc.b, zp.b, s16.b, best.b], [work.b])
                yield
                s.op("dve", lambda: V.max(out=vals[:, 8:16], in_=wk), [work.b], [s16.b, best.b])
                yield
                s.op("dve", lambda: V.max_index(out=idxs[:, 8:16], in_max=vals[:, 8:16], in_values=wk), [work.b, s16.b, best.b], [i16.b, flat.b])
                yield
            for hp in range(10, 16):
                yield from top16(sc3[:, hp, :], s16.t[:, hp, :], i16.t[:, hp, :], work.t[:, 0:128])
            s.op("dve", lambda: V.tensor_copy(out=i16f.t[:, :, :], in_=i16.t[:, :, :]), [i16.b], [i16f.b])
            yield
            s4 = s16.t[:, :, :].rearrange("p (h two) k -> p h two k", two=2)
            i4 = i16f.t[:, :, :].rearrange("p (h two) k -> p h two k", two=2)
            cand4 = cand.t[:, :].rearrange("p (h a b) -> p h a b", h=8, a=16)
            s.op("dve", lambda: V.tensor_tensor(out=cand4, in0=s4[:, :, 0, :].unsqueeze(3).to_broadcast([128, 8, 16, 16]),
                                                in1=s4[:, :, 1, :].unsqueeze(2).to_broadcast([128, 8, 16, 16]), op=ALU.add), [s16.b], [cand.b])
            yield
            cand3 = cand.t[:, :].rearrange("p (h n) -> p h n", h=8)
            for h in range(8):
                yield from top16(cand3[:, h, :], best.t[:, h, :], flat.t[:, h, :], work.t[:, 0:256])
            flat2 = flat.t[:, :, :].rearrange("p h k -> p (h k)")
            s.op("dve", lambda: V.tensor_scalar(out=au.t[:], in0=flat2, scalar1=4, scalar2=None, op0=ALU.logical_shift_right), [flat.b], [au.b])
            yield
            s.op("dve", lambda: V.tensor_scalar(out=bu.t[:], in0=flat2, scalar1=15, scalar2=None, op0=ALU.bitwise_and), [flat.b], [bu.b])
            yield
            s.op("dve", lambda: V.tensor_copy(out=af.t[:], in_=au.t[:]), [au.b], [af.b])
            yield
            s.op("dve", lambda: V.tensor_copy(out=bf_.t[:], in_=bu.t[:]), [bu.b], [bf_.b])
            yield
            oh4 = oh.t[:, :].rearrange("p (h k j) -> p h k j", h=8, k=16)
            io4 = iota16.t[:, :].unsqueeze(1).unsqueeze(1).to_broadcast([128, 8, 16, 16])
            for (xf_, half, eo) in ((af, 0, e1), (bf_, 1, e2)):
                x4 = xf_.t[:, :].rearrange("p (h k) -> p h k", h=8).unsqueeze(3).to_broadcast([128, 8, 16, 16])
                s.op("dve", lambda: V.tensor_tensor(out=oh4, in0=x4, in1=io4, op=ALU.is_equal), [xf_.b, iota16.b], [oh.b])
                yield
                s.op("dve", lambda: V.tensor_tensor(out=oh4, in0=oh4, in1=i4[:, :, half, :].unsqueeze(2).to_broadcast([128, 8, 16, 16]), op=ALU.mult),
                     [oh.b, i16f.b], [oh.b])
                yield
                s.op("dve", lambda: V.tensor_reduce(out=eo.t[:, :].rearrange("p (h k) -> p h k", h=8), in_=oh4, axis=AX.X, op=ALU.add), [oh.b], [eo.b])
                yield
            s.op("dve", lambda: V.scalar_tensor_tensor(out=ef.t[:], in0=e1.t[:], scalar=128.0, in1=e2.t[:], op0=ALU.mult, op1=ALU.add),
                 [e1.b, e2.b], [ef.b])
            yield
            g3 = gat.t[:, :].rearrange("p (h k) -> p h k", h=8)
            s.op("dve", lambda: V.tensor_tensor(out=g3, in0=best.t[:, :, :], in1=best.t[:, :, 0:1].to_broadcast([128, 8, 16]), op=ALU.subtract),
                 [best.b], [gat.b])
            yield
            s.op("act", lambda: A.activation(out=gat.t[:], in_=gat.t[:], func=AF.Exp), [gat.b], [gat.b])
            yield
            s.op("dve", lambda: V.tensor_reduce(out=gsum.t[:, 0:8], in_=g3, axis=AX.X, op=ALU.add), [gat.b], [gsum.b])
            yield
            s.op("dve", lambda: V.reciprocal(out=gsum.t[:, 8:16], in_=gsum.t[:, 0:8]), [gsum.b], [gsum.b])
            yield
            s.op("dve", lambda: V.tensor_tensor(out=g3, in0=g3, in1=gsum.t[:, 8:16].unsqueeze(2).to_broadcast([128, 8, 16]), op=ALU.mult),
                 [gat.b, gsum.b], [gat.b])
            yield
            s.op("pe", lambda: PE.transpose(out=bc_t[:, 0:128], in_=ef.t[:, :], identity=identf.t[:]), [ef.b, identf.b], [bc[0].b])
            s.op("dve", lambda: V.tensor_copy(out=idxo.t[:], in_=bc_t[:, 0:128]), [bc[0].b], [idxo.b])
            yield
            s.op("pe", lambda: PE.transpose(out=bc_t[:, 1024:1152], in_=gat.t[:, :], identity=identf.t[:]), [gat.b, identf.b], [bc[1].b])
            s.op("dve", lambda: V.tensor_copy(out=gto.t[:], in_=bc_t[:, 1024:1152]), [bc[1].b], [gto.b])
            yield
            if debug:
                s.dma("sp", [(ns.idxT_s.t[i, :, :], idxo.t[:])], [idxo.b], [ns.idxT_s.b], idxo.b)
                s.dma("sp", [(ns.gT_s.t[i, :, :], gto.t[:])], [gto.b], [ns.gT_s.b], gto.b)

        for _ in topk_gen(0):
            pass
        tg = 0
        for i in range(nt):
            T0 = i * 128
            h_, x_, i_, g_ = ht[i % 2], xt[i % 2], it[i % 2], gt[i % 2]
            tgen = topk_gen(i + 1) if i + 1 < nt else iter(())
            s.dma("sp", [(x_.t[:], ns.xn2_s.t[T0:T0 + 128, :])], [ns.xn2_s.b], [x_.b], x_.b)
            s.dma("sp", [(h_.t[:], ns.h_s.t[T0:T0 + 128, :])], [ns.h_s.b], [h_.b], h_.b)

            def matvec(t, r, zz):
                def mv():
                    for p in range(4):
                        ins = PE.matmul(outp.t[:, p * 512:(p + 1) * 512], lhsT=zz.t[:, :],
                                        rhs=Vv[r].t[:, p * 512:(p + 1) * 512], start=(t == 0), stop=(t == 127))
                    return ins
                s.op("pe", mv, [zz.b, Vv[r].b], [outp.b])
            LAG = 2
            pend = []
            for t in range(128):
                r = tg % RING
                z_ = Z[tg % 4]
                gl_ = gl[tg % 4]
                s.gather(U[r].t[:], ns.pu_b.t[:, :], i_.t[:, t:t + 1], [i_.b, ns.pu_b.b], [U[r].b], U[r].b)
                s.gather(Vv[r].t[:], ns.pv_b.t[:, :], i_.t[:, t:t + 1], [i_.b, ns.pv_b.b], [Vv[r].b], Vv[r].b)
                for hf in range(2):
                    b_ = bc[hf]

                    def bcm(hf=hf):
                        for p in range(2):
                            c0 = hf * 1024 + p * 512
                            ins = PE.matmul(bc_t[:, c0:c0 + 512], lhsT=identb.t[:, t:t + 1].to_broadcast([128, 128]),
                                            rhs=x_.t[:, c0:c0 + 512], start=True, stop=True)
                        return ins
                    s.op("pe", bcm, [x_.b, identb.b], [b_.b])
                    s.op("dve", lambda hf=hf: V.scalar_tensor_tensor(out=junk.t[:, :], in0=U[r].t[:, hf * 1024:(hf + 1) * 1024], scalar=1.0,
                                                                     in1=bc_t[:, hf * 1024:(hf + 1) * 1024], op0=ALU.mult, op1=ALU.mult,
                                                                     accum_out=hacc.t[:, t, hf:hf + 1]),
                         [U[r].b, b_.b], [junk.b, hacc.bs[t]])
                s.op("act", lambda: A.activation(out=gl_.t[:, 0:1], in_=hacc.t[:, t, 0:1], func=AF.Gelu, bias=hacc.t[:, t, 1:2]),
                     [hacc.bs[t]], [gl_.b])
                s.op("act", lambda: A.activation(out=gl_.t[:, 1:2], in_=gl_.t[:, 0:1], func=AF.Copy, scale=g_.t[:, t:t + 1]),
                     [gl_.b, g_.b], [gl_.b])
                s.op("act", lambda: A.activation(out=z_.t[:, :], in_=csel.t[:, 127 - t:255 - t], func=AF.Copy, scale=gl_.t[:, 1:2]),
                     [gl_.b, csel.b], [z_.b])
                pend.append((t, r, z_))
                if len(pend) > LAG:
                    matvec(*pend.pop(0))
                tg += 1
                next(tgen, None)
                next(tgen, None)
            while pend:
                matvec(*pend.pop(0))
            for _ in tgen:
                pass
            for p in range(4):
                s.op("dve", lambda: V.tensor_tensor(out=yt.t[:, p * 512:(p + 1) * 512], in0=outp.t[:, p * 512:(p + 1) * 512],
                                                    in1=h_.t[:, p * 512:(p + 1) * 512], op=ALU.add), [outp.b, h_.b], [yt.b])
            s.dma("sp", [(ns.ys[T0:T0 + 128, :], yt.t[:])], [yt.b], [ns.b_ys], yt.b)
    s.barrier()


def _masks():
    i = np.arange(128)[:, None]
    j = np.arange(128)[None, :]
    out = []
    for (r, deltas) in ((1, (-1, 0, 1)), (4, (-2, 0, 2)), (16, (-8, 0, 8))):
        for dl in deltas:
            rel = dl * 128 + i - j
            ok = (np.abs(rel) <= 64 * r) & (rel % r == 0)
            out.append(np.where(ok, 0.0, NEGB))
    return np.concatenate(out, axis=1).astype(np.float32)


def _rope_tables(npos):
    half = 64
    inv = (10000.0 ** (-np.arange(half, dtype=np.float32) * 2.0 / 128)).astype(np.float32)
    ang = np.arange(npos, dtype=np.float32)[:, None] * inv[None, :]
    c = np.cos(ang).astype(np.float32)
    sn = np.sin(ang).astype(np.float32)
    return np.concatenate([c, c], 1), np.concatenate([-sn, sn], 1)


def core_stream(x_prompt, x_sample, c):
    if c < 4:
        return [x_sample[c], x_prompt[c]]
    return [x_prompt[4 + 3 * (c - 4) + j] for j in range(3)]


def prep_core(inputs, c):
    seqs = core_stream(inputs["x_prompt"], inputs["x_sample"], c)
    cc4, ss4 = _rope_tables(4096)
    m = {"xs": np.ascontiguousarray(np.concatenate(seqs, 0)),
         "ccd": np.ascontiguousarray(np.concatenate([cc4[:len(q)] for q in seqs], 0)),
         "ssd": np.ascontiguousarray(np.concatenate([ss4[:len(q)] for q in seqs], 0)),
         "f01": np.full((128, 1), 1.0 if c < 4 else 0.0, np.float32),
         "maskb": _masks()}
    for k in ("norm1_g", "w_in", "q_norm_g", "k_norm_g", "w_attn_out", "conv_w", "w_conv_out", "w_o",
              "norm2_g", "peer_w_q", "peer_u", "peer_v"):
        m[k] = np.ascontiguousarray(inputs[k][0])
    m["peer_sub_keys"] = np.ascontiguousarray(inputs["peer_sub_keys"][0].reshape(16, 128, 128))
    return m


def kernel(**inputs):
    inputs = {k: np.asarray(v) for k, v in inputs.items()}
    nc, _ = build()
    in_maps = [prep_core(inputs, c) for c in range(8)]
    res = run_bass_kernel_spmd(nc, in_maps, core_ids=list(range(8)))
    yp = np.empty((16, 2048, D), np.float32)
    ysm = np.empty((4, 4096, D), np.float32)
    for c in range(8):
        y = res.results[c]["ys"]
        if c < 4:
            ysm[c] = y[0:4096]
            yp[c] = y[4096:6144]
        else:
            for j in range(3):
                yp[4 + 3 * (c - 4) + j] = y[2048 * j:2048 * (j + 1)]
    return (yp, ysm)
```

```python
import numpy as np
import ml_dtypes
from contextlib import ExitStack
import concourse.bass as bass
import concourse.mybir as mybir
from concourse.alu_op_type import AluOpType as ALU
from concourse.bass_utils import run_bass_kernel_spmd

F32 = mybir.dt.float32
BF16 = mybir.dt.bfloat16
U32 = mybir.dt.uint32
AF = mybir.ActivationFunctionType
AX = mybir.AxisListType

D = 2048
INC = 14848
NTOK = 6144
NT = NTOK // 128
TPS = 16
NEXP = 16384
EPS = 1e-6
NEGB = -30000.0


class Buf:
    __slots__ = ("name", "lw", "rd", "sem", "dcnt")

    def __init__(self, name):
        self.name = name
        self.lw = None
        self.rd = {}
        self.sem = None
        self.dcnt = 0


class Sched:
    def __init__(self, nc):
        self.nc = nc
        self.eng = {"pe": nc.tensor, "dve": nc.vector, "act": nc.scalar,
                    "pool": nc.gpsimd, "sp": nc.sync}
        self.sems = {}
        self.cnt = {}
        for e in ("pe", "dve", "act", "pool"):
            self.sems[e] = nc.semaphore("tl_" + e).__enter__()
            self.cnt[e] = 0
        self.waited = {e: {} for e in self.eng}
        self.ndsem = 0
        self.nins = 0
        self.allbufs = []

    def buf(self, name):
        b = Buf(name)
        self.allbufs.append(b)
        return b

    def _deps(self, reads, writes):
        d = {}
        for b in reads:
            if b.lw is not None:
                k, v = b.lw
                if d.get(k, 0) < v:
                    d[k] = v
        for b in writes:
            if b.lw is not None:
                k, v = b.lw
                if d.get(k, 0) < v:
                    d[k] = v
            for k, v in b.rd.items():
                if d.get(k, 0) < v:
                    d[k] = v
        return d

    def _wait(self, e, deps):
        w = self.waited[e]
        for k, v in deps.items():
            if w.get(k, 0) < v:
                self.eng[e].wait_ge(self.sems[k], v)
                w[k] = v
                self.nins += 1

    def _mark(self, tok, reads, writes):
        k, v = tok
        for b in reads:
            if b.rd.get(k, 0) < v:
                b.rd[k] = v
        for b in writes:
            b.lw = tok
            b.rd = {}

    def op(self, e, fn, reads=(), writes=()):
        d = self._deps(reads, writes)
        if e == "pe":
            d.pop("pe", None)
        self._wait(e, d)
        ins = fn()
        self.cnt[e] += 1
        ins.then_inc(self.sems[e], 1)
        self.nins += 1
        self._mark((e, self.cnt[e]), reads, writes)

    def _dsem(self, b):
        if b.sem is None:
            key = "d%d" % self.ndsem
            self.ndsem += 1
            self.sems[key] = self.nc.semaphore(key).__enter__()
            b.sem = key
        return b.sem

    def dma(self, q, pairs, reads, writes, syncbuf, **kw):
        self._wait(q, self._deps(reads, writes))
        key = self._dsem(syncbuf)
        for (o, i) in pairs:
            ins = self.eng[q].dma_start(out=o, in_=i, **kw)
            syncbuf.dcnt += 16
            ins.then_inc(self.sems[key], 16)
            self.nins += 1
        self._mark((key, syncbuf.dcnt), reads, writes)

    def gather(self, out, in_, off_ap, reads, writes, syncbuf):
        q = "pool"
        self._wait(q, self._deps(reads, writes))
        key = self._dsem(syncbuf)
        ins = self.nc.gpsimd.indirect_dma_start(
            out=out, out_offset=None, in_=in_,
            in_offset=bass.IndirectOffsetOnAxis(ap=off_ap, axis=0))
        syncbuf.dcnt += 16
        ins.then_inc(self.sems[key], 16)
        self.nins += 1
        self._mark((key, syncbuf.dcnt), reads, writes)

    def barrier(self):
        allv = {}
        for e in ("pe", "dve", "act", "pool"):
            if self.cnt[e]:
                allv[e] = self.cnt[e]
        for b in self.allbufs:
            if b.sem is not None and b.dcnt:
                allv[b.sem] = b.dcnt
        for e in self.eng:
            self._wait(e, allv)


class T:
    def __init__(self, t, b, bs=None):
        self.t = t
        self.b = b
        self.bs = bs


def build(nt=NT, phases="0ABC", debug=False):
    nc = bass.Bass("TRN2", target_bir_lowering=False)
    s = Sched(nc)
    ntok = nt * 128
    kind_dbg = "ExternalOutput" if debug else "Internal"

    def din(name, shape, dt=F32):
        return nc.dram_tensor(name, shape, dt, kind="ExternalInput").ap()

    def dscr(name, shape, dt, dbg=False):
        return T(nc.dram_tensor(name, shape, dt, kind=(kind_dbg if dbg else "Internal")).ap(), s.buf(name))

    xs = din("xs", [NTOK, D])
    ccd = din("ccd", [NTOK, 128])
    ssd = din("ssd", [NTOK, 128])
    f01d = din("f01", [128, 1])
    maskd = din("maskb", [128, 9 * 128])
    norm1_g = din("norm1_g", [1, D])
    w_in = din("w_in", [D, INC])
    q_norm_g = din("q_norm_g", [1, 128])
    k_norm_g = din("k_norm_g", [1, 128])
    w_ao = din("w_attn_out", [512, D])
    conv_w = din("conv_w", [3, D])
    w_co = din("w_conv_out", [D, D])
    w_o = din("w_o", [D, D])
    norm2_g = din("norm2_g", [1, D])
    w_q = din("peer_w_q", [D, D])
    subk = din("peer_sub_keys", [16, 128, 128])
    peer_u = din("peer_u", [NEXP, D])
    peer_v = din("peer_v", [NEXP, D])
    ys = nc.dram_tensor("ys", [NTOK, D], F32, kind="ExternalOutput").ap()
    b_ys = s.buf("ys")

    w_in_b = dscr("w_in_b", [D, INC], BF16)
    wgrp = [s.buf("w_in_grp%d" % i) for i in range(4)]

    def WGROUP(c):
        return 0 if c < 4 else (1 if c < 8 else (2 if c < 16 else 3))
    w_ao_b = dscr("w_ao_b", [512, D], BF16)
    w_co_b = dscr("w_co_b", [D, D], BF16)
    w_o_b = dscr("w_o_b", [D, D], BF16)
    w_q_b = dscr("w_q_b", [D, D], BF16)
    pu_b = dscr("pu_b", [NEXP, D], BF16)
    pv_b = dscr("pv_b", [NEXP, D], BF16)
    qT_s = dscr("qT_s", [12, 128, NTOK], BF16, True)
    kT_s = dscr("kT_s", [12, 128, NTOK], BF16, True)
    v_s = dscr("v_s", [NTOK, 12 * 129], BF16, True)
    cb_s = dscr("cb_s", [NTOK, D], F32, True)
    z_s = dscr("z_s", [NTOK + 2, D], F32, True)
    sga_s = dscr("sga_s", [NTOK, D], F32, True)
    sgc_s = dscr("sgc_s", [NTOK, D], F32, True)
    h_s = dscr("h_s", [NTOK, D], F32, True)
    xn2_s = dscr("xn2_s", [NTOK, D], BF16, True)
    idxT_s = dscr("idxT_s", [NT, 128, 128], U32, True)
    gT_s = dscr("gT_s", [NT, 128, 128], F32, True)
    sc_s = dscr("sc_s", [NTOK, 2048], F32)
    mix_s = dscr("mix_s", [NTOK, D], BF16)
    dbg_attn = dscr("dbg_attn", [NTOK, 512], F32, True) if debug else None
    dbg_sc = dscr("dbg_sc", [NTOK, 2048], F32, True) if debug else None

    def sbt(st, name, shape, dt, nb=0):
        t = st.enter_context(nc.sbuf_tensor(name, shape, dt))
        bs = [s.buf(name + "_%d" % i) for i in range(nb)] if nb else None
        return T(t, s.buf(name), bs)

    def pst(st, name):
        t = st.enter_context(nc.psum_tensor(name, [128, 512], F32))
        return T(t, s.buf(name))

    V = nc.vector
    A = nc.scalar
    G = nc.gpsimd
    PE = nc.tensor

    with ExitStack() as gst:
        identf = sbt(gst, "identf", [128, 128], F32)
        identb = sbt(gst, "identb", [128, 128], BF16)
        s.op("pool", lambda: G.memset(identf.t[:], 0.0), [], [identf.b])
        s.op("pool", lambda: G.affine_select(out=identf.t[:], in_=identf.t[:], pattern=[[-1, 128]],
                                             compare_op=ALU.not_equal, fill=1.0, base=0,
                                             channel_multiplier=1), [identf.b], [identf.b])
        s.op("dve", lambda: V.tensor_copy(out=identb.t[:], in_=identf.t[:]), [identf.b], [identb.b])

        if "0" in phases:
            def cast(dst, src, rows, cols, chunk):
                vd = dst.t.rearrange("r (a b) -> (r a) b", b=chunk)
                vs = src.rearrange("r (a b) -> (r a) b", b=chunk)
                n = rows * cols // chunk
                per = 4096
                for r0 in range(0, n, per):
                    r1 = min(n, r0 + per)
                    s.dma("pool", [(vd[r0:r1, :], vs[r0:r1, :])], [], [dst.b], dst.b)
            for cj in range(8):
                gbuf = wgrp[WGROUP(cj * 4)]
                c0, c1 = cj * 2048, min(INC, (cj + 1) * 2048)
                s.dma("pool", [(w_in_b.t[:, c0:c1], w_in[:, c0:c1])], [], [gbuf], gbuf)
            cast(w_ao_b, w_ao, 512, D, 2048)
            cast(w_co_b, w_co, D, D, 2048)
            cast(w_o_b, w_o, D, D, 2048)
            cast(w_q_b, w_q, D, D, 2048)

        if "A" in phases:
            with ExitStack() as st:
                g1b = sbt(st, "g1b", [128, D], F32)
                gq = sbt(st, "gq", [128, 256], F32)
                gk = sbt(st, "gk", [128, 256], F32)
                s.dma("sp", [(g1b.t[:], norm1_g.to_broadcast([128, D]))], [], [g1b.b], g1b.b)
                s.dma("sp", [(gq.t[:, 0:128], q_norm_g.to_broadcast([128, 128])),
                             (gq.t[:, 128:192], q_norm_g[:, 64:128].to_broadcast([128, 64])),
                             (gq.t[:, 192:256], q_norm_g[:, 0:64].to_broadcast([128, 64]))], [], [gq.b], gq.b)
                s.dma("sp", [(gk.t[:, 0:128], k_norm_g.to_broadcast([128, 128])),
                             (gk.t[:, 128:192], k_norm_g[:, 64:128].to_broadcast([128, 64])),
                             (gk.t[:, 192:256], k_norm_g[:, 0:64].to_broadcast([128, 64]))], [], [gk.b], gk.b)

                xt = [sbt(st, "xt%d" % i, [128, D], F32) for i in range(2)]
                rope = [sbt(st, "rope%d" % i, [128, 256], F32) for i in range(2)]
                xnb = [sbt(st, "xnb%d" % i, [128, D], BF16) for i in range(2)]
                junk = sbt(st, "junkA", [128, 512], BF16)
                stat = [sbt(st, "stat%d" % i, [128, 4], F32) for i in range(2)]
                xnT = [sbt(st, "xnT%d" % i, [128, 16, 512], BF16, nb=4) for i in range(2)]
                tabs = [sbt(st, "tabs%d" % i, [128, 4, 4, 128], F32, nb=4) for i in range(1)]
                wblk = [sbt(st, "wblk%d" % i, [128, 16, 512], BF16) for i in range(2)]
                ps = [pst(st, "psA%d" % i) for i in range(8)]
                qs = [sbt(st, "qs%d" % i, [128, 512], F32) for i in range(2)]
                qst = [sbt(st, "qst%d" % i, [128, 8], F32) for i in range(2)]
                tA = [sbt(st, "tA%d" % i, [128, 512], F32) for i in range(2)]
                tB = [sbt(st, "tB%d" % i, [128, 512], F32) for i in range(1)]
                ob = [sbt(st, "ob%d" % i, [128, 512], BF16) for i in range(2)]
                qTst = [sbt(st, "qTst%d" % i, [128, 12, 512], BF16) for i in range(1)]
                kTst = [sbt(st, "kTst%d" % i, [128, 12, 512], BF16) for i in range(1)]
                vst = [sbt(st, "vst%d" % i, [128, 4, 12 * 129], BF16) for i in range(1)]
                fst = [sbt(st, "fst%d" % i, [128, 4, 512], F32, nb=4) for i in range(2)]
                ccst = sbt(st, "ccst", [128, 4, D], F32, nb=16)
                s.op("dve", lambda: V.memset(vst[0].t[:], 1.0), [], [vst[0].b])
                s.op("dve", lambda: V.memset(fst[0].t[0:1, :, :], 0.0), [], fst[0].bs)
                s.dma("sp", [(z_s.t[0:1, :], fst[0].t[0:1, :, :]), (z_s.t[NTOK + 1:NTOK + 2, :], fst[0].t[0:1, :, :])],
                      fst[0].bs, [z_s.b], fst[0].b)

                nmac = nt // 4
                uvq = [(dd, ss_, r0) for (dd, ss_) in ((pu_b, peer_u), (pv_b, peer_v)) for r0 in range(0, NEXP, 4096)]

                def wload(gb):
                    wb = wblk[gb % 2]
                    c = gb % 29
                    s.dma("sp", [(wb.t[:], w_in_b.t[:, c * 512:(c + 1) * 512].rearrange("(k p) c -> p k c", p=128))],
                          [wgrp[WGROUP(c)]], [wb.b], wb.b)
                psi = 0
                pti = 0
                fsi = 0
                qi = 0
                for m in range(nmac):
                    xT = xnT[m % 2]
                    tb = tabs[0]
                    if uvq and m >= 1:
                        (dstT_, src_, r0) = uvq.pop(0)
                        s.dma("pool", [(dstT_.t[r0:r0 + 4096, :], src_[r0:r0 + 4096, :])], [xt[(m * 4) % 2].b], [dstT_.b], dstT_.b)
                    for sti in range(4):
                        i = m * 4 + sti
                        T0 = i * 128
                        x_ = xt[i % 2]
                        r_ = rope[i % 2]
                        xb_ = xnb[i % 2]
                        st_ = stat[i % 2]
                        s.dma("sp", [(x_.t[:], xs[T0:T0 + 128, :])], [], [x_.b], x_.b)
                        s.dma("sp", [(r_.t[:, 0:128], ccd[T0:T0 + 128, :]), (r_.t[:, 128:256], ssd[T0:T0 + 128, :])],
                              [], [r_.b], r_.b)
                        tbv = tb.t
                        s.op("dve", lambda: V.tensor_tensor(out=tbv[:, sti, 0, :], in0=r_.t[:, 0:128], in1=gq.t[:, 0:128], op=ALU.mult),
                             [r_.b, gq.b], [tb.bs[sti]])
                        s.op("dve", lambda: V.tensor_tensor(out=tbv[:, sti, 1, :], in0=r_.t[:, 128:256], in1=gq.t[:, 128:256], op=ALU.mult),
                             [r_.b, gq.b], [tb.bs[sti]])
                        s.op("dve", lambda: V.tensor_tensor(out=tbv[:, sti, 2, :], in0=r_.t[:, 0:128], in1=gk.t[:, 0:128], op=ALU.mult),
                             [r_.b, gk.b], [tb.bs[sti]])
                        s.op("dve", lambda: V.tensor_tensor(out=tbv[:, sti, 3, :], in0=r_.t[:, 128:256], in1=gk.t[:, 128:256], op=ALU.mult),
                             [r_.b, gk.b], [tb.bs[sti]])
                        s.op("act", lambda: A.activation(out=xb_.t[:], in_=x_.t[:], func=AF.Square, accum_out=st_.t[:, 0:1]),
                             [x_.b], [xb_.b, st_.b])
                        s.op("act", lambda: A.activation(out=st_.t[:, 1:2], in_=st_.t[:, 0:1], func=AF.Sqrt, scale=1.0 / D, bias=EPS),
                             [st_.b], [st_.b])
                        s.op("dve", lambda: V.reciprocal(out=st_.t[:, 2:3], in_=st_.t[:, 1:2]), [st_.b], [st_.b])
                        s.op("dve", lambda: V.scalar_tensor_tensor(out=xb_.t[:], in0=x_.t[:], scalar=st_.t[:, 2:3], in1=g1b.t[:],
                                                                   op0=ALU.mult, op1=ALU.mult),
                             [x_.b, st_.b, g1b.b], [xb_.b])
                        for hf in range(2):
                            p_ = ps[6 + hf]
                            pv = p_.t[:, :].bitcast(BF16)

                            def tr(hf=hf, pv=pv):
                                for kk in range(8):
                                    k = hf * 8 + kk
                                    ins = PE.transpose(out=pv[:, kk * 128:(kk + 1) * 128], in_=xb_.t[:, k * 128:(k + 1) * 128],
                                                       identity=identb.t[:])
                                return ins
                            s.op("pe", tr, [xb_.b, identb.b], [p_.b])
                            s.op("act", lambda hf=hf, pv=pv: A.copy(out=xT.t[:, hf * 8:(hf + 1) * 8, sti * 128:(sti + 1) * 128],
                                                                    in_=pv.rearrange("p (k t) -> p k t", k=8)),
                                 [p_.b], [xT.bs[sti]])
                    for cb in range(29):
                        gb = m * 29 + cb
                        if gb == 0:
                            wload(0)
                        if gb + 1 < nmac * 29:
                            wload(gb + 1)
                        wb = wblk[gb % 2]
                        typ = cb // 3 if cb < 9 else 3 + (cb - 9) // 4
                        sub = cb % 3 if cb < 9 else (cb - 9) % 4
                        if typ >= 3 and typ != 4:
                            f_ = fst[fsi % 2]
                            fsi += 1
                        for sti in range(4):
                            i = m * 4 + sti
                            p_ = ps[psi % 6]
                            psi += 1

                            def mm(p_=p_, sti=sti):
                                for k in range(16):
                                    ins = PE.matmul(p_.t[:, :], lhsT=xT.t[:, k, sti * 128:(sti + 1) * 128], rhs=wb.t[:, k, :],
                                                    start=(k == 0), stop=(k == 15))
                                return ins
                            s.op("pe", mm, [xT.bs[sti], wb.b], [p_.b])
                            if typ <= 1:
                                q_ = qs[qi % 2]
                                qs_ = qst[qi % 2]
                                a_ = tA[qi % 2]
                                b_ = tB[0]
                                o_ = ob[qi % 2]
                                qi += 1
                                tg = tb.t[:, sti, 2 * typ, :]
                                tsn = tb.t[:, sti, 2 * typ + 1, :]
                                s.op("act", lambda: A.copy(out=q_.t[:], in_=p_.t[:, :]), [p_.b], [q_.b])

                                def sq():
                                    for h in range(4):
                                        ins = A.activation(out=junk.t[:, h * 128:(h + 1) * 128], in_=p_.t[:, h * 128:(h + 1) * 128],
                                                           func=AF.Square, accum_out=qs_.t[:, h:h + 1])
                                    return ins
                                s.op("act", sq, [p_.b], [junk.b, qs_.b])
                                s.op("act", lambda: A.activation(out=qs_.t[:, 4:8], in_=qs_.t[:, 0:4], func=AF.Sqrt, scale=1.0 / 128, bias=EPS),
                                     [qs_.b], [qs_.b])
                                s.op("dve", lambda: V.reciprocal(out=qs_.t[:, 0:4], in_=qs_.t[:, 4:8]), [qs_.b], [qs_.b])
                                q3 = q_.t[:, :].rearrange("p (h d) -> p h d", h=4)
                                a3 = a_.t[:, :].rearrange("p (h d) -> p h d", h=4)
                                b3 = b_.t[:, :].rearrange("p (h d) -> p h d", h=4)
                                s.op("dve", lambda: V.tensor_tensor(out=a3, in0=q3, in1=tg.unsqueeze(1).to_broadcast([128, 4, 128]), op=ALU.mult),
                                     [q_.b, tb.bs[sti]], [a_.b])
                                s.op("dve", lambda: V.tensor_tensor(out=b3[:, :, 0:64], in0=q3[:, :, 64:128],
                                                                    in1=tsn[:, 0:64].unsqueeze(1).to_broadcast([128, 4, 64]), op=ALU.mult),
                                     [q_.b, tb.bs[sti]], [b_.b])
                                s.op("dve", lambda: V.tensor_tensor(out=b3[:, :, 64:128], in0=q3[:, :, 0:64],
                                                                    in1=tsn[:, 64:128].unsqueeze(1).to_broadcast([128, 4, 64]), op=ALU.mult),
                                     [q_.b, tb.bs[sti]], [b_.b])
                                s.op("dve", lambda: V.tensor_tensor(out=a_.t[:, :], in0=a_.t[:, :], in1=b_.t[:, :], op=ALU.add),
                                     [a_.b, b_.b], [a_.b])
                                s.op("dve", lambda: V.tensor_tensor(out=o_.t[:, :].rearrange("p (h d) -> p h d", h=4), in0=a3,
                                                                    in1=qs_.t[:, 0:4].unsqueeze(2).to_broadcast([128, 4, 128]), op=ALU.mult),
                                     [a_.b, qs_.b], [o_.b])
                                pt_ = ps[6 + pti % 2]
                                pti += 1
                                ptv = pt_.t[:, :].bitcast(BF16)

                                def tr2():
                                    for h in range(4):
                                        ins = PE.transpose(out=ptv[:, h * 128:(h + 1) * 128], in_=o_.t[:, h * 128:(h + 1) * 128],
                                                           identity=identb.t[:])
                                    return ins
                                s.op("pe", tr2, [o_.b, identb.b], [pt_.b])
                                dstT = (qTst if typ == 0 else kTst)[0]
                                s.op("act", lambda: A.copy(out=dstT.t[:, sub * 4:(sub + 1) * 4, sti * 128:(sti + 1) * 128],
                                                           in_=ptv[:, 0:512].rearrange("p (h t) -> p h t", h=4)),
                                     [pt_.b], [dstT.b])
                            elif typ == 2:
                                v_ = vst[0]
                                vv = v_.t[:, sti, :].rearrange("p (h e) -> p h e", e=129)
                                s.op("act", lambda: A.copy(out=vv[:, sub * 4:(sub + 1) * 4, 0:128],
                                                           in_=p_.t[:, :].rearrange("p (h d) -> p h d", h=4)),
                                     [p_.b], [v_.b])
                            elif typ == 3:
                                s.op("act", lambda: A.copy(out=f_.t[:, sti, :], in_=p_.t[:, :]), [p_.b], [f_.bs[sti]])
                            elif typ == 4:
                                s.op("act", lambda: A.copy(out=ccst.t[:, sti, sub * 512:(sub + 1) * 512], in_=p_.t[:, :]),
                                     [p_.b], [ccst.bs[sti * 4 + sub]])
                            elif typ == 5:
                                s.op("dve", lambda: V.tensor_tensor(out=f_.t[:, sti, :], in0=p_.t[:, :],
                                                                    in1=ccst.t[:, sti, sub * 512:(sub + 1) * 512], op=ALU.mult),
                                     [p_.b, ccst.bs[sti * 4 + sub]], [f_.bs[sti]])
                            else:
                                s.op("act", lambda: A.activation(out=f_.t[:, sti, :], in_=p_.t[:, :], func=AF.Sigmoid),
                                     [p_.b], [f_.bs[sti]])
                        R0 = m * 512
                        if typ >= 3 and typ != 4:
                            dst = {3: cb_s, 5: z_s, 6: sga_s, 7: sgc_s}[typ]
                            ro = 1 if typ == 5 else 0
                            s.dma("sp", [(dst.t[ro + R0:ro + R0 + 512, sub * 512:(sub + 1) * 512].rearrange("(s p) c -> p s c", p=128),
                                          f_.t[:, :, :])], f_.bs, [dst.b], f_.b)
                        if cb == 5:
                            for (src, dst) in ((qTst[0], qT_s), (kTst[0], kT_s)):
                                s.dma("sp", [(dst.t[:, :, R0:R0 + 512].rearrange("h d t -> d h t"), src.t[:, :, :])],
                                      [src.b], [dst.b], src.b)
                        if cb == 8:
                            v_ = vst[0]
                            s.dma("sp", [(v_s.t[R0:R0 + 512, :].rearrange("(s p) c -> p s c", p=128), v_.t[:, :, :])],
                                  [v_.b], [v_s.b], v_.b)
                while uvq:
                    (dstT_, src_, r0) = uvq.pop(0)
                    s.dma("pool", [(dstT_.t[r0:r0 + 4096, :], src_[r0:r0 + 4096, :])], [], [dstT_.b], dstT_.b)
            s.barrier()

        if "B" in phases:
            phase_b(nc, s, locals())
            phase_b2(nc, s, locals())
        if "C" in phases:
            phase_c(nc, s, locals())
        s.barrier()
    return nc, s


def phase_b(nc, s, L):
    from types import SimpleNamespace
    ns = SimpleNamespace(**L)
    V, A, G, PE = nc.vector, nc.scalar, nc.gpsimd, nc.tensor
    sbt, pst, nt, debug = ns.sbt, ns.pst, ns.nt, ns.debug
    identf, identb = ns.identf, ns.identb
    ISQ = 1.0 / np.sqrt(128.0)
    with ExitStack() as st:
        wtap = [sbt(st, "wtap%d" % i, [128, D], F32) for i in range(3)]
        for i in range(3):
            s.dma("sp", [(wtap[i].t[:], ns.conv_w[i:i + 1, :].to_broadcast([128, D]))], [], [wtap[i].b], wtap[i].b)
        f01 = sbt(st, "f01sb", [128, 2], F32)
        s.dma("sp", [(f01.t[:, 0:1], ns.f01d[:, :])], [], [f01.b], f01.b)
        s.op("dve", lambda: V.tensor_scalar(out=f01.t[:, 1:2], in0=f01.t[:, 0:1], scalar1=-1.0, scalar2=None, op0=ALU.add),
             [f01.b], [f01.b])
        ev = sbt(st, "ev", [128, 4], F32)
        for (c, col, usef) in ((0, 0, True), (1, 0, False), (2, 127, True), (3, 127, False)):
            if usef:
                s.op("dve", lambda: V.tensor_scalar(out=ev.t[:, c:c + 1], in0=identf.t[:, col:col + 1], scalar1=f01.t[:, 1:2],
                                                    scalar2=1.0, op0=ALU.mult, op1=ALU.add), [identf.b, f01.b], [ev.b])
            else:
                s.op("dve", lambda: V.tensor_scalar(out=ev.t[:, c:c + 1], in0=identf.t[:, col:col + 1], scalar1=-1.0,
                                                    scalar2=1.0, op0=ALU.mult, op1=ALU.add), [identf.b], [ev.b])
        cbt = sbt(st, "cbt", [128, D], F32)
        zp = sbt(st, "zp", [128, D], F32)
        zc = sbt(st, "zc", [128, D], F32)
        zn = sbt(st, "zn", [128, D], F32)
        maskn = sbt(st, "maskn", [128, 9 * 128], BF16)
        maskx = sbt(st, "maskx", [128, 9 * 128], BF16)
        s.dma("sp", [(cbt.t[:, 0:1152], ns.maskd[:, :])], [], [cbt.b], cbt.b)
        s.op("dve", lambda: V.tensor_copy(out=maskn.t[:], in_=cbt.t[:, 0:1152]), [cbt.b], [maskn.b])
        s.op("dve", lambda: V.tensor_scalar(out=zp.t[:, 0:1152], in0=cbt.t[:, 0:1152], scalar1=-NEGB, scalar2=f01.t[:, 0:1],
                                            op0=ALU.add, op1=ALU.mult), [cbt.b, f01.b], [zp.b])
        s.op("dve", lambda: V.tensor_scalar(out=maskx.t[:], in0=zp.t[:, 0:1152], scalar1=NEGB, scalar2=None, op0=ALU.add),
             [zp.b], [maskx.b])
        gqk = sbt(st, "gqk", [128, 256], F32)
        negc = sbt(st, "negc", [128, 4], F32)
        s.dma("sp", [(gqk.t[:, 0:128], ns.q_norm_g.to_broadcast([128, 128])),
                     (gqk.t[:, 128:256], ns.k_norm_g.to_broadcast([128, 128]))], [], [gqk.b], gqk.b)
        s.op("dve", lambda: V.tensor_reduce(out=negc.t[:, 0:2], in_=gqk.t[:, :].rearrange("p (a d) -> p a d", a=2), axis=AX.X,
                                            op=ALU.max, apply_absolute_value=True), [gqk.b], [negc.b])
        s.op("dve", lambda: V.tensor_tensor(out=negc.t[:, 2:3], in0=negc.t[:, 0:1], in1=negc.t[:, 1:2], op=ALU.mult), [negc.b], [negc.b])
        s.op("dve", lambda: V.tensor_scalar(out=negc.t[:, 3:4], in0=negc.t[:, 2:3], scalar1=-float(np.sqrt(128.0)) * 1.001, scalar2=None,
                                            op0=ALU.mult), [negc.b], [negc.b])
        iota16 = sbt(st, "iota16", [128, 16], F32)
        s.op("pool", lambda: G.iota(iota16.t[:], pattern=[[1, 16]], base=0, channel_multiplier=0,
                                    allow_small_or_imprecise_dtypes=True), [], [iota16.b])
        ps = [pst(st, "psB%d" % i) for i in range(8)]
        REACH = (1, 2, 8)
        kTt = [sbt(st, "kTt%d" % g, [128, 4, (2 * REACH[g] + 1) * 128], BF16) for g in range(3)]
        Vt = [sbt(st, "Vt%d" % g, [128, 2 * REACH[g] + 1, 516], BF16) for g in range(3)]
        qTt = sbt(st, "qTt", [128, 12, 128], BF16)
        Eb = [sbt(st, "Eb%d" % i, [128, 512], BF16) for i in range(2)]
        rden = sbt(st, "rden", [128, 4], F32)
        attnb = sbt(st, "attnb", [128, 512], BF16)
        attnT2 = [sbt(st, "attnT%d" % i, [128, 4, 128], BF16) for i in range(2)]
        cvb = sbt(st, "cvb", [128, D], BF16)
        convT = sbt(st, "convT", [128, 16, 128], BF16)
        mixT = convT
        xn2T = convT
        mixb = cvb
        xn2b = cvb
        wblk = [sbt(st, "wblkB%d" % i, [128, 4, 512], BF16) for i in range(1)]
        wco_r = sbt(st, "wco_r", [128, 16, D], BF16)
        s.dma("sp", [(wco_r.t[:, :, :], ns.w_co_b.t[:, :].rearrange("(k p) c -> p k c", p=128))], [ns.w_co_b.b], [wco_r.b], wco_r.b)
        sgab = [sbt(st, "sgab%d" % i, [128, 512], F32) for i in range(1)]
        sgcb = [sbt(st, "sgcb%d" % i, [128, 512], F32) for i in range(1)]
        m1 = [sbt(st, "m1_%d" % i, [128, 512], F32) for i in range(1)]
        m2 = [sbt(st, "m2_%d" % i, [128, 512], F32) for i in range(1)]
        dba = sbt(st, "dba", [128, 512], F32) if debug else None

        wseq = []
        for i in range(nt):
            for n in range(4):
                wseq.append((ns.w_ao_b, 4, n))
        wstate = {"issued": 0, "used": 0}

        def wissue():
            g = wstate["issued"]
            if g >= len(wseq):
                return
            src, nk, n = wseq[g]
            wb = wblk[0]
            s.dma("sp", [(wb.t[:, 0:nk, :], src.t[:, n * 512:(n + 1) * 512].rearrange("(k p) c -> p k c", p=128))],
                  [src.b], [wb.b], wb.b)
            wstate["issued"] += 1

        def wget():
            wissue()
            return wblk[0]

        def mask_ap(g, dl, cross):
            r = REACH[g]
            mi = g * 3 + (0 if dl == -r else (2 if dl == r else 1))
            return (maskx if cross else maskn), mi

        def att_gen(i):
                T0 = i * 128
                lo, hi = (0, 2 * TPS) if i < 2 * TPS else (2 * TPS, 3 * TPS)
                hi = min(hi, nt)
                s.dma("sp", [(qTt.t[:, :, :], ns.qT_s.t[:, :, T0:T0 + 128].rearrange("h d t -> d h t"))], [ns.qT_s.b], [qTt.b], qTt.b)
                krange = []
                for g in range(3):
                    k0 = max(lo, i - REACH[g])
                    k1 = min(hi - 1, i + REACH[g])
                    nk = k1 - k0 + 1
                    krange.append((k0, k1))
                    s.dma("sp", [(kTt[g].t[:, :, 0:nk * 128], ns.kT_s.t[4 * g:4 * g + 4, :, k0 * 128:(k1 + 1) * 128].rearrange("h d t -> d h t"))],
                          [ns.kT_s.b], [kTt[g].b], kTt[g].b)
                    s.dma("sp", [(Vt[g].t[:, 0:nk, :], ns.v_s.t[k0 * 128:(k1 + 1) * 128, g * 516:(g + 1) * 516].rearrange("(b p) c -> p b c", p=128))],
                          [ns.v_s.b], [Vt[g].b], Vt[g].b)
                yield
                bcount = 0
                for j in range(4):
                    blocks = []
                    for g in range(3):
                        k0, k1 = krange[g]
                        for kt in range(k0, k1 + 1):
                            cross = (i < 2 * TPS) and ((i < TPS) != (kt < TPS))
                            blocks.append((g, kt, kt - k0, cross))
                    O_ = ps[2 + j // 2]
                    Oj = O_.t[:, (j % 2) * 129:(j % 2) * 129 + 129]
                    nb_tot = len(blocks)
                    done = 0
                    for b0 in range(0, nb_tot, 4):
                        bl = blocks[b0:b0 + 4]
                        S_ = ps[bcount % 2]
                        E_ = Eb[bcount % 2]
                        bcount += 1

                        def smm(bl=bl, S_=S_):
                            for bi, (g, kt, ko, cross) in enumerate(bl):
                                mt, mi = mask_ap(g, kt - i, cross)
                                PE.matmul(S_.t[:, bi * 128:(bi + 1) * 128], lhsT=kTt[g].t[:, j, ko * 128:(ko + 1) * 128],
                                          rhs=qTt.t[:, 4 * g + j, :], start=True, stop=False)
                                ins = PE.matmul(S_.t[:, bi * 128:(bi + 1) * 128], lhsT=identb.t[:, :],
                                                rhs=mt.t[:, mi * 128:(mi + 1) * 128], start=False, stop=True)
                            return ins
                        s.op("pe", smm, [kTt[0].b, kTt[1].b, kTt[2].b, qTt.b, identb.b, maskn.b, maskx.b], [S_.b])
                        w = len(bl) * 128
                        s.op("act", lambda: A.activation(out=E_.t[:, 0:w], in_=S_.t[:, 0:w], func=AF.Exp, scale=ISQ, bias=negc.t[:, 3:4]),
                             [S_.b, negc.b], [E_.b])

                        def pv(bl=bl, E_=E_, done=done):
                            for bi, (g, kt, ko, cross) in enumerate(bl):
                                ins = PE.matmul(Oj, lhsT=E_.t[:, bi * 128:(bi + 1) * 128], rhs=Vt[g].t[:, ko, j * 129:(j + 1) * 129],
                                                start=(done + bi == 0), stop=(done + bi == nb_tot - 1))
                            return ins
                        s.op("pe", pv, [E_.b, Vt[0].b, Vt[1].b, Vt[2].b], [O_.b])
                        done += len(bl)
                        yield
                for hb in range(2):
                    O_ = ps[2 + hb]
                    s.op("dve", lambda: V.reciprocal(out=rden.t[:, 2 * hb:2 * hb + 2],
                                                     in_=O_.t[:, 0:258].rearrange("p (s e) -> p s e", e=129)[:, :, 128]), [O_.b], [rden.b])
                for j in range(4):
                    O_ = ps[2 + j // 2]
                    s.op("act", lambda: A.activation(out=attnb.t[:, j * 128:(j + 1) * 128], in_=O_.t[:, (j % 2) * 129:(j % 2) * 129 + 128],
                                                     func=AF.Copy, scale=rden.t[:, j:j + 1]), [O_.b, rden.b], [attnb.b])
                if debug:
                    s.op("dve", lambda: V.tensor_copy(out=dba.t[:], in_=attnb.t[:]), [attnb.b], [dba.b])
                    s.dma("pool", [(ns.dbg_attn.t[T0:T0 + 128, :], dba.t[:])], [dba.b], [ns.dbg_attn.b], dba.b)
                p_ = ps[6]
                pv_ = p_.t[:, :].bitcast(BF16)

                def tra():
                    for jj in range(4):
                        ins = PE.transpose(out=pv_[:, jj * 128:(jj + 1) * 128], in_=attnb.t[:, jj * 128:(jj + 1) * 128], identity=identb.t[:])
                    return ins
                s.op("pe", tra, [attnb.b, identb.b], [p_.b])
                s.op("act", lambda: A.copy(out=attnT2[i % 2].t[:, :, :], in_=pv_[:, 0:512].rearrange("p (k t) -> p k t", k=4)), [p_.b], [attnT2[i % 2].b])
                yield

        def conv_loads(i):
            T0 = i * 128
            s.dma("sp", [(cbt.t[:], ns.cb_s.t[T0:T0 + 128, :])], [ns.cb_s.b], [cbt.b], cbt.b)
            s.dma("sp", [(zp.t[:], ns.z_s.t[T0:T0 + 128, :])], [ns.z_s.b], [zp.b], zp.b)
            s.dma("sp", [(zc.t[:], ns.z_s.t[T0 + 1:T0 + 129, :])], [ns.z_s.b], [zc.b], zc.b)
            s.dma("sp", [(zn.t[:], ns.z_s.t[T0 + 2:T0 + 130, :])], [ns.z_s.b], [zn.b], zn.b)

        conv_loads(0)
        for _ in att_gen(0):
            pass
        agn = att_gen(1) if nt > 1 else iter(())
        next(agn, None)
        for i in range(nt):
            T0 = i * 128
            ag = agn

            def adv(k):
                for _ in range(k):
                    next(ag, None)
            if i % TPS == 0 and i > 0:
                c = 0 if i == TPS else 1
                s.op("pool", lambda: G.tensor_scalar(out=zp.t[:], in0=zp.t[:], scalar1=ev.t[:, c:c + 1], scalar2=None, op0=ALU.mult),
                     [zp.b, ev.b], [zp.b])
            if i % TPS == TPS - 1 and i < 3 * TPS - 1:
                c = 2 if i == TPS - 1 else 3
                s.op("pool", lambda: G.tensor_scalar(out=zn.t[:], in0=zn.t[:], scalar1=ev.t[:, c:c + 1], scalar2=None, op0=ALU.mult),
                     [zn.b, ev.b], [zn.b])
            s.op("pool", lambda: G.tensor_tensor(out=zp.t[:], in0=zp.t[:], in1=wtap[0].t[:], op=ALU.mult), [zp.b, wtap[0].b], [zp.b])
            s.op("dve", lambda: V.tensor_tensor(out=zc.t[:], in0=zc.t[:], in1=wtap[1].t[:], op=ALU.mult), [zc.b, wtap[1].b], [zc.b])
            s.op("dve", lambda: V.tensor_tensor(out=zn.t[:], in0=zn.t[:], in1=wtap[2].t[:], op=ALU.mult), [zn.b, wtap[2].b], [zn.b])
            s.op("dve", lambda: V.tensor_tensor(out=zc.t[:], in0=zc.t[:], in1=zn.t[:], op=ALU.add), [zc.b, zn.b], [zc.b])
            s.op("dve", lambda: V.tensor_tensor(out=zp.t[:], in0=zp.t[:], in1=zc.t[:], op=ALU.add), [zp.b, zc.b], [zp.b])
            s.op("dve", lambda: V.tensor_tensor(out=cvb.t[:], in0=zp.t[:], in1=cbt.t[:], op=ALU.mult), [zp.b, cbt.b], [cvb.b])
            if i + 1 < nt:
                conv_loads(i + 1)

            def tr16(srcT, dstT):
                for hf in range(2):
                    p_ = ps[6 + hf]
                    pv = p_.t[:, :].bitcast(BF16)

                    def tr(hf=hf, pv=pv):
                        for kk in range(8):
                            k = hf * 8 + kk
                            ins = PE.transpose(out=pv[:, kk * 128:(kk + 1) * 128], in_=srcT.t[:, k * 128:(k + 1) * 128], identity=identb.t[:])
                        return ins
                    s.op("pe", tr, [srcT.b, identb.b], [p_.b])
                    s.op("act", lambda hf=hf, pv=pv: A.copy(out=dstT.t[:, hf * 8:(hf + 1) * 8, :], in_=pv.rearrange("p (k t) -> p k t", k=8)),
                         [p_.b], [dstT.b])
            adv(8)
            tr16(cvb, convT)
            for n in range(4):
                wa = wget()
                pa, pc = ps[4], ps[5]
                sa, sc_ = sgab[0], sgcb[0]
                s.dma("sp", [(sa.t[:], ns.sga_s.t[T0:T0 + 128, n * 512:(n + 1) * 512])], [ns.sga_s.b], [sa.b], sa.b)
                s.dma("sp", [(sc_.t[:], ns.sgc_s.t[T0:T0 + 128, n * 512:(n + 1) * 512])], [ns.sgc_s.b], [sc_.b], sc_.b)

                attnT = attnT2[i % 2]

                def mma():
                    for jj in range(4):
                        ins = PE.matmul(pa.t[:, :], lhsT=attnT.t[:, jj, :], rhs=wa.t[:, jj, :], start=(jj == 0), stop=(jj == 3))
                    return ins
                s.op("pe", mma, [attnT.b, wa.b], [pa.b])

                def mmc():
                    for k in range(16):
                        ins = PE.matmul(pc.t[:, :], lhsT=convT.t[:, k, :], rhs=wco_r.t[:, k, n * 512:(n + 1) * 512], start=(k == 0), stop=(k == 15))
                    return ins
                s.op("pe", mmc, [convT.b, wco_r.b], [pc.b])
                a1, a2 = m1[0], m2[0]
                s.op("dve", lambda: V.tensor_tensor(out=a1.t[:], in0=pa.t[:, :], in1=sa.t[:], op=ALU.mult), [pa.b, sa.b], [a1.b])
                s.op("dve", lambda: V.tensor_tensor(out=a2.t[:], in0=pc.t[:, :], in1=sc_.t[:], op=ALU.mult), [pc.b, sc_.b], [a2.b])
                s.op("pool", lambda: G.tensor_tensor(out=mixb.t[:, n * 512:(n + 1) * 512], in0=a1.t[:], in1=a2.t[:], op=ALU.add),
                     [a1.b, a2.b], [mixb.b])
                adv(3)
            s.dma("pool", [(ns.mix_s.t[T0:T0 + 128, :], mixb.t[:])], [mixb.b], [ns.mix_s.b], mixb.b)
            adv(100)
            agn = att_gen(i + 2) if i + 2 < nt else iter(())
            next(agn, None)
    s.barrier()


def phase_b2(nc, s, L):
    from types import SimpleNamespace
    ns = SimpleNamespace(**L)
    V, A, G, PE = nc.vector, nc.scalar, nc.gpsimd, nc.tensor
    sbt, pst, nt, debug = ns.sbt, ns.pst, ns.nt, ns.debug
    identf, identb = ns.identf, ns.identb
    with ExitStack() as st:
        g2b = sbt(st, "g2b", [128, D], F32)
        s.dma("sp", [(g2b.t[:], ns.norm2_g.to_broadcast([128, D]))], [], [g2b.b], g2b.b)
        ps = [pst(st, "psB2_%d" % i) for i in range(8)]
        hb = [sbt(st, "hb%d" % i, [128, D], F32) for i in range(2)]
        scb = [sbt(st, "scb%d" % i, [128, D], F32) for i in range(2)]
        mixl = [sbt(st, "mixl%d" % i, [128, D], BF16) for i in range(2)]
        xn2l = [sbt(st, "xn2l%d" % i, [128, D], BF16) for i in range(2)]
        mixT = sbt(st, "mixT2", [128, 16, 128], BF16)
        xn2T = sbt(st, "xn2T2", [128, 16, 128], BF16)
        xpb = [sbt(st, "xpb%d" % i, [128, 512], F32) for i in range(2)]
        st2l = [sbt(st, "st2_%d" % i, [128, 4], F32) for i in range(2)]
        wo_r = sbt(st, "wo_r", [128, 16, D], BF16)
        wq_r = sbt(st, "wq_r", [128, 16, D], BF16)
        s.dma("sp", [(wo_r.t[:, :, :], ns.w_o_b.t[:, :].rearrange("(k p) c -> p k c", p=128))], [ns.w_o_b.b], [wo_r.b], wo_r.b)
        s.dma("sp", [(wq_r.t[:, :, :], ns.w_q_b.t[:, :].rearrange("(k p) c -> p k c", p=128))], [ns.w_q_b.b], [wq_r.b], wq_r.b)
        zc = hb[0]
        skT = sbt(st, "skT", [128, 16, 128], BF16)
        pqT = sbt(st, "pqT", [128, 16, 128], BF16)
        skb = pqT
        s.dma("sp", [(zc.t[:, :].rearrange("p (a d) -> p a d", a=16), ns.subk.rearrange("a n d -> n a d"))], [], [zc.b], zc.b)
        s.op("dve", lambda: V.tensor_copy(out=skb.t[:, :, :], in_=zc.t[:, :].rearrange("p (a d) -> p a d", a=16)), [zc.b], [skb.b])
        for hf in range(2):
            p_ = ps[6 + hf]
            pv = p_.t[:, :].bitcast(BF16)

            def trk(hf=hf, pv=pv):
                for kk in range(8):
                    ins = PE.transpose(out=pv[:, kk * 128:(kk + 1) * 128], in_=skb.t[:, hf * 8 + kk, :], identity=identb.t[:])
                return ins
            s.op("pe", trk, [skb.b, identb.b], [p_.b])
            s.op("act", lambda hf=hf, pv=pv: A.copy(out=skT.t[:, hf * 8:(hf + 1) * 8, :], in_=pv.rearrange("p (k t) -> p k t", k=8)),
                 [p_.b], [skT.b])


        def tr16(srcT, dstT):
            for hf in range(2):
                p_ = ps[6 + hf]
                pv = p_.t[:, :].bitcast(BF16)

                def tr(hf=hf, pv=pv):
                    for kk in range(8):
                        k = hf * 8 + kk
                        ins = PE.transpose(out=pv[:, kk * 128:(kk + 1) * 128], in_=srcT.t[:, k * 128:(k + 1) * 128], identity=identb.t[:])
                    return ins
                s.op("pe", tr, [srcT.b, identb.b], [p_.b])
                s.op("act", lambda hf=hf, pv=pv: A.copy(out=dstT.t[:, hf * 8:(hf + 1) * 8, :], in_=pv.rearrange("p (k t) -> p k t", k=8)),
                     [p_.b], [dstT.b])

        def adv(k):
            pass

        for i in range(nt):
            T0 = i * 128
            mixb = mixl[i % 2]
            xn2b = xn2l[i % 2]
            zn = hb[i % 2]
            cbt = scb[i % 2]
            st2 = st2l[i % 2]
            s.dma("sp", [(mixb.t[:], ns.mix_s.t[T0:T0 + 128, :])], [ns.mix_s.b], [mixb.b], mixb.b)
            tr16(mixb, mixT)
            adv(2)
            h_ = zn
            for n in range(4):
                wo = wo_r
                po = ps[4 + n % 2]
                xp = xpb[n % 2]
                s.dma("sp", [(xp.t[:], ns.xs[T0:T0 + 128, n * 512:(n + 1) * 512])], [], [xp.b], xp.b)

                def mmo():
                    for k in range(16):
                        ins = PE.matmul(po.t[:, :], lhsT=mixT.t[:, k, :], rhs=wo.t[:, k, n * 512:(n + 1) * 512], start=(k == 0), stop=(k == 15))
                    return ins
                s.op("pe", mmo, [mixT.b, wo.b], [po.b])
                s.op("dve", lambda: V.tensor_tensor(out=h_.t[:, n * 512:(n + 1) * 512], in0=po.t[:, :], in1=xp.t[:], op=ALU.add),
                     [po.b, xp.b], [h_.b])
                adv(1)
            s.dma("pool", [(ns.h_s.t[T0:T0 + 128, :], h_.t[:])], [h_.b], [ns.h_s.b], h_.b)
            s.op("act", lambda: A.activation(out=xn2b.t[:], in_=h_.t[:], func=AF.Square, accum_out=st2.t[:, 0:1]), [h_.b], [xn2b.b, st2.b])
            s.op("act", lambda: A.activation(out=st2.t[:, 1:2], in_=st2.t[:, 0:1], func=AF.Sqrt, scale=1.0 / D, bias=EPS), [st2.b], [st2.b])
            s.op("dve", lambda: V.reciprocal(out=st2.t[:, 2:3], in_=st2.t[:, 1:2]), [st2.b], [st2.b])
            s.op("dve", lambda: V.scalar_tensor_tensor(out=xn2b.t[:], in0=h_.t[:], scalar=st2.t[:, 2:3], in1=g2b.t[:], op0=ALU.mult, op1=ALU.mult),
                 [h_.b, st2.b, g2b.b], [xn2b.b])
            s.dma("pool", [(ns.xn2_s.t[T0:T0 + 128, :], xn2b.t[:])], [xn2b.b], [ns.xn2_s.b], xn2b.b)
            tr16(xn2b, xn2T)
            for n in range(4):
                wq_ = wq_r
                pq = ps[4 + n % 2]

                def mmq():
                    for c in range(4):
                        for k in range(16):
                            ins = PE.matmul(pq.t[:, c * 128:(c + 1) * 128], lhsT=wq_.t[:, k, n * 512 + c * 128:n * 512 + (c + 1) * 128], rhs=xn2T.t[:, k, :],
                                            start=(k == 0), stop=(k == 15))
                    return ins
                s.op("pe", mmq, [wq_.b, xn2T.b], [pq.b])
                s.op("act", lambda: A.copy(out=pqT.t[:, 4 * n:4 * n + 4, :], in_=pq.t[:, :].rearrange("p (c t) -> p c t", c=4)), [pq.b], [pqT.b])
            sc = cbt
            for b in range(4):
                pb = ps[b]

                def mms():
                    for c in range(4):
                        hp = 4 * b + c
                        ins = PE.matmul(pb.t[:, c * 128:(c + 1) * 128], lhsT=pqT.t[:, hp, :], rhs=skT.t[:, hp, :], start=True, stop=True)
                    return ins
                s.op("pe", mms, [pqT.b, skT.b], [pb.b])
                s.op("act", lambda: A.copy(out=sc.t[:, b * 512:(b + 1) * 512], in_=pb.t[:, :]), [pb.b], [sc.b])
            if debug:
                s.dma("pool", [(ns.dbg_sc.t[T0:T0 + 128, :], sc.t[:])], [sc.b], [ns.dbg_sc.b], sc.b)
            s.dma("pool", [(ns.sc_s.t[T0:T0 + 128, :], sc.t[:])], [sc.b], [ns.sc_s.b], sc.b)

    s.barrier()


def phase_c(nc, s, L):
    from types import SimpleNamespace
    ns = SimpleNamespace(**L)
    V, A, G, PE = nc.vector, nc.scalar, nc.gpsimd, nc.tensor
    sbt, nt = ns.sbt, ns.nt
    identb = ns.identb
    F32R = mybir.dt.float32r
    RING = 8
    with ExitStack() as st:
        csel = sbt(st, "csel", [128, 255], F32)
        s.op("dve", lambda: V.memset(csel.t[:], 0.0), [], [csel.b])
        s.op("dve", lambda: V.memset(csel.t[:, 127:128], 1.0), [csel.b], [csel.b])
        ht = [sbt(st, "ht%d" % i, [128, D], F32) for i in range(2)]
        xt = [sbt(st, "xn2t%d" % i, [128, D], BF16) for i in range(2)]
        it = [sbt(st, "idxt%d" % i, [128, 128], U32) for i in range(2)]
        gt = [sbt(st, "gtt%d" % i, [128, 128], F32) for i in range(2)]
        U = [sbt(st, "U%d" % i, [128, D], BF16) for i in range(RING)]
        Vv = [sbt(st, "Vv%d" % i, [128, D], BF16) for i in range(RING)]
        junk = sbt(st, "junkC", [128, 1024], BF16)
        hacc = sbt(st, "hacc", [128, 128, 2], F32, nb=128)
        gl = [sbt(st, "gl%d" % i, [128, 2], F32) for i in range(4)]
        Z = [sbt(st, "Z%d" % i, [128, 128], BF16) for i in range(4)]
        yt = sbt(st, "yt", [128, D], F32)
        bc_t = st.enter_context(nc.psum_tensor("bcC", [128, 2048], F32))
        bc = [T(bc_t, s.buf("bcC%d" % i)) for i in range(2)]
        out_t = st.enter_context(nc.psum_tensor("outC", [128, 2048], F32))
        outp = T(out_t, s.buf("outC"))

        identf = ns.identf
        debug = ns.debug
        sct = [sbt(st, "sct%d" % i, [128, D], F32) for i in range(2)]
        cand = sbt(st, "cand", [128, D], F32)
        oh = sbt(st, "oh", [128, D], F32)
        iota16 = sbt(st, "iota16c", [128, 16], F32)
        s.op("pool", lambda: G.iota(iota16.t[:], pattern=[[1, 16]], base=0, channel_multiplier=0,
                                    allow_small_or_imprecise_dtypes=True), [], [iota16.b])
        s16 = sbt(st, "s16", [128, 16, 16], F32)
        i16 = sbt(st, "i16", [128, 16, 16], U32)
        i16f = sbt(st, "i16f", [128, 16, 16], F32)
        work = sbt(st, "work", [128, 256], F32)
        best = sbt(st, "best", [128, 8, 16], F32)
        flat = sbt(st, "flat", [128, 8, 16], U32)
        au = sbt(st, "au", [128, 128], U32)
        bu = sbt(st, "bu", [128, 128], U32)
        af = sbt(st, "af", [128, 128], F32)
        bf_ = sbt(st, "bf", [128, 128], F32)
        e1 = sbt(st, "e1", [128, 128], F32)
        e2 = sbt(st, "e2", [128, 128], F32)
        ef = sbt(st, "ef", [128, 128], F32)
        gat = sbt(st, "gat", [128, 128], F32)
        gsum = sbt(st, "gsum", [128, 16], F32)

        def topk_gen(i):
            T0 = i * 128
            sc = sct[i % 2]
            idxo, gto = it[i % 2], gt[i % 2]
            zp = cand
            s.dma("sp", [(sc.t[:], ns.sc_s.t[T0:T0 + 128, :])], [ns.sc_s.b], [sc.b], sc.b)
            yield
            sc3 = sc.t[:, :].rearrange("p (a n) -> p a n", a=16)

            def top16(src_ap, vals, idxs, wk):
                s.op("dve", lambda: V.max(out=vals[:, 0:8], in_=src_ap), [sc.b, zp.b], [s16.b, best.b])
                yield
                s.op("dve", lambda: V.max_index(out=idxs[:, 0:8], in_max=vals[:, 0:8], in_values=src_ap), [sc.b, zp.b, s16.b, best.b], [i16.b, flat.b])
                yield
                s.op("dve", lambda: V.match_replace(out=wk, in_to_replace=vals[:, 0:8], in_values=src_ap, imm_value=-1e30),
                     [sc.b, zp.b, s16.b, best.b], [work.b])
                yield
                s.op("dve", lambda: V.max(out=vals[:, 8:16], in_=wk), [work.b], [s16.b, best.b])
                yield
                s.op("dve", lambda: V.max_index(out=idxs[:, 8:16], in_max=vals[:, 8:16], in_values=wk), [work.b, s16.b, best.b], [i16.b, flat.b])
                yield
            for hp in range(16):
                yield from top16(sc3[:, hp, :], s16.t[:, hp, :], i16.t[:, hp, :], work.t[:, 0:128])
            s.op("dve", lambda: V.tensor_copy(out=i16f.t[:, :, :], in_=i16.t[:, :, :]), [i16.b], [i16f.b])
            yield
            s4 = s16.t[:, :, :].rearrange("p (h two) k -> p h two k", two=2)
            i4 = i16f.t[:, :, :].rearrange("p (h two) k -> p h two k", two=2)
            cand4 = cand.t[:, :].rearrange("p (h a b) -> p h a b", h=8, a=16)
            s.op("dve", lambda: V.tensor_tensor(out=cand4, in0=s4[:, :, 0, :].unsqueeze(3).to_broadcast([128, 8, 16, 16]),
                                                in1=s4[:, :, 1, :].unsqueeze(2).to_broadcast([128, 8, 16, 16]), op=ALU.add), [s16.b], [cand.b])
            yield
            cand3 = cand.t[:, :].rearrange("p (h n) -> p h n", h=8)
            for h in range(8):
                yield from top16(cand3[:, h, :], best.t[:, h, :], flat.t[:, h, :], work.t[:, 0:256])
            flat2 = flat.t[:, :, :].rearrange("p h k -> p (h k)")
            s.op("dve", lambda: V.tensor_scalar(out=au.t[:], in0=flat2, scalar1=4, scalar2=None, op0=ALU.logical_shift_right), [flat.b], [au.b])
            yield
            s.op("dve", lambda: V.tensor_scalar(out=bu.t[:], in0=flat2, scalar1=15, scalar2=None, op0=ALU.bitwise_and), [flat.b], [bu.b])
            yield
            s.op("dve", lambda: V.tensor_copy(out=af.t[:], in_=au.t[:]), [au.b], [af.b])
            yield
            s.op("dve", lambda: V.tensor_copy(out=bf_.t[:], in_=bu.t[:]), [bu.b], [bf_.b])
            yield
            oh4 = oh.t[:, :].rearrange("p (h k j) -> p h k j", h=8, k=16)
            io4 = iota16.t[:, :].unsqueeze(1).unsqueeze(1).to_broadcast([128, 8, 16, 16])
            for (xf_, half, eo) in ((af, 0, e1), (bf_, 1, e2)):
                x4 = xf_.t[:, :].rearrange("p (h k) -> p h k", h=8).unsqueeze(3).to_broadcast([128, 8, 16, 16])
                s.op("dve", lambda: V.tensor_tensor(out=oh4, in0=x4, in1=io4, op=ALU.is_equal), [xf_.b, iota16.b], [oh.b])
                yield
                s.op("dve", lambda: V.tensor_tensor(out=oh4, in0=oh4, in1=i4[:, :, half, :].unsqueeze(2).to_broadcast([128, 8, 16, 16]), op=ALU.mult),
                     [oh.b, i16f.b], [oh.b])
                yield
                s.op("dve", lambda: V.tensor_reduce(out=eo.t[:, :].rearrange("p (h k) -> p h k", h=8), in_=oh4, axis=AX.X, op=ALU.add), [oh.b], [eo.b])
                yield
            s.op("dve", lambda: V.scalar_tensor_tensor(out=ef.t[:], in0=e1.t[:], scalar=128.0, in1=e2.t[:], op0=ALU.mult, op1=ALU.add),
                 [e1.b, e2.b], [ef.b])
            yield
            g3 = gat.t[:, :].rearrange("p (h k) -> p h k", h=8)
            s.op("dve", lambda: V.tensor_tensor(out=g3, in0=best.t[:, :, :], in1=best.t[:, :, 0:1].to_broadcast([128, 8, 16]), op=ALU.subtract),
                 [best.b], [gat.b])
            yield
            s.op("act", lambda: A.activation(out=gat.t[:], in_=gat.t[:], func=AF.Exp), [gat.b], [gat.b])
            yield
            s.op("dve", lambda: V.tensor_reduce(out=gsum.t[:, 0:8], in_=g3, axis=AX.X, op=ALU.add), [gat.b], [gsum.b])
            yield
            s.op("dve", lambda: V.reciprocal(out=gsum.t[:, 8:16], in_=gsum.t[:, 0:8]), [gsum.b], [gsum.b])
            yield
            s.op("dve", lambda: V.tensor_tensor(out=g3, in0=g3, in1=gsum.t[:, 8:16].unsqueeze(2).to_broadcast([128, 8, 16]), op=ALU.mult),
                 [gat.b, gsum.b], [gat.b])
            yield
            s.op("pe", lambda: PE.transpose(out=bc_t[:, 0:128], in_=ef.t[:, :], identity=identf.t[:]), [ef.b, identf.b], [bc[0].b])
            s.op("dve", lambda: V.tensor_copy(out=idxo.t[:], in_=bc_t[:, 0:128]), [bc[0].b], [idxo.b])
            yield
            s.op("pe", lambda: PE.transpose(out=bc_t[:, 1024:1152], in_=gat.t[:, :], identity=identf.t[:]), [gat.b, identf.b], [bc[1].b])
            s.op("dve", lambda: V.tensor_copy(out=gto.t[:], in_=bc_t[:, 1024:1152]), [bc[1].b], [gto.b])
            yield
            if debug:
                s.dma("sp", [(ns.idxT_s.t[i, :, :], idxo.t[:])], [idxo.b], [ns.idxT_s.b], idxo.b)
                s.dma("sp", [(ns.gT_s.t[i, :, :], gto.t[:])], [gto.b], [ns.gT_s.b], gto.b)

        for _ in topk_gen(0):
            pass
        tg = 0
        for i in range(nt):
            T0 = i * 128
            h_, x_, i_, g_ = ht[i % 2], xt[i % 2], it[i % 2], gt[i % 2]
            tgen = topk_gen(i + 1) if i + 1 < nt else iter(())
            s.dma("sp", [(x_.t[:], ns.xn2_s.t[T0:T0 + 128, :])], [ns.xn2_s.b], [x_.b], x_.b)
            s.dma("sp", [(h_.t[:], ns.h_s.t[T0:T0 + 128, :])], [ns.h_s.b], [h_.b], h_.b)

            def matvec(t, r, zz):
                def mv():
                    for p in range(4):
                        ins = PE.matmul(outp.t[:, p * 512:(p + 1) * 512], lhsT=zz.t[:, :],
                                        rhs=Vv[r].t[:, p * 512:(p + 1) * 512], start=(t == 0), stop=(t == 127))
                    return ins
                s.op("pe", mv, [zz.b, Vv[r].b], [outp.b])
            LAG = 3
            pend = []
            for t in range(128):
                r = tg % RING
                z_ = Z[tg % 4]
                gl_ = gl[tg % 4]
                s.gather(U[r].t[:], ns.pu_b.t[:, :], i_.t[:, t:t + 1], [i_.b, ns.pu_b.b], [U[r].b], U[r].b)
                s.gather(Vv[r].t[:], ns.pv_b.t[:, :], i_.t[:, t:t + 1], [i_.b, ns.pv_b.b], [Vv[r].b], Vv[r].b)
                for hf in range(2):
                    b_ = bc[hf]

                    def bcm(hf=hf):
                        for p in range(2):
                            c0 = hf * 1024 + p * 512
                            ins = PE.matmul(bc_t[:, c0:c0 + 512], lhsT=identb.t[:, t:t + 1].to_broadcast([128, 128]),
                                            rhs=x_.t[:, c0:c0 + 512], start=True, stop=True)
                        return ins
                    s.op("pe", bcm, [x_.b, identb.b], [b_.b])
                    s.op("dve", lambda hf=hf: V.scalar_tensor_tensor(out=junk.t[:, :], in0=U[r].t[:, hf * 1024:(hf + 1) * 1024], scalar=1.0,
                                                                     in1=bc_t[:, hf * 1024:(hf + 1) * 1024], op0=ALU.mult, op1=ALU.mult,
                                                                     accum_out=hacc.t[:, t, hf:hf + 1]),
                         [U[r].b, b_.b], [junk.b, hacc.bs[t]])
                s.op("act", lambda: A.activation(out=gl_.t[:, 0:1], in_=hacc.t[:, t, 0:1], func=AF.Gelu, bias=hacc.t[:, t, 1:2]),
                     [hacc.bs[t]], [gl_.b])
                s.op("act", lambda: A.activation(out=gl_.t[:, 1:2], in_=gl_.t[:, 0:1], func=AF.Copy, scale=g_.t[:, t:t + 1]),
                     [gl_.b, g_.b], [gl_.b])
                s.op("act", lambda: A.activation(out=z_.t[:, :], in_=csel.t[:, 127 - t:255 - t], func=AF.Copy, scale=gl_.t[:, 1:2]),
                     [gl_.b, csel.b], [z_.b])
                pend.append((t, r, z_))
                if len(pend) > LAG:
                    matvec(*pend.pop(0))
                tg += 1
                next(tgen, None)
                next(tgen, None)
            while pend:
                matvec(*pend.pop(0))
            for _ in tgen:
                pass
            for p in range(4):
                s.op("dve", lambda: V.tensor_tensor(out=yt.t[:, p * 512:(p + 1) * 512], in0=outp.t[:, p * 512:(p + 1) * 512],
                                                    in1=h_.t[:, p * 512:(p + 1) * 512], op=ALU.add), [outp.b, h_.b], [yt.b])
            s.dma("sp", [(ns.ys[T0:T0 + 128, :], yt.t[:])], [yt.b], [ns.b_ys], yt.b)
    s.barrier()


def _masks():
    i = np.arange(128)[:, None]
    j = np.arange(128)[None, :]
    out = []
    for (r, deltas) in ((1, (-1, 0, 1)), (4, (-2, 0, 2)), (16, (-8, 0, 8))):
        for dl in deltas:
            rel = dl * 128 + i - j
            ok = (np.abs(rel) <= 64 * r) & (rel % r == 0)
            out.append(np.where(ok, 0.0, NEGB))
    return np.concatenate(out, axis=1).astype(np.float32)


def _rope_tables(npos):
    half = 64
    inv = (10000.0 ** (-np.arange(half, dtype=np.float32) * 2.0 / 128)).astype(np.float32)
    ang = np.arange(npos, dtype=np.float32)[:, None] * inv[None, :]
    c = np.cos(ang).astype(np.float32)
    sn = np.sin(ang).astype(np.float32)
    return np.concatenate([c, c], 1), np.concatenate([-sn, sn], 1)


def core_stream(x_prompt, x_sample, c):
    if c < 4:
        return [x_sample[c], x_prompt[c]]
    return [x_prompt[4 + 3 * (c - 4) + j] for j in range(3)]


def prep_core(inputs, c):
    seqs = core_stream(inputs["x_prompt"], inputs["x_sample"], c)
    cc4, ss4 = _rope_tables(4096)
    m = {"xs": np.ascontiguousarray(np.concatenate(seqs, 0)),
         "ccd": np.ascontiguousarray(np.concatenate([cc4[:len(q)] for q in seqs], 0)),
         "ssd": np.ascontiguousarray(np.concatenate([ss4[:len(q)] for q in seqs], 0)),
         "f01": np.full((128, 1), 1.0 if c < 4 else 0.0, np.float32),
         "maskb": _masks()}
    for k in ("norm1_g", "w_in", "q_norm_g", "k_norm_g", "w_attn_out", "conv_w", "w_conv_out", "w_o",
              "norm2_g", "peer_w_q", "peer_u", "peer_v"):
        m[k] = np.ascontiguousarray(inputs[k][0])
    m["peer_sub_keys"] = np.ascontiguousarray(inputs["peer_sub_keys"][0].reshape(16, 128, 128))
    return m


def kernel(**inputs):
    inputs = {k: np.asarray(v) for k, v in inputs.items()}
    nc, _ = build()
    in_maps = [prep_core(inputs, c) for c in range(8)]
    res = run_bass_kernel_spmd(nc, in_maps, core_ids=list(range(8)))
    yp = np.empty((16, 2048, D), np.float32)
    ysm = np.empty((4, 4096, D), np.float32)
    for c in range(8):
        y = res.results[c]["ys"]
        if c < 4:
            ysm[c] = y[0:4096]
            yp[c] = y[4096:6144]
        else:
            for j in range(3):
                yp[4 + 3 * (c - 4) + j] = y[2048 * j:2048 * (j + 1)]
    return (yp, ysm)
```
